# Optimizing a Trainium2 kernel written in Bass

```python
import math
import jax, jax.numpy as jnp
from jax import lax
import numpy as np

D_MODEL = 1024
BATCH = 16
SEQ = 2048
DEPTH = 4

MLA_HEADS = 6
MLA_NOPE = 64
MLA_ROPE = 32
MLA_V = 64
MLA_QK = MLA_NOPE + MLA_ROPE
MLA_Q_RANK = 192
MLA_KV_RANK = 128
DIL_HEADS = 6
DIL_HEAD_DIM = 64
DIL_PATTERNS = ((128, 1), (512, 4), (2048, 16))
SSM_GROUPS = 16
SSM_GROUP_CH = 16
SSM_CH = SSM_GROUPS * SSM_GROUP_CH
SSM_STATE = 64
DT_MIN = 1e-3
DT_MAX = 1e-1
WIDTH_A = MLA_HEADS * MLA_V
WIDTH_B = DIL_HEADS * DIL_HEAD_DIM
MIX_WIDTH = WIDTH_A + WIDTH_B + SSM_CH
IN_A = MLA_Q_RANK + MLA_KV_RANK + MLA_ROPE
IN_B = 3 * WIDTH_B
IN_C = SSM_CH
IN_WIDTH = IN_A + IN_B + IN_C
FFN_HIDDEN = 2816
CONV_WIDTH = 3
ROPE_THETA = 10000.0
EPS = 1e-6
Q_BLOCK = 128
NEG_INF = -1e30
N_MOD = 6

kernel_name = "hymba_style_mla_dilated_s5_convffn_encoder"


def rmsnorm(x, gain):
    xf = x.astype(jnp.float32)
    y = xf * lax.rsqrt(jnp.mean(xf * xf, axis=-1, keepdims=True) + EPS)
    return (y * gain.astype(jnp.float32)).astype(x.dtype)


def rope_tables(positions, dim):
    inv_freq = 1.0 / (ROPE_THETA ** (jnp.arange(0, dim, 2, dtype=jnp.float32) / dim))
    ang = positions.astype(jnp.float32)[..., None] * inv_freq
    return jnp.cos(ang), jnp.sin(ang)


def apply_rope(t, cos, sin):
    tf = t.astype(jnp.float32)
    t1, t2 = jnp.split(tf, 2, axis=-1)
    cs, sn = cos[:, None], sin[:, None]
    return jnp.concatenate([t1 * cs - t2 * sn, t1 * sn + t2 * cs], axis=-1).astype(t.dtype)


def split_heads(t, n_heads):
    b, s, _ = t.shape
    return t.reshape(b, s, n_heads, -1).transpose(0, 2, 1, 3)


def merge_heads(t):
    b, h, s, e = t.shape
    return t.transpose(0, 2, 1, 3).reshape(b, s, h * e)


def dense_attention(q, k, v):
    b, h, s, e = q.shape
    nb = s // Q_BLOCK
    scale = e ** -0.5
    vf = v.astype(jnp.float32)
    q_blocks = q.reshape(b, h, nb, Q_BLOCK, e).transpose(2, 0, 1, 3, 4)

    def attend(qb):
        sc = jnp.einsum("bhqe,bhke->bhqk", qb, k, preferred_element_type=jnp.float32) * scale
        p = jax.nn.softmax(sc, axis=-1)
        return jnp.einsum("bhqk,bhkd->bhqd", p, vf)

    out = lax.map(attend, q_blocks)
    return out.transpose(1, 2, 0, 3, 4).reshape(b, h, s, v.shape[-1]).astype(v.dtype)


def mla_mixer(u, cos_r, sin_r, q_gain, w_uq, kv_gain, w_ukv, qk_gain):
    c_q, c_kv, k_rope = jnp.split(u, [MLA_Q_RANK, MLA_Q_RANK + MLA_KV_RANK], axis=-1)
    q = split_heads(rmsnorm(c_q, q_gain) @ w_uq, MLA_HEADS)
    kv = split_heads(rmsnorm(c_kv, kv_gain) @ w_ukv, MLA_HEADS)
    k_nope, v = jnp.split(kv, [MLA_NOPE], axis=-1)
    k_rope = jnp.broadcast_to(k_rope[:, None], k_nope.shape[:-1] + (MLA_ROPE,))
    q = rmsnorm(q, qk_gain[0])
    k = rmsnorm(jnp.concatenate([k_nope, k_rope], axis=-1), qk_gain[1])
    q = jnp.concatenate([q[..., :MLA_NOPE], apply_rope(q[..., MLA_NOPE:], cos_r, sin_r)], axis=-1)
    k = jnp.concatenate([k[..., :MLA_NOPE], apply_rope(k[..., MLA_NOPE:], cos_r, sin_r)], axis=-1)
    return merge_heads(dense_attention(q, k, v))


def dilated_branch(q, k, v, window, dilation):
    b, h, s, e = q.shape
    half = window // (2 * dilation)
    blk = half
    n_sub = s // dilation
    nb = -(-n_sub // blk)
    lp = nb * blk

    def by_residue(t):
        return t.reshape(b, h, n_sub, dilation, e).transpose(0, 1, 3, 2, 4)

    qb = jnp.pad(by_residue(q), ((0, 0), (0, 0), (0, 0), (0, lp - n_sub), (0, 0)))
    qb = qb.reshape(b, h, dilation, nb, blk, e)

    def band(t):
        tp = jnp.pad(by_residue(t), ((0, 0), (0, 0), (0, 0), (blk, lp - n_sub + blk), (0, 0)))
        return jnp.concatenate(
            [tp[:, :, :, i * blk:i * blk + lp].reshape(b, h, dilation, nb, blk, e) for i in range(3)],
            axis=-2)

    kb, vb = band(k), band(v)
    sc = jnp.einsum("bhrnqe,bhrnke->bhrnqk", qb, kb, preferred_element_type=jnp.float32) * (e ** -0.5)
    q_idx = jnp.arange(lp).reshape(nb, blk, 1)
    k_idx = (jnp.arange(nb)[:, None, None] - 1) * blk + jnp.arange(3 * blk)[None, None, :]
    valid = (jnp.abs(k_idx - q_idx) <= half) & (k_idx >= 0) & (k_idx < n_sub)
    sc = jnp.where(valid, sc, NEG_INF)
    m = jnp.max(sc, axis=-1, keepdims=True)
    p = jnp.exp(sc - m)
    den = jnp.sum(p, axis=-1, keepdims=True)
    o = jnp.einsum("bhrnqk,bhrnke->bhrnqe", p, vb.astype(jnp.float32)) / den
    lse = (m + jnp.log(den))[..., 0]

    def back(t):
        t = t.reshape((b, h, dilation, lp) + t.shape[5:])[:, :, :, :n_sub]
        t = jnp.moveaxis(t, 2, 3)
        return t.reshape((b, h, s) + t.shape[4:])

    return back(o), back(lse)


def dilated_mixer(u, cos_f, sin_f, qk_gain):
    q, k, v = (split_heads(t, DIL_HEADS) for t in jnp.split(u, 3, axis=-1))
    q = apply_rope(rmsnorm(q, qk_gain[0]), cos_f, sin_f)
    k = apply_rope(rmsnorm(k, qk_gain[1]), cos_f, sin_f)
    branches = [dilated_branch(q, k, v, w, d) for w, d in DIL_PATTERNS]
    outs = jnp.stack([o for o, _ in branches])
    wts = jax.nn.softmax(jnp.stack([l for _, l in branches]), axis=0)
    out = jnp.einsum("pbhs,pbhse->bhse", wts, outs)
    return merge_heads(out).astype(u.dtype)


def s5_scan(uf, a_re, a_im, log_dt, b_re, b_im, c_re, c_im, reverse):
    a = lax.complex(a_re.astype(jnp.float32), a_im.astype(jnp.float32))
    dt = jnp.exp(log_dt.astype(jnp.float32))[:, None]
    a_bar = jnp.exp(a * dt)
    b_bar = ((a_bar - 1.0) / a)[..., None] * lax.complex(b_re.astype(jnp.float32), b_im.astype(jnp.float32))
    bu = jnp.einsum("bsgc,gpc->bsgp", uf.astype(jnp.complex64), b_bar)
    a_seq = jnp.broadcast_to(a_bar, bu.shape)

    def combine(left, right):
        a_l, b_l = left
        a_r, b_r = right
        return a_r * a_l, a_r * b_l + b_r

    _, states = lax.associative_scan(combine, (a_seq, bu), axis=1, reverse=reverse)
    c = lax.complex(c_re.astype(jnp.float32), c_im.astype(jnp.float32))
    return jnp.einsum("bsgp,gcp->bsgc", states, c).real


def s5_mixer(u, a_re, a_im, log_dt, b_re, b_im, c_re, c_im, d_skip, w_glu, b_glu):
    b, s, _ = u.shape
    uf = u.astype(jnp.float32).reshape(b, s, SSM_GROUPS, SSM_GROUP_CH)
    y_fwd = s5_scan(uf, a_re[0], a_im[0], log_dt[0], b_re[0], b_im[0], c_re[0], c_im[0], False)
    y_bwd = s5_scan(uf, a_re[1], a_im[1], log_dt[1], b_re[1], b_im[1], c_re[1], c_im[1], True)
    y = (y_fwd + y_bwd).reshape(b, s, SSM_CH) + d_skip.astype(jnp.float32) * uf.reshape(b, s, SSM_CH)
    y = jax.nn.gelu(y).astype(u.dtype)
    val, gate = jnp.split(y @ w_glu + b_glu, 2, axis=-1)
    return val * jax.nn.sigmoid(gate)


def conv_ffn(h, w_up, conv_w, conv_b, w_down):
    z = h @ w_up
    z = lax.conv_general_dilated(
        z, conv_w[:, None, :], window_strides=(1,),
        padding=((CONV_WIDTH // 2, CONV_WIDTH // 2),),
        dimension_numbers=("NWC", "WIO", "NWC"),
        feature_group_count=z.shape[-1]) + conv_b
    val, gate = jnp.split(z, 2, axis=-1)
    return (jax.nn.silu(gate) * val) @ w_down


def setup_inputs(seed: int = 0) -> dict:
    key = jax.random.key(seed)
    ks = iter(jax.random.split(key, 40))
    L = DEPTH

    def nrm(shape, scale):
        return jax.random.normal(next(ks), shape, jnp.float32) * scale

    def gain(shape):
        return 1.0 + nrm(shape, 0.02)

    x = nrm((BATCH, SEQ, D_MODEL), 1.0)
    c = nrm((BATCH, D_MODEL), 1.0)
    offsets = jax.random.randint(next(ks), (BATCH, 1), 0, 4096, dtype=jnp.int32)
    positions = offsets + jnp.arange(SEQ, dtype=jnp.int32)[None, :]
    n_idx = jnp.arange(SSM_STATE, dtype=jnp.float32)
    return {
        "x": x,
        "c": c,
        "positions": positions,
        "w_mod": nrm((L, D_MODEL, N_MOD * D_MODEL), 0.5 * D_MODEL ** -0.5),
        "b_mod": nrm((L, N_MOD * D_MODEL), 0.01),
        "norm1": gain((L, D_MODEL)),
        "w_in": nrm((L, D_MODEL, IN_WIDTH), D_MODEL ** -0.5),
        "mla_q_norm": gain((L, MLA_Q_RANK)),
        "mla_w_uq": nrm((L, MLA_Q_RANK, MLA_HEADS * MLA_QK), MLA_Q_RANK ** -0.5),
        "mla_kv_norm": gain((L, MLA_KV_RANK)),
        "mla_w_ukv": nrm((L, MLA_KV_RANK, MLA_HEADS * (MLA_NOPE + MLA_V)), MLA_KV_RANK ** -0.5),
        "mla_qk_gain": gain((L, 2, MLA_QK)),
        "dil_qk_gain": gain((L, 2, DIL_HEAD_DIM)),
        "ssm_a_re": -0.5 + nrm((L, 2, SSM_GROUPS, SSM_STATE), 0.01),
        "ssm_a_im": math.pi * n_idx + nrm((L, 2, SSM_GROUPS, SSM_STATE), 0.01),
        "ssm_log_dt": jax.random.uniform(next(ks), (L, 2, SSM_GROUPS), jnp.float32,
                                         math.log(DT_MIN), math.log(DT_MAX)),
        "ssm_b_re": nrm((L, 2, SSM_GROUPS, SSM_STATE, SSM_GROUP_CH), (2 * SSM_GROUP_CH) ** -0.5),
        "ssm_b_im": nrm((L, 2, SSM_GROUPS, SSM_STATE, SSM_GROUP_CH), (2 * SSM_GROUP_CH) ** -0.5),
        "ssm_c_re": nrm((L, 2, SSM_GROUPS, SSM_GROUP_CH, SSM_STATE), SSM_STATE ** -0.5),
        "ssm_c_im": nrm((L, 2, SSM_GROUPS, SSM_GROUP_CH, SSM_STATE), SSM_STATE ** -0.5),
        "ssm_d": nrm((L, SSM_CH), 1.0),
        "ssm_w_glu": nrm((L, SSM_CH, 2 * SSM_CH), SSM_CH ** -0.5),
        "ssm_b_glu": nrm((L, 2 * SSM_CH), 0.01),
        "mix_norm": gain((L, MIX_WIDTH)),
        "w_out": nrm((L, MIX_WIDTH, D_MODEL), MIX_WIDTH ** -0.5),
        "norm2": gain((L, D_MODEL)),
        "ffn_w_up": nrm((L, D_MODEL, 2 * FFN_HIDDEN), D_MODEL ** -0.5),
        "ffn_conv_w": nrm((L, CONV_WIDTH, 2 * FFN_HIDDEN), 0.2)
                      + jnp.array([0.0, 1.0, 0.0], jnp.float32)[None, :, None],
        "ffn_conv_b": nrm((L, 2 * FFN_HIDDEN), 0.01),
        "ffn_w_down": nrm((L, FFN_HIDDEN, D_MODEL), FFN_HIDDEN ** -0.5),
    }


def reference(x, c, positions, w_mod, b_mod, norm1, w_in, mla_q_norm, mla_w_uq, mla_kv_norm,
              mla_w_ukv, mla_qk_gain, dil_qk_gain, ssm_a_re, ssm_a_im, ssm_log_dt, ssm_b_re,
              ssm_b_im, ssm_c_re, ssm_c_im, ssm_d, ssm_w_glu, ssm_b_glu, mix_norm, w_out, norm2,
              ffn_w_up, ffn_conv_w, ffn_conv_b, ffn_w_down):
    cos_r, sin_r = rope_tables(positions, MLA_ROPE)
    cos_f, sin_f = rope_tables(positions, DIL_HEAD_DIM)
    c_act = jax.nn.silu(c)
    for l in range(DEPTH):
        mod = c_act @ w_mod[l] + b_mod[l]
        sh1, sc1, g1, sh2, sc2, g2 = (m[:, None, :] for m in jnp.split(mod, N_MOD, axis=-1))

        h = rmsnorm(x, norm1[l]) * (1.0 + sc1) + sh1
        u = h @ w_in[l]
        u_a, u_b, u_c = jnp.split(u, [IN_A, IN_A + IN_B], axis=-1)
        o_a = mla_mixer(u_a, cos_r, sin_r, mla_q_norm[l], mla_w_uq[l], mla_kv_norm[l],
                        mla_w_ukv[l], mla_qk_gain[l])
        o_b = dilated_mixer(u_b, cos_f, sin_f, dil_qk_gain[l])
        o_c = s5_mixer(u_c, ssm_a_re[l], ssm_a_im[l], ssm_log_dt[l], ssm_b_re[l], ssm_b_im[l],
                       ssm_c_re[l], ssm_c_im[l], ssm_d[l], ssm_w_glu[l], ssm_b_glu[l])
        g_mix = mix_norm[l]
        mixed = jnp.concatenate([
            rmsnorm(o_a, g_mix[:WIDTH_A]),
            rmsnorm(o_b, g_mix[WIDTH_A:WIDTH_A + WIDTH_B]),
            rmsnorm(o_c, g_mix[WIDTH_A + WIDTH_B:]),
        ], axis=-1)
        x = x + g1 * (mixed @ w_out[l])

        h2 = rmsnorm(x, norm2[l]) * (1.0 + sc2) + sh2
        x = x + g2 * conv_ffn(h2, ffn_w_up[l], ffn_conv_w[l], ffn_conv_b[l], ffn_w_down[l])
    return x
```

```python
import math
from contextlib import ExitStack

import ml_dtypes
import numpy as np
import concourse.bass as bass
import concourse.mybir as mybir
from concourse.bass_utils import run_bass_kernel_spmd

F32 = mybir.dt.float32
BF16 = mybir.dt.bfloat16
I32 = mybir.dt.int32
AF = mybir.ActivationFunctionType
ALU = mybir.AluOpType
AX = mybir.AxisListType

ENGS = ("pe", "act", "dve", "pool", "sp")
EPOCH = 20000
NDMASEM = 14


class Res:
    __slots__ = ("name", "writers", "readers", "wgroup")

    def __init__(self, name):
        self.name = name
        self.writers = []
        self.readers = []
        self.wgroup = None


class Op:
    __slots__ = ("idx", "eng", "fn", "dma", "deps", "lidx", "waits", "signal", "needed", "slot", "prev_slot_op")

    def __init__(self):
        self.waits = []
        self.signal = None
        self.needed = False


class Prog:
    def __init__(self, nc):
        self.nc = nc
        self.ops = []
        self.eng_ops = {e: [] for e in ENGS}
        self.dma_rr = {e: 0 for e in ENGS}
        self.dma_slot_last = {}
        self.nres = 0
        self.cur_barrier = set()

    def res(self, name=None):
        self.nres += 1
        return Res(name or f"r{self.nres}")

    def barrier(self):
        tails = set()
        for e in ENGS:
            if self.eng_ops[e]:
                tails.add(self.eng_ops[e][-1].idx)
        for o in self.dma_slot_last.values():
            tails.add(o.idx)
        self.cur_barrier = tails

    def op(self, eng, fn, reads=(), writes=(), dma=False, wgroup=None):
        o = Op()
        o.idx = len(self.ops)
        o.eng = eng
        o.fn = fn
        o.dma = dma
        deps = set(self.cur_barrier)
        for r in reads:
            deps.update(r.writers)
        for w in writes:
            if not (wgroup is not None and w.wgroup == wgroup):
                deps.update(w.writers)
            deps.update(w.readers)
        o.deps = deps
        o.lidx = len(self.eng_ops[eng])
        o.slot = None
        o.prev_slot_op = None
        if dma:
            s = self.dma_rr[eng]
            self.dma_rr[eng] = (s + 1) % NDMASEM
            o.slot = s
            o.prev_slot_op = self.dma_slot_last.get((eng, s))
            self.dma_slot_last[(eng, s)] = o
        self.ops.append(o)
        self.eng_ops[eng].append(o)
        for r in reads:
            r.readers.append(o.idx)
        for w in writes:
            if wgroup is not None and w.wgroup == wgroup:
                w.writers.append(o.idx)
            else:
                w.writers = [o.idx]
                w.readers = []
                w.wgroup = wgroup
        return o

    def finalize(self):
        ops = self.ops
        tails = set()
        for e in ENGS:
            if self.eng_ops[e]:
                tails.add(self.eng_ops[e][-1].idx)
        for o in self.dma_slot_last.values():
            tails.add(o.idx)
        fin = Op()
        fin.idx = len(ops)
        fin.eng = "sp"
        fin.fn = None
        fin.dma = False
        fin.deps = tails
        fin.lidx = len(self.eng_ops["sp"])
        fin.slot = None
        fin.prev_slot_op = None
        ops.append(fin)
        self.eng_ops["sp"].append(fin)

        seen = {e: {f: -1 for f in ENGS} for e in ENGS}
        seen_dma = {e: set() for e in ENGS}
        for o in ops:
            e = o.eng
            dl = []
            if o.dma and o.prev_slot_op is not None:
                dl.append(o.prev_slot_op)
            for d in sorted(o.deps):
                dl.append(ops[d])
            for d in dl:
                if d.dma:
                    if d.idx in seen_dma[e]:
                        continue
                    seen_dma[e].add(d.idx)
                    d.needed = True
                    o.waits.append(d)
                else:
                    f = d.eng
                    if f == e:
                        if e in ("pe", "sp"):
                            continue
                        if d.lidx < o.lidx - 3:
                            continue
                    if d.lidx <= seen[e][f]:
                        continue
                    seen[e][f] = d.lidx
                    d.needed = True
                    o.waits.append(d)
        cnt = {e: 0 for e in ENGS}
        dma_tot = {}
        for o in ops:
            if o.dma:
                k = (o.eng, o.slot)
                dma_tot[k] = dma_tot.get(k, 0) + 16
                o.signal = ("d", o.eng, o.slot, dma_tot[k])
            elif o.needed:
                c = cnt[o.eng]
                cnt[o.eng] = c + 1
                o.signal = ("c", o.eng, c // EPOCH, c % EPOCH + 1)
        self.n_epochs = {e: max(1, (cnt[e] + EPOCH - 1) // EPOCH) for e in ENGS}

    def emit(self, stack):
        nc = self.nc
        self.finalize()
        sems = {}
        for e in ENGS:
            if e == "sp":
                continue
            for ep in range(self.n_epochs[e]):
                sems[("c", e, ep)] = stack.enter_context(nc.semaphore(f"s_{e}_{ep}"))
        for e in ENGS:
            if any(k[0] == e for k in self.dma_slot_last):
                for s in range(NDMASEM):
                    sems[("d", e, s)] = stack.enter_context(nc.semaphore(f"d_{e}_{s}"))
        block = stack.enter_context(nc.Block())

        def run(engname):
            def body(eng):
                for o in self.eng_ops[engname]:
                    for d in o.waits:
                        sg = d.signal
                        eng.wait_ge(sems[sg[:3]], sg[3])
                    if o.fn is None:
                        continue
                    inst = o.fn(eng)
                    if o.signal is not None:
                        sg = o.signal
                        inst.then_inc(sems[sg[:3]], 16 if sg[0] == "d" else 1)
            return body

        block.tensor(run("pe"))
        block.scalar(run("act"))
        block.vector(run("dve"))
        block.gpsimd(run("pool"))
        block.sync(run("sp"))


D = 1024
S = 2048
NB = 2
T = NB * S
NTT = T // 128
DEPTH = 4
HM, NOPE, ROPE_M, VM, QKM = 6, 64, 32, 64, 96
QR, KVR = 192, 128
HD, DD = 6, 64
G, GC, NST = 16, 16, 64
CH = 256
INW = 1760
FF = 2816
EPS = 1e-6
OFF_CQ, OFF_CKV, OFF_KR, OFF_DQ, OFF_DK, OFF_DV, OFF_UC = 0, 192, 320, 352, 736, 1120, 1504
MASKW = 3968
TWO_PI_S = 6.2831845
INV2PI = 1.0 / (2.0 * math.pi)
HALF_PI_S = 1.5707960
MAGIC = 12582912.0


class Tile:
    __slots__ = ("ap", "res")

    def __init__(self, ap, res):
        self.ap = ap
        self.res = res


class Arena:
    def __init__(self, P, arena_ap, words):
        self.P = P
        self.a = arena_ap
        self.words = words
        self.top = 0
        self.peak = 0

    def mark(self):
        return self.top

    def reset(self, m):
        self.top = m

    def alloc(self, name, shape, dtype):
        n = 1
        for s in shape[1:]:
            n *= s
        nw = n if dtype in (F32, I32) else (n + 1) // 2
        nw = (nw + 7) // 8 * 8
        assert self.top + nw <= self.words, f"SBUF arena overflow at {name}: {self.top}+{nw}>{self.words}"
        ap = self.a[0:shape[0], self.top:self.top + nw]
        self.top += nw
        self.peak = max(self.peak, self.top)
        if dtype != F32:
            ap = ap.bitcast(dtype)
        ap = ap[:, 0:n]
        if len(shape) == 3:
            ap = ap.rearrange("p (a b) -> p a b", a=shape[1])
        elif len(shape) == 4:
            ap = ap.rearrange("p (a b c) -> p a b c", a=shape[1], b=shape[2])
        return Tile(ap, self.P.res(name))

    def ring(self, name, shape, dtype, n):
        return Ring([self.alloc(f"{name}{i}", shape, dtype) for i in range(n)])


class Ring:
    def __init__(self, tiles):
        self.tiles = tiles
        self.i = 0

    def next(self):
        t = self.tiles[self.i % len(self.tiles)]
        self.i += 1
        return t


def build_program(depth=DEPTH, debug=None, stop_after=None):
    debug = debug or {}
    nc = bass.Bass("TRN2", target_bir_lowering=False)
    dt_in = lambda name, shape, dt=F32: nc.dram_tensor(name, list(shape), dt, kind="ExternalInput").ap()
    dt_scr = lambda name, shape, dt: nc.dram_tensor(name, list(shape), dt, kind="Internal").ap()
    L = DEPTH
    x_in = dt_in("x", [T, D])
    c_in = dt_in("c", [NB, D])
    pos_in = dt_in("positions", [NB, S], I32)
    w_mod = dt_in("w_mod", [L, D, 6 * D]); b_mod = dt_in("b_mod", [L, 6 * D])
    norm1 = dt_in("norm1", [L, D]); w_in = dt_in("w_in", [L, D, INW])
    mla_q_norm = dt_in("mla_q_norm", [L, QR]); mla_w_uq = dt_in("mla_w_uq", [L, QR, HM * QKM])
    mla_kv_norm = dt_in("mla_kv_norm", [L, KVR]); mla_w_ukv = dt_in("mla_w_ukv", [L, KVR, HM * 128])
    mla_qk_gain = dt_in("mla_qk_gain", [L, 2, QKM]); dil_qk_gain = dt_in("dil_qk_gain", [L, 2, DD])
    ssm_a_re = dt_in("ssm_a_re", [L, 2, G, NST]); ssm_a_im = dt_in("ssm_a_im", [L, 2, G, NST])
    ssm_log_dt = dt_in("ssm_log_dt", [L, 2, G])
    ssm_b_re = dt_in("ssm_b_re", [L, 2, G, NST, GC]); ssm_b_im = dt_in("ssm_b_im", [L, 2, G, NST, GC])
    ssm_c_re = dt_in("ssm_c_re", [L, 2, G, GC, NST]); ssm_c_im = dt_in("ssm_c_im", [L, 2, G, GC, NST])
    ssm_d = dt_in("ssm_d", [L, CH]); ssm_w_glu = dt_in("ssm_w_glu", [L, CH, 2 * CH]); ssm_b_glu = dt_in("ssm_b_glu", [L, 2 * CH])
    mix_norm = dt_in("mix_norm", [L, D]); w_out = dt_in("w_out", [L, D, D]); norm2 = dt_in("norm2", [L, D])
    ffn_w_up = dt_in("ffn_w_up", [L, D, 2 * FF]); ffn_conv_w = dt_in("ffn_conv_w", [L, 3, 2 * FF])
    ffn_conv_b = dt_in("ffn_conv_b", [L, 2 * FF]); ffn_w_down = dt_in("ffn_w_down", [L, FF, D])
    k_ident = dt_in("k_ident", [128, 128], BF16)
    k_mask = dt_in("k_mask", [128, MASKW], BF16)
    k_misc = dt_in("k_misc", [128, 256])
    k_iota = dt_in("k_iota", [128, 1024])
    out = nc.dram_tensor("out", [T, D], F32, kind="ExternalOutput").ap()

    xmid = dt_scr("xmid", [T, D], F32)
    modD = dt_scr("modD", [NB, 6 * D], F32)
    qmT = dt_scr("qmT", [NB, HM, QKM, S], BF16); kmT = dt_scr("kmT", [NB, HM, QKM, S], BF16)
    vmD = dt_scr("vmD", [NB, S, HM * 65], BF16)
    qdT = dt_scr("qdT", [NB, 3, 128, S], BF16); kdT = dt_scr("kdT", [NB, 3, 128, S], BF16)
    vdD = dt_scr("vdD", [NB, S, HD * 65], BF16)
    ucT = dt_scr("ucT", [NB, CH, S], BF16)
    uctok = dt_scr("uctok", [NB, S, CH], F32)
    mixedD = dt_scr("mixedD", [NB, S, D], BF16)
    h2TD = dt_scr("h2TD", [NB, D, S], BF16)
    win_b = dt_scr("win_b", [D, INW], BF16); wuq_b = dt_scr("wuq_b", [QR, HM * QKM], BF16)
    wukv_b = dt_scr("wukv_b", [KVR, HM * 128], BF16); wglu_b = dt_scr("wglu_b", [CH, 2 * CH], BF16)
    wout_b = dt_scr("wout_b", [D, D], BF16); wup_b = dt_scr("wup_b", [44, 128, 1024], BF16)
    wdown_b = dt_scr("wdown_b", [FF, D], BF16)
    dbg_out = {k: nc.dram_tensor("dbg_" + k, list(shp), dt_, kind="ExternalOutput").ap() for k, (shp, dt_) in debug.items()}

    st = ExitStack()
    AW = 53000
    arena_t = st.enter_context(nc.sbuf_tensor("arena", [128, AW], F32))
    psF = st.enter_context(nc.psum_tensor("psF", [128, 3072], F32))
    psBt = st.enter_context(nc.psum_tensor("psB", [128, 2048], BF16))
    P = Prog(nc)
    A = Arena(P, arena_t, AW)
    psres = [P.res(f"psum{i}") for i in range(8)]
    psB = psBt[:, :]
    psB_f = psBt[:, :].bitcast(F32)

    def bankF(i):
        if i < 6:
            return psF[:, 512 * i:512 * (i + 1)]
        return psB_f[:, 512 * (i - 6):512 * (i - 5)]

    dres = {k: P.res("D_" + k) for k in ["xin", "xmid", "out", "modD", "qmT", "kmT", "vmD", "qdT", "kdT", "vdD", "ucT",
                                          "uctok", "mixedD", "h2TD", "win_b", "wuq_b", "wukv_b", "wglu_b", "wout_b",
                                          "wup_b", "wdown_b", "dbg"]}
    for k in ["qmT", "kmT", "vmD", "qdT", "kdT", "vdD", "ucT", "uctok", "mixedD", "h2TD"]:
        for b in range(NB):
            dres[(k, b)] = P.res(f"D_{k}_{b}")
    xres_tt = [P.res(f"D_x_{i}") for i in range(NTT)]
    xmid_tt = [P.res(f"D_xm_{i}") for i in range(NTT)]

    def dma(out_ap, in_ap, reads=(), writes=(), wgroup=None, eng="sp", ncont=False):
        if ncont:
            f = lambda e: e.dma_start(out=out_ap, in_=in_ap, allow_slow_non_contiguous=True)
        else:
            f = lambda e: e.dma_start(out=out_ap, in_=in_ap)
        return P.op(eng, f, reads=reads, writes=writes, dma=True, wgroup=wgroup)

    def dbg(name, ap, res_list):
        if name in dbg_out:
            dma(dbg_out[name], ap, reads=res_list, writes=[dres["dbg"]], wgroup="dbg")

    V = lambda fn, r=(), w=(), g=None: P.op("dve", fn, reads=r, writes=w, wgroup=g)
    ACT = lambda fn, r=(), w=(), g=None: P.op("act", fn, reads=r, writes=w, wgroup=g)
    GP = lambda fn, r=(), w=(), g=None: P.op("pool", fn, reads=r, writes=w, wgroup=g)
    PE = lambda fn, r=(), w=(), g=None: P.op("pe", fn, reads=r, writes=w, wgroup=g)

    ident = A.alloc("ident", [128, 128], BF16)
    maskS = A.alloc("maskS", [128, MASKW], BF16)
    misc = A.alloc("misc", [128, 256], F32)
    dma(ident.ap, k_ident, writes=[ident.res])
    dma(maskS.ap, k_mask, writes=[maskS.res])
    dma(misc.ap, k_misc, writes=[misc.res])
    iota = A.alloc("iota", [128, 1024], F32)
    dma(iota.ap, k_iota, writes=[iota.res])
    cosM = A.alloc("cosM", [128, NTT, 16], F32); sinM = A.alloc("sinM", [128, NTT, 16], F32)
    cosD = A.alloc("cosD", [128, NTT, 32], F32); sinD = A.alloc("sinD", [128, NTT, 32], F32)
    cT = A.alloc("cT", [128, NB, 8], F32)

    def rsqrt_to(dst_tile, src_ap, scale, n_reads, tmp_tile):
        ACT(lambda e: e.activation(tmp_tile.ap, src_ap, AF.Sqrt, bias=EPS, scale=scale), r=n_reads, w=[tmp_tile.res])
        V(lambda e: e.reciprocal(dst_tile.ap, tmp_tile.ap), r=[tmp_tile.res], w=[dst_tile.res])

    def sincos_scratch(n):
        return (A.alloc("yi", [128, n], I32), A.alloc("yf", [128, n], F32), A.alloc("yc", [128, n], F32))

    def sincos(dst_sin, dst_cos, y_tile, n, scr):
        yi = Tile(scr[0].ap[:, 0:n], scr[0].res); yf = Tile(scr[1].ap[:, 0:n], scr[1].res); yc = Tile(scr[2].ap[:, 0:n], scr[2].res)
        yflat = y_tile.ap
        for dst, off in ((dst_sin, 0.0), (dst_cos, 0.25)):
            if off != 0.0:
                V(lambda e, off=off: e.tensor_scalar(yc.ap, yflat, off, None, ALU.add), r=[y_tile.res], w=[yc.res])
                src = yc
            else:
                src = y_tile
            V(lambda e, src=src: e.tensor_scalar(yf.ap, src.ap, MAGIC, MAGIC, ALU.add, ALU.subtract), r=[src.res], w=[yf.res])
            V(lambda e, src=src: e.tensor_sub(yf.ap, src.ap, yf.ap), r=[src.res, yf.res], w=[yf.res])
            ACT(lambda e, dst=dst: e.activation(dst.ap, yf.ap, AF.Sin, bias=0.0, scale=TWO_PI_S), r=[yf.res], w=[dst.res])

    m_init = A.mark()
    posi = A.alloc("posi", [128, NTT], I32); posf = A.alloc("posf", [128, NTT], F32)
    dma(posi.ap, pos_in.rearrange("b (t p) -> p (b t)", p=128), writes=[posi.res], ncont=True)
    V(lambda e: e.tensor_copy(posf.ap, posi.ap), r=[posi.res], w=[posf.res])
    angM = A.alloc("angM", [128, NTT * 16], F32); angD = A.alloc("angD", [128, NTT * 32], F32)
    for j in range(NTT):
        V(lambda e, j=j: e.tensor_scalar(angM.ap[:, 16 * j:16 * j + 16], misc.ap[:, 160:176], posf.ap[:, j:j + 1], INV2PI, ALU.mult, ALU.mult),
          r=[misc.res, posf.res], w=[angM.res], g="angM")
        V(lambda e, j=j: e.tensor_scalar(angD.ap[:, 32 * j:32 * j + 32], misc.ap[:, 192:224], posf.ap[:, j:j + 1], INV2PI, ALU.mult, ALU.mult),
          r=[misc.res, posf.res], w=[angD.res], g="angD")
    sM = Tile(sinM.ap.rearrange("p a b -> p (a b)"), sinM.res); cM = Tile(cosM.ap.rearrange("p a b -> p (a b)"), cosM.res)
    sDt = Tile(sinD.ap.rearrange("p a b -> p (a b)"), sinD.res); cDt = Tile(cosD.ap.rearrange("p a b -> p (a b)"), cosD.res)
    scr0 = sincos_scratch(NTT * 32)
    sincos(sM, cM, angM, NTT * 16, scr0)
    sincos(sDt, cDt, angD, NTT * 32, scr0)
    dbg("posf", posf.ap, [posf.res]); dbg("angM", angM.ap, [angM.res]); dbg("sinM", sM.ap, [sinM.res]); dbg("cosM", cM.ap, [cosM.res])
    craw = A.alloc("craw", [128, NB, 8], F32)
    for b_ in range(NB):
        dma(craw.ap[:, b_, :], c_in[b_].rearrange("(k p) -> p k", p=128), writes=[craw.res], ncont=True, wgroup="craw")
    ACT(lambda e: e.activation(cT.ap, craw.ap, AF.Silu), r=[craw.res], w=[cT.res])
    P.barrier()
    A.reset(m_init)

    norm1B = A.alloc("norm1B", [128, D], F32); norm2B = A.alloc("norm2B", [128, D], F32)
    mixB = A.alloc("mixB", [128, D], F32)
    qgB = A.alloc("qgB", [128, QR], F32); kvgB = A.alloc("kvgB", [128, KVR], F32)
    gM = A.alloc("gM", [128, 2, QKM], F32); gD = A.alloc("gD", [128, 2, DD], F32)
    dB = A.alloc("dB", [128, CH], F32); bgluB = A.alloc("bgluB", [128, 2 * CH], F32)
    cw = A.alloc("cw", [128, 3, 44], F32); cb = A.alloc("cb", [128, 44], F32)
    modT = [A.alloc(f"modT{i}", [128, D], F32) for i in range(3)]
    stat = A.alloc("stat", [128, 64], F32)
    statres = [P.res(f"stat{i}") for i in range(16)]
    m_layer = A.mark()

    bc = lambda ap1d: ap1d.partition_broadcast(128)

    for l in range(depth):
        xin_ap = x_in if l == 0 else out
        last = (l == depth - 1)
        dma(norm1B.ap, bc(norm1[l]), writes=[norm1B.res]); dma(norm2B.ap, bc(norm2[l]), writes=[norm2B.res])
        dma(mixB.ap, bc(mix_norm[l]), writes=[mixB.res])
        dma(qgB.ap, bc(mla_q_norm[l]), writes=[qgB.res]); dma(kvgB.ap, bc(mla_kv_norm[l]), writes=[kvgB.res])
        dma(gM.ap.rearrange("p a b -> p (a b)"), bc(mla_qk_gain[l].rearrange("a b -> (a b)")), writes=[gM.res])
        dma(gD.ap.rearrange("p a b -> p (a b)"), bc(dil_qk_gain[l].rearrange("a b -> (a b)")), writes=[gD.res])
        dma(dB.ap, bc(ssm_d[l]), writes=[dB.res]); dma(bgluB.ap, bc(ssm_b_glu[l]), writes=[bgluB.res])
        dma(cw.ap, ffn_conv_w[l].rearrange("j (f p) -> p j f", p=128), writes=[cw.res], ncont=True)
        dma(cb.ap, ffn_conv_b[l].rearrange("(f p) -> p f", p=128), writes=[cb.res], ncont=True)

        m = A.mark()
        CHK = 4096
        stg_f = A.ring("stgf", [128, CHK], F32, 4); stg_b = A.ring("stgb", [128, CHK], BF16, 4)
        ci = 0
        for (src, dst, key, rows, cols) in ((w_in[l], win_b, "win_b", D, INW), (mla_w_uq[l], wuq_b, "wuq_b", QR, HM * QKM),
                                            (mla_w_ukv[l], wukv_b, "wukv_b", KVR, HM * 128), (ssm_w_glu[l], wglu_b, "wglu_b", CH, 2 * CH),
                                            (w_out[l], wout_b, "wout_b", D, D),
                                            (ffn_w_down[l], wdown_b, "wdown_b", FF, D)):
            n = rows * cols
            per = n // 128
            assert per * 128 == n
            sflat = src.rearrange("r c -> (r c)").rearrange("(p x) -> p x", p=128)
            dflat = dst.rearrange("r c -> (r c)").rearrange("(p x) -> p x", p=128)
            o0 = 0
            while o0 < per:
                w_ = min(CHK, per - o0)
                sf = stg_f.next(); sb = stg_b.next()
                dma(sf.ap[:, 0:w_], sflat[:, o0:o0 + w_], writes=[sf.res])
                eng = ("dve", "act")[ci % 2]
                ci += 1
                if eng == "act":
                    ACT(lambda e, sf=sf, sb=sb, w_=w_: e.copy(sb.ap[:, 0:w_], sf.ap[:, 0:w_]), r=[sf.res], w=[sb.res])
                else:
                    P.op(eng, lambda e, sf=sf, sb=sb, w_=w_: e.tensor_copy(sb.ap[:, 0:w_], sf.ap[:, 0:w_]), reads=[sf.res], writes=[sb.res])
                dma(dflat[:, o0:o0 + w_], sb.ap[:, 0:w_], reads=[sb.res], writes=[dres[key]], wgroup="wcast")
                o0 += w_
        for c in range(11):
            sf = stg_f.next(); sb = stg_b.next()
            dma(sf.ap.rearrange("p (k n) -> p k n", k=8), ffn_w_up[l][:, 512 * c:512 * c + 512].rearrange("(k p) n -> p k n", p=128), writes=[sf.res])
            src4 = sf.ap.rearrange("p (k f n) -> p k f n", k=8, f=4)
            dst4 = sb.ap.rearrange("p (f k n) -> p k f n", f=4, k=8)
            eng = ("dve", "act")[ci % 2]
            ci += 1
            if eng == "act":
                ACT(lambda e, src4=src4, dst4=dst4: e.copy(dst4, src4), r=[sf.res], w=[sb.res])
            else:
                P.op(eng, lambda e, src4=src4, dst4=dst4: e.tensor_copy(dst4, src4), reads=[sf.res], writes=[sb.res])
            dma(wup_b[4 * c:4 * c + 4].rearrange("f p x -> p f x"), sb.ap.rearrange("p (f x) -> p f x", f=4), reads=[sb.res], writes=[dres["wup_b"]], wgroup="wcast")
        P.barrier()
        A.reset(m)

        m = A.mark()
        wm = A.ring("wm", [128, 8, 512], F32, 2)
        modS = A.alloc("modS", [NB, 6 * D], F32); bmS = A.alloc("bmS", [NB, 6 * D], F32)
        dma(bmS.ap, b_mod[l].partition_broadcast(NB), writes=[bmS.res])
        for cc in range(12):
            wt = wm.next()
            dma(wt.ap, w_mod[l][:, 512 * cc:512 * (cc + 1)].rearrange("(k p) n -> p k n", p=128), writes=[wt.res])
            bk = cc % 2
            for k in range(8):
                PE(lambda e, k=k, wt=wt, bk=bk: e.matmul(bankF(bk)[0:NB, :], cT.ap[:, :, k], wt.ap[:, k, :], start=(k == 0), stop=(k == 7)),
                   r=[cT.res, wt.res], w=[psres[bk]], g=("modmm", l, cc))
            V(lambda e, cc=cc, bk=bk: e.tensor_add(modS.ap[:, 512 * cc:512 * (cc + 1)], bankF(bk)[0:NB, :], bmS.ap[:, 512 * cc:512 * (cc + 1)]),
              r=[psres[bk], bmS.res], w=[modS.res], g="modS")
        dma(modD, modS.ap, reads=[modS.res], writes=[dres["modD"]])
        dbg(f"mod{l}", modS.ap, [modS.res])
        dbg("cT", cT.ap.rearrange("p a b -> p (a b)"), [cT.res])
        A.reset(m)
        P.barrier()
        if stop_after == "mod":
            break

        def load_mod(tile, b, idx, plus_one_times=None):
            dma(tile.ap, modD[b, idx * D:(idx + 1) * D].partition_broadcast(128), reads=[dres["modD"]], writes=[tile.res])
            if plus_one_times is not None:
                V(lambda e: e.scalar_tensor_tensor(tile.ap, tile.ap, 1.0, plus_one_times.ap, ALU.add, ALU.mult),
                  r=[tile.res, plus_one_times.res], w=[tile.res])

        def phase_proj(b):
            m = A.mark()
            G1, SH1 = modT[0], modT[1]
            load_mod(G1, b, 1, norm1B)
            load_mod(SH1, b, 0)
            win_sb = A.alloc("win_sb", [128, 8, INW], BF16)
            wuq_sb = A.alloc("wuq_sb", [128, 2, HM * QKM], BF16)
            wukv_sb = A.alloc("wukv_sb", [128, HM * 128], BF16)
            dma(win_sb.ap, win_b.rearrange("(k p) n -> p k n", p=128), reads=[dres["win_b"]], writes=[win_sb.res])
            dma(wuq_sb.ap[:, 0, :], wuq_b[0:128, :], reads=[dres["wuq_b"]], writes=[wuq_sb.res], wgroup="wuq")
            dma(wuq_sb.ap[0:64, 1, :], wuq_b[128:192, :], reads=[dres["wuq_b"]], writes=[wuq_sb.res], wgroup="wuq")
            dma(wukv_sb.ap, wukv_b, reads=[dres["wukv_b"]], writes=[wukv_sb.res])
            xt_r = A.ring("xt", [128, D], F32, 2)
            junk = A.alloc("junk", [128, D], BF16)
            hf_r = A.ring("hf", [128, D], F32, 2)
            hb_r = A.ring("hb", [128, D], BF16, 2)
            hT_r = A.ring("hT", [128, 8, 128], BF16, 2)
            u_sb_r = A.ring("u_sb", [128, INW], F32, 2)
            sq_r = A.ring("sq", [128, 768], F32, 2)
            cn_r = A.ring("cn", [128, 320], BF16, 2)
            cTs_r = A.ring("cTs", [128, 3, 128], BF16, 2)
            q_sb_r = A.ring("q_sb", [128, HM, QKM], F32, 2)
            kv_sb_r = A.ring("kv_sb", [128, HM, 128], F32, 2)
            qn_r = A.ring("qn", [128, HM, QKM], F32, 2)
            qbf_r = A.ring("qbf", [128, HM, QKM], BF16, 2)
            kbf_r = A.ring("kbf", [128, HM, QKM], BF16, 2)
            kr_r = A.ring("kr", [128, 4, 32], F32, 2)
            vbf_r = A.ring("vbf", [128, HM, 65], BF16, 2)
            vdbf_r = A.ring("vdbf", [128, HD, 65], BF16, 2)
            for t_ in vbf_r.tiles + vdbf_r.tiles:
                GP(lambda e, t_=t_: e.memset(t_.ap, 1.0), w=[t_.res])
            qkT_r = A.ring("qkT", [128, 2, HM, 128], BF16, 2)
            dn_r = A.ring("dn", [128, 2, HD, DD], F32, 2)
            dbf_r = A.ring("dbf", [128, 2, HD, DD], BF16, 2)
            rt_r = A.ring("rt", [128, 2, HD, 32], F32, 2)
            rt2_r = A.ring("rt2", [128, 2, HD, 32], F32, 2)
            dT_r = A.ring("dT", [128, 2, 3, 128], BF16, 2)
            ucb_r = A.ring("ucb", [128, CH], BF16, 2)
            ucT_r = A.ring("ucTs", [128, 2, 128], BF16, 2)
            S_ = lambda i, n=1: stat.ap[:, 4 * i:4 * i + n]

            def tile_gen(tl):
                tt = b * (S // 128) + tl
                t0 = tl * 128
                hf = hf_r.next(); hb = hb_r.next(); u_sb = u_sb_r.next(); sq = sq_r.next(); cn = cn_r.next(); cTs = cTs_r.next()
                q_sb = q_sb_r.next(); kv_sb = kv_sb_r.next(); qn = qn_r.next(); qbf = qbf_r.next(); kbf = kbf_r.next(); kr = kr_r.next()
                dn = dn_r.next(); dbf = dbf_r.next(); rt = rt_r.next(); rt2 = rt2_r.next(); ucb = ucb_r.next()
                xt = xt_r.next()
                dma(xt.ap, xin_ap[tt * 128:(tt + 1) * 128, :], reads=[xres_tt[tt]], writes=[xt.res])
                ACT(lambda e, xt=xt: e.activation(junk.ap, xt.ap, AF.Square, accum_out=S_(0)), r=[xt.res], w=[junk.res, statres[0]])
                ACT(lambda e: e.activation(S_(1), S_(0), AF.Sqrt, bias=EPS, scale=1.0 / D), r=[statres[0]], w=[statres[1]])
                V(lambda e: e.reciprocal(S_(2), S_(1)), r=[statres[1]], w=[statres[2]])
                V(lambda e, xt=xt: e.scalar_tensor_tensor(hf.ap, xt.ap, S_(2), G1.ap, ALU.mult, ALU.mult), r=[xt.res, statres[2], G1.res], w=[hf.res])
                V(lambda e: e.tensor_tensor(hb.ap, hf.ap, SH1.ap, ALU.add), r=[hf.res, SH1.res], w=[hb.res])
                if tl == 0 and b == 0:
                    dbg(f"h{l}", hb.ap, [hb.res])
                yield
                for k in range(8):
                    PE(lambda e, k=k: e.transpose(psB[:, 128 * k:128 * (k + 1)], hb.ap[:, 128 * k:128 * (k + 1)], ident.ap),
                       r=[hb.res, ident.res], w=[psres[6]], g=("hTt", l, tt))
                hT = hT_r.next()
                ACT(lambda e, hT=hT: e.copy(hT.ap.rearrange("p a b -> p (a b)"), psB[:, 0:1024]), r=[psres[6]], w=[hT.res])
                for ci_, (c0, c1) in enumerate(((0, 512), (512, 1024), (1024, 1536), (1536, INW))):
                    for k in range(8):
                        PE(lambda e, k=k, c0=c0, c1=c1, hT=hT: e.matmul(psF[:, c0:c1], hT.ap[:, k, :], win_sb.ap[:, k, c0:c1], start=(k == 0), stop=(k == 7)),
                           r=[hT.res, win_sb.res], w=[psres[ci_]], g=("umm", l, tt, ci_))
                yield
                ACT(lambda e: e.copy(u_sb.ap[:, 0:1024], psF[:, 0:1024]), r=[psres[0], psres[1]], w=[u_sb.res], g=("uev", l, tt))
                V(lambda e: e.tensor_copy(u_sb.ap[:, 1024:INW], psF[:, 1024:INW]), r=[psres[2], psres[3]], w=[u_sb.res], g=("uev", l, tt))
                if tl == 0 and b == 0:
                    dbg(f"u{l}", u_sb.ap, [u_sb.res])
                yield
                ACT(lambda e: e.activation(sq.ap[:, 0:QR], u_sb.ap[:, 0:QR], AF.Square, accum_out=S_(3)), r=[u_sb.res], w=[sq.res, statres[3]])
                ACT(lambda e: e.activation(sq.ap[:, 0:KVR], u_sb.ap[:, OFF_CKV:OFF_CKV + KVR], AF.Square, accum_out=S_(4)), r=[u_sb.res], w=[sq.res, statres[4]])
                ACT(lambda e: e.activation(S_(5), S_(3), AF.Sqrt, bias=EPS, scale=1.0 / QR), r=[statres[3]], w=[statres[5]])
                ACT(lambda e: e.activation(S_(6), S_(4), AF.Sqrt, bias=EPS, scale=1.0 / KVR), r=[statres[4]], w=[statres[6]])
                V(lambda e: e.reciprocal(S_(5), S_(5)), r=[statres[5]], w=[statres[5]])
                V(lambda e: e.reciprocal(S_(6), S_(6)), r=[statres[6]], w=[statres[6]])
                V(lambda e: e.scalar_tensor_tensor(cn.ap[:, 0:QR], u_sb.ap[:, 0:QR], S_(5), qgB.ap, ALU.mult, ALU.mult),
                  r=[u_sb.res, statres[5], qgB.res], w=[cn.res], g=("cn", l, tt))
                V(lambda e: e.scalar_tensor_tensor(cn.ap[:, QR:QR + KVR], u_sb.ap[:, OFF_CKV:OFF_CKV + KVR], S_(6), kvgB.ap, ALU.mult, ALU.mult),
                  r=[u_sb.res, statres[6], kvgB.res], w=[cn.res], g=("cn", l, tt))
                yield
                PE(lambda e: e.transpose(psB[:, 1024:1152], cn.ap[:, 0:128], ident.ap), r=[cn.res, ident.res], w=[psres[7]], g=("cTt", l, tt))
                PE(lambda e: e.transpose(psB[0:64, 1152:1280], cn.ap[:, 128:192], ident.ap), r=[cn.res, ident.res], w=[psres[7]], g=("cTt", l, tt))
                PE(lambda e: e.transpose(psB[:, 1280:1408], cn.ap[:, 192:320], ident.ap), r=[cn.res, ident.res], w=[psres[7]], g=("cTt", l, tt))
                ACT(lambda e: e.copy(cTs.ap.rearrange("p a b -> p (a b)"), psB[:, 1024:1408]), r=[psres[7]], w=[cTs.res])
                yield
                PE(lambda e: e.matmul(psF[:, 2048:2560], cTs.ap[:, 0, :], wuq_sb.ap[:, 0, 0:512], start=True, stop=False), r=[cTs.res, wuq_sb.res], w=[psres[4]], g=("qp", l, tt))
                PE(lambda e: e.matmul(psF[:, 2048:2560], cTs.ap[0:64, 1, :], wuq_sb.ap[0:64, 1, 0:512], start=False, stop=True), r=[cTs.res, wuq_sb.res], w=[psres[4]], g=("qp", l, tt))
                PE(lambda e: e.matmul(psF[:, 2560:2624], cTs.ap[:, 0, :], wuq_sb.ap[:, 0, 512:576], start=True, stop=False), r=[cTs.res, wuq_sb.res], w=[psres[5]], g=("qp2", l, tt))
                PE(lambda e: e.matmul(psF[:, 2560:2624], cTs.ap[0:64, 1, :], wuq_sb.ap[0:64, 1, 512:576], start=False, stop=True), r=[cTs.res, wuq_sb.res], w=[psres[5]], g=("qp2", l, tt))
                ACT(lambda e: e.copy(q_sb.ap.rearrange("p a b -> p (a b)"), psF[:, 2048:2624]), r=[psres[4], psres[5]], w=[q_sb.res])
                PE(lambda e: e.matmul(psF[:, 2048:2560], cTs.ap[:, 2, :], wukv_sb.ap[:, 0:512], start=True, stop=True), r=[cTs.res, wukv_sb.res], w=[psres[4]])
                PE(lambda e: e.matmul(psF[:, 2560:2816], cTs.ap[:, 2, :], wukv_sb.ap[:, 512:768], start=True, stop=True), r=[cTs.res, wukv_sb.res], w=[psres[5]])
                V(lambda e: e.tensor_copy(kv_sb.ap.rearrange("p a b -> p (a b)"), psF[:, 2048:2816]), r=[psres[4], psres[5]], w=[kv_sb.res])
                yield
                sq3 = sq.ap[:, 0:HM * QKM].rearrange("p (a b) -> p a b", a=HM)
                V(lambda e: e.tensor_tensor(sq3, q_sb.ap, q_sb.ap, ALU.mult), r=[q_sb.res], w=[sq.res])
                V(lambda e: e.tensor_reduce(S_(7, 6), sq3, AX.X, ALU.add), r=[sq.res], w=[statres[7]])
                ACT(lambda e: e.activation(S_(7, 6), S_(7, 6), AF.Sqrt, bias=EPS, scale=1.0 / QKM), r=[statres[7]], w=[statres[7]])
                V(lambda e: e.reciprocal(S_(7, 6), S_(7, 6)), r=[statres[7]], w=[statres[7]])
                V(lambda e: e.tensor_tensor(qn.ap, q_sb.ap, S_(7, 6).unsqueeze(2).to_broadcast([128, HM, QKM]), ALU.mult), r=[q_sb.res, statres[7]], w=[qn.res])
                V(lambda e: e.tensor_tensor(qn.ap, qn.ap, gM.ap[:, 0:1, :].to_broadcast([128, HM, QKM]), ALU.mult), r=[qn.res, gM.res], w=[qn.res])
                cs_m = cosM.ap[:, tt:tt + 1, :].to_broadcast([128, HM, 16]); sn_m = sinM.ap[:, tt:tt + 1, :].to_broadcast([128, HM, 16])
                qa = qn.ap[:, :, 64:80]; qb = qn.ap[:, :, 80:96]
                r1 = rt.ap[:, 0, :, 0:16]; r2 = rt.ap[:, 0, :, 16:32]; r3 = rt.ap[:, 1, :, 0:16]; r4 = rt.ap[:, 1, :, 16:32]
                ACT(lambda e: e.copy(qbf.ap[:, :, 0:64], qn.ap[:, :, 0:64]), r=[qn.res], w=[qbf.res], g=("qbf", l, tt))
                V(lambda e, cs_m=cs_m: e.tensor_tensor(r1, qa, cs_m, ALU.mult), r=[qn.res, cosM.res], w=[rt.res], g=("rt", l, tt, 0))
                V(lambda e, sn_m=sn_m: e.tensor_tensor(r2, qb, sn_m, ALU.mult), r=[qn.res, sinM.res], w=[rt.res], g=("rt", l, tt, 0))
                V(lambda e, sn_m=sn_m: e.tensor_tensor(r3, qa, sn_m, ALU.mult), r=[qn.res, sinM.res], w=[rt.res], g=("rt", l, tt, 0))
                V(lambda e, cs_m=cs_m: e.tensor_tensor(r4, qb, cs_m, ALU.mult), r=[qn.res, cosM.res], w=[rt.res], g=("rt", l, tt, 0))
                V(lambda e: e.tensor_tensor(qbf.ap[:, :, 64:80], r1, r2, ALU.subtract), r=[rt.res], w=[qbf.res], g=("qbf", l, tt))
                V(lambda e: e.tensor_tensor(qbf.ap[:, :, 80:96], r3, r4, ALU.add), r=[rt.res], w=[qbf.res], g=("qbf", l, tt))
                yield
                sqk = sq.ap[:, 0:HM * 64].rearrange("p (a b) -> p a b", a=HM)
                V(lambda e: e.tensor_tensor(sqk, kv_sb.ap[:, :, 0:64], kv_sb.ap[:, :, 0:64], ALU.mult), r=[kv_sb.res], w=[sq.res])
                V(lambda e: e.tensor_reduce(S_(9, 6), sqk, AX.X, ALU.add), r=[sq.res], w=[statres[9]])
                ACT(lambda e: e.activation(kr.ap[:, 3, :], u_sb.ap[:, OFF_KR:OFF_KR + 32], AF.Square, accum_out=S_(11)), r=[u_sb.res], w=[kr.res, statres[11]], g=("kr", l, tt))
                V(lambda e: e.tensor_scalar(S_(9, 6), S_(9, 6), S_(11), None, ALU.add), r=[statres[9], statres[11]], w=[statres[9]])
                ACT(lambda e: e.activation(S_(9, 6), S_(9, 6), AF.Sqrt, bias=EPS, scale=1.0 / QKM), r=[statres[9]], w=[statres[9]])
                V(lambda e: e.reciprocal(S_(9, 6), S_(9, 6)), r=[statres[9]], w=[statres[9]])
                V(lambda e: e.tensor_tensor(qn.ap[:, :, 0:64], kv_sb.ap[:, :, 0:64], S_(9, 6).unsqueeze(2).to_broadcast([128, HM, 64]), ALU.mult),
                  r=[kv_sb.res, statres[9], qbf.res, rt.res], w=[qn.res])
                V(lambda e: e.tensor_tensor(kbf.ap[:, :, 0:64], qn.ap[:, :, 0:64], gM.ap[:, 1:2, 0:64].to_broadcast([128, HM, 64]), ALU.mult),
                  r=[qn.res, gM.res], w=[kbf.res], g=("kbf", l, tt))
                yield
                V(lambda e: e.tensor_tensor(kr.ap[:, 0, :], u_sb.ap[:, OFF_KR:OFF_KR + 32], gM.ap[:, 1, 64:96], ALU.mult), r=[u_sb.res, gM.res], w=[kr.res], g=("kr", l, tt))
                ka = kr.ap[:, 0, 0:16]; kb_ = kr.ap[:, 0, 16:32]
                c1 = cosM.ap[:, tt, :]; s1 = sinM.ap[:, tt, :]
                V(lambda e, c1=c1: e.tensor_tensor(kr.ap[:, 1, 0:16], ka, c1, ALU.mult), r=[kr.res, cosM.res], w=[kr.res])
                V(lambda e, s1=s1: e.tensor_tensor(kr.ap[:, 1, 16:32], kb_, s1, ALU.mult), r=[kr.res, sinM.res], w=[kr.res])
                V(lambda e, s1=s1: e.tensor_tensor(kr.ap[:, 2, 0:16], ka, s1, ALU.mult), r=[kr.res, sinM.res], w=[kr.res])
                V(lambda e, c1=c1: e.tensor_tensor(kr.ap[:, 2, 16:32], kb_, c1, ALU.mult), r=[kr.res, cosM.res], w=[kr.res])
                V(lambda e: e.tensor_tensor(kr.ap[:, 3, 0:16], kr.ap[:, 1, 0:16], kr.ap[:, 1, 16:32], ALU.subtract), r=[kr.res], w=[kr.res])
                V(lambda e: e.tensor_tensor(kr.ap[:, 3, 16:32], kr.ap[:, 2, 0:16], kr.ap[:, 2, 16:32], ALU.add), r=[kr.res], w=[kr.res])
                V(lambda e: e.tensor_tensor(kbf.ap[:, :, 64:96], kr.ap[:, 3:4, :].to_broadcast([128, HM, 32]), S_(9, 6).unsqueeze(2).to_broadcast([128, HM, 32]), ALU.mult),
                  r=[kr.res, statres[9]], w=[kbf.res], g=("kbf", l, tt))
                vbf = vbf_r.next()
                ACT(lambda e, vbf=vbf: e.copy(vbf.ap[:, :, 0:64], kv_sb.ap[:, :, 64:128]), r=[kv_sb.res], w=[vbf.res])
                dma(vmD[b, t0:t0 + 128, :], vbf.ap.rearrange("p a b -> p (a b)"), reads=[vbf.res], writes=[dres[("vmD", b)]], wgroup="vm")
                if tl == 0 and b == 0:
                    dbg(f"qbf{l}", qbf.ap.rearrange("p a b -> p (a b)"), [qbf.res])
                    dbg(f"kbf{l}", kbf.ap.rearrange("p a b -> p (a b)"), [kbf.res])
                yield
                for h in range(HM):
                    PE(lambda e, h=h: e.transpose(psB[0:QKM, 128 * h:128 * (h + 1)], qbf.ap[:, h, :], ident.ap), r=[qbf.res, ident.res], w=[psres[6]], g=("qTt", l, tt))
                    PE(lambda e, h=h: e.transpose(psB[0:QKM, 1024 + 128 * h:1024 + 128 * (h + 1)], kbf.ap[:, h, :], ident.ap), r=[kbf.res, ident.res], w=[psres[7]], g=("kTt", l, tt))
                qkT = qkT_r.next()
                ACT(lambda e, qkT=qkT: e.copy(qkT.ap[0:QKM, 0, :, :].rearrange("p a b -> p (a b)"), psB[0:QKM, 0:768]), r=[psres[6]], w=[qkT.res], g=("qkT", l, tt))
                V(lambda e, qkT=qkT: e.tensor_copy(qkT.ap[0:QKM, 1, :, :].rearrange("p a b -> p (a b)"), psB[0:QKM, 1024:1792]), r=[psres[7]], w=[qkT.res], g=("qkT", l, tt))
                dma(qmT[b, :, :, t0:t0 + 128].rearrange("h d t -> d h t"), qkT.ap[0:QKM, 0, :, :], reads=[qkT.res], writes=[dres[("qmT", b)]], wgroup="qm")
                dma(kmT[b, :, :, t0:t0 + 128].rearrange("h d t -> d h t"), qkT.ap[0:QKM, 1, :, :], reads=[qkT.res], writes=[dres[("kmT", b)]], wgroup="km")
                yield
                dqk = u_sb.ap[:, OFF_DQ:OFF_DQ + 768].rearrange("p (s h d) -> p s h d", s=2, h=HD)
                sq4 = sq.ap[:, 0:768].rearrange("p (s h d) -> p s h d", s=2, h=HD)
                V(lambda e: e.tensor_tensor(sq.ap[:, 0:768], u_sb.ap[:, OFF_DQ:OFF_DQ + 768], u_sb.ap[:, OFF_DQ:OFF_DQ + 768], ALU.mult), r=[u_sb.res], w=[sq.res])
                V(lambda e: e.tensor_reduce(S_(12, 12), sq.ap[:, 0:768].rearrange("p (a b) -> p a b", a=12), AX.X, ALU.add), r=[sq.res], w=[statres[12]])
                ACT(lambda e: e.activation(S_(12, 12), S_(12, 12), AF.Sqrt, bias=EPS, scale=1.0 / DD), r=[statres[12]], w=[statres[12]])
                V(lambda e: e.reciprocal(S_(12, 12), S_(12, 12)), r=[statres[12]], w=[statres[12]])
                dn3 = dn.ap.rearrange("p s h d -> p (s h) d")
                V(lambda e: e.tensor_tensor(dn3, u_sb.ap[:, OFF_DQ:OFF_DQ + 768].rearrange("p (a b) -> p a b", a=12), S_(12, 12).unsqueeze(2).to_broadcast([128, 12, DD]), ALU.mult),
                  r=[u_sb.res, statres[12]], w=[dn.res])
                for s_ in range(2):
                    V(lambda e, s_=s_: e.tensor_tensor(dn.ap[:, s_, :, :], dn.ap[:, s_, :, :], gD.ap[:, s_:s_ + 1, :].to_broadcast([128, HD, DD]), ALU.mult),
                      r=[dn.res, gD.res], w=[dn.res])
                yield
                cs_d = cosD.ap[:, tt:tt + 1, :].to_broadcast([128, 12, 32]); sn_d = sinD.ap[:, tt:tt + 1, :].to_broadcast([128, 12, 32])
                da = dn3[:, :, 0:32]; db = dn3[:, :, 32:64]
                rt3 = rt.ap.rearrange("p s h d -> p (s h) d"); rt23 = rt2.ap.rearrange("p s h d -> p (s h) d")
                dbf3 = dbf.ap.rearrange("p s h d -> p (s h) d")
                V(lambda e, cs_d=cs_d: e.tensor_tensor(rt3, da, cs_d, ALU.mult), r=[dn.res, cosD.res, qbf.res], w=[rt.res])
                V(lambda e, sn_d=sn_d: e.tensor_tensor(rt23, db, sn_d, ALU.mult), r=[dn.res, sinD.res], w=[rt2.res])
                V(lambda e: e.tensor_tensor(dbf3[:, :, 0:32], rt3, rt23, ALU.subtract), r=[rt.res, rt2.res], w=[dbf.res], g=("dbf", l, tt))
                V(lambda e, sn_d=sn_d: e.tensor_tensor(rt3, da, sn_d, ALU.mult), r=[dn.res, sinD.res, dbf.res], w=[rt.res])
                V(lambda e, cs_d=cs_d: e.tensor_tensor(rt23, db, cs_d, ALU.mult), r=[dn.res, cosD.res, dbf.res], w=[rt2.res])
                V(lambda e: e.tensor_tensor(dbf3[:, :, 32:64], rt3, rt23, ALU.add), r=[rt.res, rt2.res], w=[dbf.res], g=("dbf", l, tt))
                vdbf = vdbf_r.next()
                ACT(lambda e, vdbf=vdbf: e.copy(vdbf.ap[:, :, 0:64], u_sb.ap[:, OFF_DV:OFF_DV + 384].rearrange("p (a b) -> p a b", a=HD)), r=[u_sb.res], w=[vdbf.res])
                dma(vdD[b, t0:t0 + 128, :], vdbf.ap.rearrange("p a b -> p (a b)"), reads=[vdbf.res], writes=[dres[("vdD", b)]], wgroup="vd")
                if tl == 0 and b == 0:
                    dbg(f"dbf{l}", dbf.ap.rearrange("p s h d -> p (s h d)"), [dbf.res])
                yield
                dbf2 = dbf.ap.rearrange("p s h d -> p s (h d)")
                for s_ in range(2):
                    for j in range(3):
                        PE(lambda e, s_=s_, j=j: e.transpose(psB[:, 1024 * s_ + 128 * j:1024 * s_ + 128 * (j + 1)], dbf2[:, s_, 128 * j:128 * (j + 1)], ident.ap),
                           r=[dbf.res, ident.res], w=[psres[6 + s_]], g=("dTt", l, tt, s_))
                dT = dT_r.next()
                ACT(lambda e, dT=dT: e.copy(dT.ap[:, 0, :, :].rearrange("p a b -> p (a b)"), psB[:, 0:384]), r=[psres[6]], w=[dT.res], g=("dT", l, tt))
                V(lambda e, dT=dT: e.tensor_copy(dT.ap[:, 1, :, :].rearrange("p a b -> p (a b)"), psB[:, 1024:1408]), r=[psres[7]], w=[dT.res], g=("dT", l, tt))
                dma(qdT[b, :, :, t0:t0 + 128].rearrange("j d t -> d j t"), dT.ap[:, 0, :, :], reads=[dT.res], writes=[dres[("qdT", b)]], wgroup="qd")
                dma(kdT[b, :, :, t0:t0 + 128].rearrange("j d t -> d j t"), dT.ap[:, 1, :, :], reads=[dT.res], writes=[dres[("kdT", b)]], wgroup="kd")
                yield
                dma(uctok[b, t0:t0 + 128, :], u_sb.ap[:, OFF_UC:OFF_UC + CH], reads=[u_sb.res], writes=[dres[("uctok", b)]], wgroup="uct")
                ACT(lambda e: e.copy(ucb.ap, u_sb.ap[:, OFF_UC:OFF_UC + CH]), r=[u_sb.res], w=[ucb.res])
                for hh in range(2):
                    PE(lambda e, hh=hh: e.transpose(psB[:, 512 + 128 * hh:512 + 128 * (hh + 1)], ucb.ap[:, 128 * hh:128 * (hh + 1)], ident.ap),
                       r=[ucb.res, ident.res], w=[psres[6]], g=("ucTt", l, tt))
                ucTs = ucT_r.next()
                ACT(lambda e, ucTs=ucTs: e.copy(ucTs.ap.rearrange("p a b -> p (a b)"), psB[:, 512:768]), r=[psres[6]], w=[ucTs.res])
                dma(ucT[b, :, t0:t0 + 128].rearrange("(hh p) t -> p hh t", p=128), ucTs.ap, reads=[ucTs.res], writes=[dres[("ucT", b)]], wgroup="uc")
            active = []
            nxt = 0
            ntile = S // 128
            while active or nxt < ntile:
                if len(active) < 2 and nxt < ntile:
                    active.append(tile_gen(nxt))
                    nxt += 1
                for g_ in list(active):
                    try:
                        next(g_)
                    except StopIteration:
                        active.remove(g_)
            A.reset(m)
            P.barrier()

        def phase_attn(b, kind):
            m = A.mark()
            if kind == "mla":
                KD, scale, off = QKM, QKM ** -0.5, 0
                qsrc, ksrc, vsrc, rq, rk_, rv = qmT, kmT, vmD, dres[("qmT", b)], dres[("kmT", b)], dres[("vmD", b)]
                qall = A.alloc("qall", [128, HM, S], BF16); kall = A.alloc("kall", [128, HM, S], BF16)
                dma(qall.ap[0:QKM], qsrc[b].rearrange("h d t -> d h t"), reads=[rq], writes=[qall.res])
                dma(kall.ap[0:QKM], ksrc[b].rearrange("h d t -> d h t"), reads=[rk_], writes=[kall.res])
                qv = lambda h, c0, c1: qall.ap[0:QKM, h, c0:c1]
                kv_ = lambda h, c0, c1: kall.ap[0:QKM, h, c0:c1]
            else:
                KD, scale, off = DD, DD ** -0.5, 384
                qsrc, ksrc, vsrc, rq, rk_, rv = qdT, kdT, vdD, dres[("qdT", b)], dres[("kdT", b)], dres[("vdD", b)]
                qall = A.alloc("qall", [128, 3, S], BF16); kall = A.alloc("kall", [128, 3, S], BF16)
                dma(qall.ap, qsrc[b].rearrange("j d t -> d j t"), reads=[rq], writes=[qall.res])
                dma(kall.ap, ksrc[b].rearrange("j d t -> d j t"), reads=[rk_], writes=[kall.res])
                qv = lambda h, c0, c1: qall.ap[64 * (h % 2):64 * (h % 2) + 64, h // 2, c0:c1]
                kv_ = lambda h, c0, c1: kall.ap[64 * (h % 2):64 * (h % 2) + 64, h // 2, c0:c1]
            Vs = A.alloc("Vs", [128, 16, 6 * 65], BF16)
            dma(Vs.ap, vsrc[b].rearrange("(t p) c -> p t c", p=128), reads=[rv], writes=[Vs.res])
            pT_r = A.ring("pT", [128, 1024], BF16, 4)
            pTm_r = A.ring("pTm", [128, 1024], BF16, 4)
            osb_r = A.ring("osb", [128, 4, 65], F32, 2)
            oa = A.alloc("oa", [128, 4, 384], F32)
            rden = A.alloc("rden", [128, 4], F32)
            mx_r = A.ring("mx", [128, 4, 384], BF16, 2)
            junk2 = A.alloc("junk2", [128, 384], BF16)
            steps = []
            for qc in range(4):
                for h in range(6):
                    kts = []
                    for kt in range(16):
                        if kind == "dil":
                            dmin = 128 * kt - 512 * qc - 511
                            dmax = 128 * kt + 127 - 512 * qc
                            if dmin > 1024 or dmax < -1024:
                                continue
                        kts.append(kt)
                    prs = [kts[i:i + 2] for i in range(0, len(kts), 2)]
                    for pi, pr in enumerate(prs):
                        steps.append((qc, h, pr, pi == 0, pi == len(prs) - 1))

            SPAIR = (0, 2, 6)

            def spair_ap(pb, ncol):
                return psF[:, 512 * pb:512 * pb + ncol] if pb < 6 else psB_f[:, 0:ncol]

            def emit_S(si):
                qc, h, pr, _, _ = steps[si]
                pb = SPAIR[si % 3]
                for i_, kt in enumerate(pr):
                    PE(lambda e, bk=pb + i_, h=h, kt=kt, qc=qc: e.matmul(bankF(bk), kv_(h, 128 * kt, 128 * kt + 128), qv(h, 512 * qc, 512 * qc + 512), start=True, stop=True),
                       r=[qall.res, kall.res], w=[psres[pb + i_]])

            S_ = lambda i, n=1: stat.ap[:, 4 * i:4 * i + n]
            ocnt = 0
            emit_S(0)
            emit_S(1)
            for si, (qc, h, pr, first, last) in enumerate(steps):
                if si + 2 < len(steps):
                    emit_S(si + 2)
                pb = SPAIR[si % 3]
                npr = len(pr)
                ob = 4 + (ocnt % 2)
                pT = pT_r.next()
                ACT(lambda e, pb=pb, pT=pT, npr=npr: e.activation(pT.ap[:, 0:512 * npr], spair_ap(pb, 512 * npr), AF.Exp, bias=0.0, scale=scale),
                    r=[psres[pb + i_] for i_ in range(npr)], w=[pT.res])
                if kind == "dil":
                    pm = pTm_r.next()
                    for i_, kt in enumerate(pr):
                        x0 = 1920 - 128 * (kt - 4 * qc)
                        eng = "dve"
                        P.op(eng, lambda e, pm=pm, pT=pT, x0=x0, i_=i_: e.tensor_tensor(pm.ap[:, 512 * i_:512 * i_ + 512], pT.ap[:, 512 * i_:512 * i_ + 512], maskS.ap[:, x0:x0 + 512], ALU.mult),
                             reads=[pT.res, maskS.res], writes=[pm.res], wgroup=("pm", l, b, si))
                    pT = pm
                for i_, kt in enumerate(pr):
                    for j in range(4):
                        PE(lambda e, j=j, pT=pT, kt=kt, h=h, ob=ob, i_=i_, st_=(first and i_ == 0 and j == 0), sp_=(last and i_ == npr - 1 and j == 3):
                           e.matmul(bankF(ob)[:, 65 * j:65 * j + 65], pT.ap[:, 512 * i_ + 128 * j:512 * i_ + 128 * j + 128], Vs.ap[:, kt, 65 * h:65 * h + 65], start=st_, stop=sp_),
                           r=[pT.res, Vs.res], w=[psres[ob]], g=("pv", l, b, kind, qc, h))
                if last:
                    ocnt += 1
                    osb = osb_r.next()
                    ACT(lambda e, osb=osb, ob=ob: e.copy(osb.ap.rearrange("p a b -> p (a b)"), bankF(ob)[:, 0:260]), r=[psres[ob]], w=[osb.res])
                    V(lambda e, osb=osb: e.reciprocal(rden.ap, osb.ap[:, :, 64]), r=[osb.res], w=[rden.res])
                    V(lambda e, osb=osb, h=h: e.tensor_tensor(oa.ap[:, :, 64 * h:64 * h + 64], osb.ap[:, :, 0:64], rden.ap.unsqueeze(2).to_broadcast([128, 4, 64]), ALU.mult),
                      r=[osb.res, rden.res], w=[oa.res], g=("oa", l, b, kind, qc))
                    if h == 5:
                        for j in range(4):
                            ACT(lambda e, j=j: e.activation(junk2.ap, oa.ap[:, j, :], AF.Square, accum_out=stat.ap[:, j:j + 1]),
                                r=[oa.res], w=[junk2.res, statres[0]], g=("oass", l, b, kind, qc))
                        ACT(lambda e: e.activation(stat.ap[:, 4:8], stat.ap[:, 0:4], AF.Sqrt, bias=EPS, scale=1.0 / 384), r=[statres[0]], w=[statres[1]])
                        V(lambda e: e.reciprocal(stat.ap[:, 4:8], stat.ap[:, 4:8]), r=[statres[1]], w=[statres[1]])
                        mx = mx_r.next()
                        V(lambda e: e.tensor_tensor(oa.ap, oa.ap, stat.ap[:, 4:8].unsqueeze(2).to_broadcast([128, 4, 384]), ALU.mult), r=[oa.res, statres[1]], w=[oa.res])
                        GP(lambda e, mx=mx: e.tensor_tensor(mx.ap, oa.ap, mixB.ap[:, off:off + 384].unsqueeze(1).to_broadcast([128, 4, 384]), ALU.mult), r=[oa.res, mixB.res], w=[mx.res])
                        dma(mixedD[b, 512 * qc:512 * qc + 512, off:off + 384].rearrange("(j p) c -> p j c", p=128), mx.ap, reads=[mx.res], writes=[dres[("mixedD", b)]], wgroup="mixw")
            A.reset(m)
            P.barrier()

        def phase_s5():
            m = A.mark()
            rr = A.alloc("rr", [128, 32], F32); th = A.alloc("th", [128, 32], F32); thb = A.alloc("thb", [128, 32], F32)
            LB = A.alloc("LB", [128, 32, 128], BF16); LBs = A.alloc("LBs", [128, 32, 128], BF16)
            W1T = A.alloc("W1T", [128, 4, 128], BF16); W2T = A.alloc("W2T", [128, 4, 128], BF16)
            scr5 = sincos_scratch(144)
            m_keep = A.mark()
            lre = A.alloc("lre", [128, 32], F32); lim = A.alloc("lim", [128, 32], F32); ldt = A.alloc("ldt", [128, 32], F32)
            for half in range(2):
                dma(lre.ap[64 * half:64 * half + 64, :], ssm_a_re[l].rearrange("d g p -> p (d g)"), writes=[lre.res], wgroup="lre", ncont=True)
                dma(lim.ap[64 * half:64 * half + 64, :], ssm_a_im[l].rearrange("d g p -> p (d g)"), writes=[lim.res], wgroup="lim", ncont=True)
            dma(ldt.ap, ssm_log_dt[l].rearrange("d g -> (d g)").partition_broadcast(128), writes=[ldt.res])
            dtt = A.alloc("dtt", [128, 32], F32)
            ACT(lambda e: e.activation(dtt.ap, ldt.ap, AF.Exp), r=[ldt.res], w=[dtt.res])
            V(lambda e: e.tensor_mul(rr.ap, lre.ap, dtt.ap), r=[lre.res, dtt.res], w=[rr.res])
            ACT(lambda e: e.activation(rr.ap, rr.ap, AF.Exp), r=[rr.res], w=[rr.res])
            V(lambda e: e.tensor_mul(th.ap, lim.ap, dtt.ap), r=[lim.res, dtt.res], w=[th.res])
            V(lambda e: e.tensor_scalar(th.ap, th.ap, INV2PI, None, ALU.mult), r=[th.res], w=[th.res])
            V(lambda e: e.tensor_scalar(thb.ap, th.ap, 1024.0, None, ALU.mult), r=[th.res], w=[thb.res])
            sn0 = A.alloc("sn0", [128, 32], F32); cs0 = A.alloc("cs0", [128, 32], F32)
            sincos(sn0, cs0, th, 32, scr5)
            kre = A.alloc("kre", [128, 32], F32); kim = A.alloc("kim", [128, 32], F32)
            t_a = A.alloc("t_a", [128, 32], F32); t_b = A.alloc("t_b", [128, 32], F32); den = A.alloc("den", [128, 32], F32)
            V(lambda e: e.tensor_mul(cs0.ap, cs0.ap, rr.ap), r=[cs0.res, rr.res], w=[cs0.res])
            V(lambda e: e.tensor_scalar(cs0.ap, cs0.ap, -1.0, None, ALU.add), r=[cs0.res], w=[cs0.res])
            V(lambda e: e.tensor_mul(sn0.ap, sn0.ap, rr.ap), r=[sn0.res, rr.res], w=[sn0.res])
            V(lambda e: e.tensor_mul(t_a.ap, lre.ap, lre.ap), r=[lre.res], w=[t_a.res])
            V(lambda e: e.tensor_mul(t_b.ap, lim.ap, lim.ap), r=[lim.res], w=[t_b.res])
            V(lambda e: e.tensor_add(den.ap, t_a.ap, t_b.ap), r=[t_a.res, t_b.res], w=[den.res])
            V(lambda e: e.reciprocal(den.ap, den.ap), r=[den.res], w=[den.res])
            V(lambda e: e.tensor_mul(t_a.ap, cs0.ap, lre.ap), r=[cs0.res, lre.res, den.res], w=[t_a.res])
            V(lambda e: e.tensor_mul(t_b.ap, sn0.ap, lim.ap), r=[sn0.res, lim.res], w=[t_b.res])
            V(lambda e: e.tensor_add(kre.ap, t_a.ap, t_b.ap), r=[t_a.res, t_b.res], w=[kre.res])
            V(lambda e: e.tensor_mul(kre.ap, kre.ap, den.ap), r=[kre.res, den.res], w=[kre.res])
            V(lambda e: e.tensor_mul(t_a.ap, sn0.ap, lre.ap), r=[sn0.res, lre.res, kre.res], w=[t_a.res])
            V(lambda e: e.tensor_mul(t_b.ap, cs0.ap, lim.ap), r=[cs0.res, lim.res, kre.res], w=[t_b.res])
            V(lambda e: e.tensor_sub(kim.ap, t_a.ap, t_b.ap), r=[t_a.res, t_b.res], w=[kim.res])
            V(lambda e: e.tensor_mul(kim.ap, kim.ap, den.ap), r=[kim.res, den.res], w=[kim.res])
            bre = A.alloc("bre", [64, 32, GC], F32); bim = A.alloc("bim", [64, 32, GC], F32)
            dma(bre.ap, ssm_b_re[l].rearrange("d g p c -> p (d g) c"), writes=[bre.res])
            dma(bim.ap, ssm_b_im[l].rearrange("d g p c -> p (d g) c"), writes=[bim.res])
            Bre = A.alloc("Bre", [64, 32, GC], BF16); Bim = A.alloc("Bim", [64, 32, GC], BF16); Bren = A.alloc("Bren", [64, 32, GC], BF16)
            tb1 = A.alloc("tb1", [64, 32, GC], F32); tb2 = A.alloc("tb2", [64, 32, GC], F32)
            kre_b = kre.ap[0:64, :].unsqueeze(2).to_broadcast([64, 32, GC]); kim_b = kim.ap[0:64, :].unsqueeze(2).to_broadcast([64, 32, GC])
            V(lambda e: e.tensor_tensor(tb1.ap, bre.ap, kre_b, ALU.mult), r=[bre.res, kre.res], w=[tb1.res])
            V(lambda e: e.tensor_tensor(tb2.ap, bim.ap, kim_b, ALU.mult), r=[bim.res, kim.res], w=[tb2.res])
            V(lambda e: e.tensor_tensor(Bre.ap, tb1.ap, tb2.ap, ALU.subtract), r=[tb1.res, tb2.res], w=[Bre.res])
            V(lambda e: e.tensor_tensor(Bren.ap, tb2.ap, tb1.ap, ALU.subtract), r=[tb1.res, tb2.res], w=[Bren.res])
            V(lambda e: e.tensor_tensor(tb1.ap, bim.ap, kre_b, ALU.mult), r=[bim.res, kre.res, Bre.res, Bren.res], w=[tb1.res])
            V(lambda e: e.tensor_tensor(tb2.ap, bre.ap, kim_b, ALU.mult), r=[bre.res, kim.res, Bre.res, Bren.res], w=[tb2.res])
            V(lambda e: e.tensor_tensor(Bim.ap, tb1.ap, tb2.ap, ALU.add), r=[tb1.res, tb2.res], w=[Bim.res])
            for blk in range(4):
                sl = lambda t_, blk=blk: t_.ap[:, 8 * blk:8 * blk + 8, :].rearrange("p a b -> p (a b)")
                s_re, s_im, s_ren = sl(Bre), sl(Bim), sl(Bren)
                PE(lambda e, a_=s_re: e.transpose(psB[:, 0:64], a_, ident.ap[0:64, 0:64]), r=[Bre.res, ident.res], w=[psres[6]], g=("Bt", l, blk))
                PE(lambda e, a_=s_im: e.transpose(psB[:, 64:128], a_, ident.ap[0:64, 0:64]), r=[Bim.res, ident.res], w=[psres[6]], g=("Bt", l, blk))
                PE(lambda e, a_=s_im: e.transpose(psB[:, 128:192], a_, ident.ap[0:64, 0:64]), r=[Bim.res, ident.res], w=[psres[6]], g=("Bt", l, blk))
                PE(lambda e, a_=s_ren: e.transpose(psB[:, 192:256], a_, ident.ap[0:64, 0:64]), r=[Bren.res, ident.res], w=[psres[6]], g=("Bt", l, blk))
                for g8 in range(8):
                    gd = blk * 8 + g8
                    V(lambda e, gd=gd, g8=g8: e.tensor_scalar(LB.ap[:, gd, :], psB[:, 0:128], misc.ap[:, 144 + g8:145 + g8], None, ALU.mult), r=[psres[6], misc.res], w=[LB.res], g=("LB", l))
                    V(lambda e, gd=gd, g8=g8: e.tensor_scalar(LBs.ap[:, gd, :], psB[:, 128:256], misc.ap[:, 144 + g8:145 + g8], None, ALU.mult), r=[psres[6], misc.res], w=[LBs.res], g=("LBs", l))
            Cin = A.alloc("Cin", [128, 4, 128], F32)
            dma(Cin.ap[:, :, 0:64], ssm_c_re[l].rearrange("d g c p -> (d g c) p").rearrange("(k q) p -> q k p", q=128), writes=[Cin.res], wgroup="cin")
            dma(Cin.ap[:, :, 64:128], ssm_c_im[l].rearrange("d g c p -> (d g c) p").rearrange("(k q) p -> q k p", q=128), writes=[Cin.res], wgroup="cin")
            W1s = A.alloc("W1s", [128, 4, 128], BF16); W2s = A.alloc("W2s", [128, 4, 128], BF16)
            V(lambda e: e.tensor_copy(W1s.ap[:, :, 0:64], Cin.ap[:, :, 0:64]), r=[Cin.res], w=[W1s.res], g="w1s")
            V(lambda e: e.tensor_scalar(W1s.ap[:, :, 64:128], Cin.ap[:, :, 64:128], -1.0, None, ALU.mult), r=[Cin.res], w=[W1s.res], g="w1s")
            V(lambda e: e.tensor_scalar(W2s.ap[:, :, 0:64], Cin.ap[:, :, 64:128], -1.0, None, ALU.mult), r=[Cin.res], w=[W2s.res], g="w2s")
            V(lambda e: e.tensor_scalar(W2s.ap[:, :, 64:128], Cin.ap[:, :, 0:64], -1.0, None, ALU.mult), r=[Cin.res], w=[W2s.res], g="w2s")
            for blk in range(4):
                PE(lambda e, blk=blk: e.transpose(psB[:, 1024:1152], W1s.ap[:, blk, :], ident.ap), r=[W1s.res, ident.res], w=[psres[7]], g=("Wt", l, blk))
                PE(lambda e, blk=blk: e.transpose(psB[:, 1152:1280], W2s.ap[:, blk, :], ident.ap), r=[W2s.res, ident.res], w=[psres[7]], g=("Wt", l, blk))
                V(lambda e, blk=blk: e.tensor_copy(W1T.ap[:, blk, :], psB[:, 1024:1152]), r=[psres[7]], w=[W1T.res], g="W1T")
                V(lambda e, blk=blk: e.tensor_copy(W2T.ap[:, blk, :], psB[:, 1152:1280]), r=[psres[7]], w=[W2T.res], g="W2T")
            P.barrier()
            A.reset(m_keep)
            m_main = A.mark()

            def rev(ap2d, c0, n):
                a = ap2d[:, c0:c0 + n]
                return bass.AP(a.tensor, a.offset + (n - 1) * a.ap[-1][0], [list(a.ap[0]), [-a.ap[-1][0], n]])

            yalls = [A.alloc(f"yall{b}", [128, 16, CH], F32) for b in range(NB)]
            m_y = A.mark()
            uT = A.alloc("uT", [128, NB, 2, S], BF16)
            for b in range(NB):
                dma(uT.ap[:, b, :, :], ucT[b].rearrange("(hh p) t -> p hh t", p=128), reads=[dres[("ucT", b)]], writes=[uT.res], wgroup="uTl")
            ytab = A.alloc("ytab", [128, 1024], F32); ftab = A.alloc("ftab", [128, 1024], F32)
            cosTs = [A.alloc(f"cosT{d_}", [128, 16, 128], F32) for d_ in range(2)]
            sinTs = [A.alloc(f"sinT{d_}", [128, 16, 128], F32) for d_ in range(2)]
            z_r = A.ring("z", [128, S], F32, 2)
            t1_r = A.ring("t1", [128, 512], F32, 2); t2_r = A.ring("t2", [128, 512], F32, 2)
            P12 = [[A.alloc(f"P{i}{d_}", [128, S], BF16) for i in range(2)] for d_ in range(2)]
            ycnt = 0
            for g in range(G):
                half = g // 8
                for d_ in range(2):
                    gd = d_ * 16 + g
                    cosT, sinT = cosTs[d_], sinTs[d_]
                    cosf = cosT.ap.rearrange("p a b -> p (a b)"); sinf = sinT.ap.rearrange("p a b -> p (a b)")
                    for hb_ in range(2):
                        c0_ = 1024 * hb_
                        if hb_ == 0:
                            ACT(lambda e, gd=gd: e.activation(ytab.ap, iota.ap, AF.Identity, bias=0.0, scale=th.ap[:, gd:gd + 1]), r=[iota.res, th.res], w=[ytab.res])
                        else:
                            ACT(lambda e, gd=gd: e.activation(ytab.ap, iota.ap, AF.Identity, bias=thb.ap[:, gd:gd + 1], scale=th.ap[:, gd:gd + 1]), r=[iota.res, th.res, thb.res], w=[ytab.res])
                        V(lambda e: e.tensor_scalar(ftab.ap, ytab.ap, MAGIC, MAGIC, ALU.add, ALU.subtract), r=[ytab.res], w=[ftab.res])
                        V(lambda e: e.tensor_sub(ftab.ap, ytab.ap, ftab.ap), r=[ytab.res, ftab.res], w=[ftab.res])
                        ACT(lambda e, sinf=sinf, c0_=c0_: e.activation(sinf[:, c0_:c0_ + 1024], ftab.ap, AF.Sin, bias=0.0, scale=TWO_PI_S), r=[ftab.res], w=[sinT.res], g=("sinT", l, gd))
                        ACT(lambda e: e.activation(ytab.ap, ftab.ap, AF.Abs), r=[ftab.res], w=[ytab.res])
                        ACT(lambda e, cosf=cosf, c0_=c0_: e.activation(cosf[:, c0_:c0_ + 1024], ytab.ap, AF.Sin, bias=HALF_PI_S, scale=-TWO_PI_S), r=[ytab.res], w=[cosT.res], g=("cosT", l, gd))
                units = [(b, d_) for b in range(NB) for d_ in range(2)]

                def stage_A(b, d_, g=g, half=half):
                    gd = d_ * 16 + g
                    cosT, sinT = cosTs[d_], sinTs[d_]
                    cosTf = cosT.ap.rearrange("p a b -> p (a b)"); sinTf = sinT.ap.rearrange("p a b -> p (a b)")
                    z = z_r.next()
                    for c in range(4):
                        cn_ = c if d_ == 0 else 3 - c
                        PE(lambda e, gd=gd, cn_=cn_, b=b: e.matmul(bankF(0), LB.ap[:, gd, :], uT.ap[:, b, half, 512 * cn_:512 * cn_ + 512], start=True, stop=True),
                           r=[LB.res, uT.res], w=[psres[0]])
                        PE(lambda e, gd=gd, cn_=cn_, b=b: e.matmul(bankF(1), LBs.ap[:, gd, :], uT.ap[:, b, half, 512 * cn_:512 * cn_ + 512], start=True, stop=True),
                           r=[LBs.res, uT.res], w=[psres[1]])
                        t1 = t1_r.next(); t2 = t2_r.next()
                        v0 = bankF(0) if d_ == 0 else rev(bankF(0), 0, 512)
                        v1 = bankF(1) if d_ == 0 else rev(bankF(1), 0, 512)
                        V(lambda e, t1=t1, v0=v0, c=c, cosTf=cosTf: e.tensor_tensor(t1.ap, v0, cosTf[:, 512 * c:512 * c + 512], ALU.mult), r=[psres[0], cosT.res], w=[t1.res])
                        V(lambda e, t2=t2, v1=v1, c=c, sinTf=sinTf: e.tensor_tensor(t2.ap, v1, sinTf[:, 512 * c:512 * c + 512], ALU.mult), r=[psres[1], sinT.res], w=[t2.res])
                        GP(lambda e, t1=t1, t2=t2, c=c, z=z: e.tensor_tensor(z.ap[:, 512 * c:512 * c + 512], t1.ap, t2.ap, ALU.add), r=[t1.res, t2.res], w=[z.res], g=("z", l, g, b, d_))
                    return z

                def stage_B(b, d_, z, g=g, half=half):
                    gd = d_ * 16 + g
                    cosT, sinT = cosTs[d_], sinTs[d_]
                    cosTf = cosT.ap.rearrange("p a b -> p (a b)"); sinTf = sinT.ap.rearrange("p a b -> p (a b)")
                    V(lambda e, gd=gd, z=z: e.tensor_tensor_scan(z.ap, rr.ap[:, gd:gd + 1].to_broadcast([128, S]), z.ap, 0.0, ALU.mult, ALU.add), r=[z.res, rr.res], w=[z.res])
                    p1, p2 = P12[d_]
                    o1 = p1.ap if d_ == 0 else rev(p1.ap, 0, S)
                    o2 = p2.ap if d_ == 0 else rev(p2.ap, 0, S)
                    V(lambda e, o1=o1, cosTf=cosTf, z=z: e.tensor_tensor(o1, z.ap, cosTf, ALU.mult), r=[z.res, cosT.res], w=[p1.res])
                    GP(lambda e, o2=o2, sinTf=sinTf, z=z: e.tensor_tensor(o2, z.ap, sinTf, ALU.mult), r=[z.res, sinT.res], w=[p2.res])
                    if b == 0 and g == 0:
                        dbg(f"P1_{l}_{d_}", p1.ap, [p1.res])

                def y_mm(b, g=g, half=half):
                    nonlocal ycnt
                    yall = yalls[b]
                    yb_ = 2 + ycnt % 2
                    ycnt += 1
                    for tl in range(16):
                        for d_ in range(2):
                            blk = d_ * 2 + half
                            for i in range(2):
                                Wt = (W1T, W2T)[i]
                                pp = P12[d_][i]
                                PE(lambda e, tl=tl, i=i, d_=d_, Wt=Wt, blk=blk, yb_=yb_, pp=pp, g=g: e.matmul(bankF(yb_)[:, 16 * tl:16 * tl + 16], pp.ap[:, 128 * tl:128 * tl + 128],
                                                                                       Wt.ap[:, blk, 16 * (g % 8):16 * (g % 8) + 16], start=(d_ == 0 and i == 0), stop=(d_ == 1 and i == 1)),
                                   r=[pp.res, Wt.res], w=[psres[yb_]], g=("ymm", l, g, b, tl))
                    ACT(lambda e, yb_=yb_, yall=yall, g=g: e.copy(yall.ap[:, :, 16 * g:16 * g + 16], bankF(yb_)[:, 0:256].rearrange("p (a b) -> p a b", a=16)), r=[psres[yb_]], w=[yall.res], g=("yall", l, b))

                zs_ = {}
                zs_[0] = stage_A(*units[0])
                for ui, (b, d_) in enumerate(units):
                    if ui + 1 < len(units):
                        zs_[ui + 1] = stage_A(*units[ui + 1])
                    stage_B(b, d_, zs_[ui])
                    if d_ == 1:
                        y_mm(b)
            P.barrier()
            for b in range(NB):
                A.reset(m_y)
                s5_epilogue(b, yalls[b])
                P.barrier()
            A.reset(m)
            P.barrier()

        def s5_epilogue(b, yall):
            m2 = A.mark()
            wglu_sb = A.alloc("wglu_sb", [128, 2, 2 * CH], BF16)
            dma(wglu_sb.ap, wglu_b.rearrange("(k p) n -> p k n", p=128), reads=[dres["wglu_b"]], writes=[wglu_sb.res])
            uct = A.alloc("uct", [128, 16, CH], F32)
            dma(uct.ap, uctok[b].rearrange("(t p) c -> p t c", p=128), reads=[dres[("uctok", b)]], writes=[uct.res])
            yw = A.alloc("yw", [128, 16, CH], F32)
            ybf = A.alloc("ybf", [128, 16, CH], BF16)
            V(lambda e: e.tensor_tensor(uct.ap, uct.ap, dB.ap.unsqueeze(1).to_broadcast([128, 16, CH]), ALU.mult), r=[uct.res, dB.res], w=[uct.res])
            V(lambda e: e.tensor_tensor(yw.ap, yall.ap, uct.ap, ALU.add), r=[yall.res, uct.res], w=[yw.res])
            if b == 0:
                dbg(f"ypre{l}", yw.ap[:, 0, :], [yw.res])
            GP(lambda e: e.tensor_tensor(uct.ap, yw.ap, yw.ap, ALU.mult), r=[yw.res], w=[uct.res])
            V(lambda e: e.tensor_scalar(uct.ap, uct.ap, 0.044715, 1.0, ALU.mult, ALU.add), r=[uct.res], w=[uct.res])
            V(lambda e: e.tensor_tensor(uct.ap, uct.ap, yw.ap, ALU.mult), r=[uct.res, yw.res], w=[uct.res])
            ACT(lambda e: e.activation(uct.ap.rearrange("p a b -> p (a b)"), uct.ap.rearrange("p a b -> p (a b)"), AF.Sigmoid, bias=0.0, scale=1.5957691216057308), r=[uct.res], w=[uct.res])
            V(lambda e: e.tensor_tensor(ybf.ap, uct.ap, yw.ap, ALU.mult), r=[uct.res, yw.res], w=[ybf.res])
            yT_r = A.ring("yTs", [128, 2, 128], BF16, 2)
            glu_r = A.ring("glu", [128, 2 * CH], F32, 2); sg_r = A.ring("sg", [128, CH], F32, 2); oc_r = A.ring("oc", [128, CH], F32, 2)
            junk3 = A.alloc("junk3", [128, CH], BF16)
            ocb_r = A.ring("ocb", [128, CH], BF16, 2)
            def tile_gen(tl):
                glu = glu_r.next(); sg = sg_r.next(); oc = oc_r.next()
                for hh in range(2):
                    PE(lambda e, tl=tl, hh=hh: e.transpose(psB[:, 128 * hh:128 * hh + 128], ybf.ap[:, tl, 128 * hh:128 * hh + 128], ident.ap), r=[ybf.res, ident.res], w=[psres[6]], g=("yTt", l, b, tl))
                yT = yT_r.next()
                ACT(lambda e, yT=yT: e.copy(yT.ap.rearrange("p a b -> p (a b)"), psB[:, 0:256]), r=[psres[6]], w=[yT.res])
                yield
                for hh in range(2):
                    PE(lambda e, yT=yT, hh=hh: e.matmul(bankF(4), yT.ap[:, hh, :], wglu_sb.ap[:, hh, :], start=(hh == 0), stop=(hh == 1)), r=[yT.res, wglu_sb.res], w=[psres[4]], g=("glumm", l, b, tl))
                V(lambda e: e.tensor_tensor(glu.ap, bankF(4), bgluB.ap, ALU.add), r=[psres[4], bgluB.res], w=[glu.res])
                yield
                ACT(lambda e: e.activation(sg.ap, glu.ap[:, CH:2 * CH], AF.Sigmoid), r=[glu.res], w=[sg.res])
                V(lambda e: e.tensor_tensor(oc.ap, glu.ap[:, 0:CH], sg.ap, ALU.mult), r=[glu.res, sg.res], w=[oc.res])
                yield
                ACT(lambda e: e.activation(junk3.ap, oc.ap, AF.Square, accum_out=stat.ap[:, 0:1]), r=[oc.res], w=[junk3.res, statres[0]])
                ACT(lambda e: e.activation(stat.ap[:, 4:5], stat.ap[:, 0:1], AF.Sqrt, bias=EPS, scale=1.0 / CH), r=[statres[0]], w=[statres[1]])
                V(lambda e: e.reciprocal(stat.ap[:, 4:5], stat.ap[:, 4:5]), r=[statres[1]], w=[statres[1]])
                yield
                ocb = ocb_r.next()
                V(lambda e, ocb=ocb: e.scalar_tensor_tensor(ocb.ap, oc.ap, stat.ap[:, 4:5], mixB.ap[:, 768:1024], ALU.mult, ALU.mult), r=[oc.res, statres[1], mixB.res], w=[ocb.res])
                dma(mixedD[b, 128 * tl:128 * tl + 128, 768:1024], ocb.ap, reads=[ocb.res], writes=[dres[("mixedD", b)]], wgroup="mixw")
            active = []
            nxt = 0
            while active or nxt < 16:
                if len(active) < 2 and nxt < 16:
                    active.append(tile_gen(nxt))
                    nxt += 1
                for g_ in list(active):
                    try:
                        next(g_)
                    except StopIteration:
                        active.remove(g_)
            A.reset(m2)

        def phase_out(b):
            m = A.mark()
            g1, G2, SH2 = modT[0], modT[1], modT[2]
            load_mod(g1, b, 2)
            load_mod(G2, b, 4, norm2B)
            load_mod(SH2, b, 3)
            wout_sb = A.alloc("wout_sb", [128, 8, D], BF16)
            dma(wout_sb.ap, wout_b.rearrange("(k p) n -> p k n", p=128), reads=[dres["wout_b"]], writes=[wout_sb.res])
            mx_r = A.ring("mxin", [128, D], BF16, 2)
            mT_r = A.ring("mT", [128, 8, 128], BF16, 2)
            xt_r = A.ring("xt6", [128, D], F32, 2)
            tmp_r = A.ring("tmp6", [128, D], F32, 2)
            xn_r = A.ring("xn", [128, D], F32, 2)
            junk = A.alloc("junk6", [128, D], BF16)
            hf_r = A.ring("hf6", [128, D], F32, 2)
            hb_r = A.ring("hb6", [128, D], BF16, 2)
            h2T_r = A.ring("h2Ts", [128, 8, 128], BF16, 2)
            def tile_gen(tl):
                tt = b * (S // 128) + tl
                t0 = tl * 128
                tmp = tmp_r.next(); hf = hf_r.next(); hb = hb_r.next()
                mx = mx_r.next(); xt = xt_r.next()
                dma(mx.ap, mixedD[b, t0:t0 + 128, :], reads=[dres[("mixedD", b)]], writes=[mx.res])
                dma(xt.ap, xin_ap[tt * 128:(tt + 1) * 128, :], reads=[xres_tt[tt]], writes=[xt.res])
                if tl == 0 and b == 0:
                    dbg(f"mixed{l}", mx.ap, [mx.res])
                for k in range(8):
                    PE(lambda e, k=k, mx=mx: e.transpose(psB[:, 128 * k:128 * (k + 1)], mx.ap[:, 128 * k:128 * (k + 1)], ident.ap), r=[mx.res, ident.res], w=[psres[6]], g=("mTt", l, tt))
                yield
                mT = mT_r.next()
                ACT(lambda e, mT=mT: e.copy(mT.ap.rearrange("p a b -> p (a b)"), psB[:, 0:1024]), r=[psres[6]], w=[mT.res])
                yield
                for cch in range(2):
                    for k in range(8):
                        PE(lambda e, k=k, cch=cch, mT=mT: e.matmul(bankF(cch), mT.ap[:, k, :], wout_sb.ap[:, k, 512 * cch:512 * cch + 512], start=(k == 0), stop=(k == 7)),
                           r=[mT.res, wout_sb.res], w=[psres[cch]], g=("omm", l, tt, cch))
                yield
                V(lambda e: e.tensor_tensor(tmp.ap, psF[:, 0:1024], g1.ap, ALU.mult), r=[psres[0], psres[1], g1.res], w=[tmp.res])
                xn = xn_r.next()
                V(lambda e, xn=xn, xt=xt: e.tensor_tensor(xn.ap, tmp.ap, xt.ap, ALU.add), r=[tmp.res, xt.res], w=[xn.res])
                dma(xmid[tt * 128:(tt + 1) * 128, :], xn.ap, reads=[xn.res], writes=[xmid_tt[tt]])
                yield
                ACT(lambda e, xn=xn: e.activation(junk.ap, xn.ap, AF.Square, accum_out=stat.ap[:, 0:1]), r=[xn.res], w=[junk.res, statres[0]])
                ACT(lambda e: e.activation(stat.ap[:, 4:5], stat.ap[:, 0:1], AF.Sqrt, bias=EPS, scale=1.0 / D), r=[statres[0]], w=[statres[1]])
                V(lambda e: e.reciprocal(stat.ap[:, 8:9], stat.ap[:, 4:5]), r=[statres[1]], w=[statres[2]])
                V(lambda e, xn=xn: e.scalar_tensor_tensor(hf.ap, xn.ap, stat.ap[:, 8:9], G2.ap, ALU.mult, ALU.mult), r=[xn.res, statres[2], G2.res], w=[hf.res])
                V(lambda e: e.tensor_tensor(hb.ap, hf.ap, SH2.ap, ALU.add), r=[hf.res, SH2.res], w=[hb.res])
                yield
                for k in range(8):
                    PE(lambda e, k=k: e.transpose(psB[:, 1024 + 128 * k:1024 + 128 * (k + 1)], hb.ap[:, 128 * k:128 * (k + 1)], ident.ap), r=[hb.res, ident.res], w=[psres[7]], g=("h2Tt", l, tt))
                yield
                h2T = h2T_r.next()
                ACT(lambda e, h2T=h2T: e.copy(h2T.ap.rearrange("p a b -> p (a b)"), psB[:, 1024:2048]), r=[psres[7]], w=[h2T.res])
                dma(h2TD[b, :, t0:t0 + 128].rearrange("(k p) t -> p k t", p=128), h2T.ap, reads=[h2T.res], writes=[dres[("h2TD", b)]], wgroup="h2w")
            active = []
            nxt = 0
            ntile = S // 128
            while active or nxt < ntile:
                if len(active) < 2 and nxt < ntile:
                    active.append(tile_gen(nxt))
                    nxt += 1
                for g_ in list(active):
                    try:
                        next(g_)
                    except StopIteration:
                        active.remove(g_)
            A.reset(m)
            P.barrier()

        def phase_ffn(b):
            m = A.mark()
            g2 = modT[0]
            load_mod(g2, b, 5)
            wdown_sb = A.alloc("wdown_sb", [128, 22, D], BF16)
            dma(wdown_sb.ap, wdown_b.rearrange("(f p) n -> p f n", p=128), reads=[dres["wdown_b"]], writes=[wdown_sb.res])
            hw_r = A.ring("hw", [128, 8, 514], BF16, 2)
            wup_r = A.ring("wup", [128, 2, 8, 128], BF16, 3)
            tv_r = A.ring("tv", [128, 512], F32, 2); tg_r = A.ring("tg", [128, 512], F32, 3)
            aT_r = A.ring("aT", [128, 22, 512], BF16, 2)
            xt_r = A.ring("xt7", [128, D], F32, 2)
            tmp_r = A.ring("tmp7", [128, D], F32, 2)
            xo_r = A.ring("xo", [128, D], F32, 2)
            zsel = 0
            pend = None

            def finish_pair(tv, tg, f, w_, aT):
                ACT(lambda e, tg=tg: e.activation(tg.ap, tg.ap, AF.Silu), r=[tg.res], w=[tg.res])
                GP(lambda e, tv=tv, tg=tg, f=f, aT=aT: e.tensor_tensor(aT.ap[:, f, :], tv.ap, tg.ap, ALU.mult), r=[tv.res, tg.res], w=[aT.res], g=("aT", l, b, w_))

            def down(w_, aT):
                for j in range(4):
                    tt = b * 16 + w_ * 4 + j
                    xt = xt_r.next()
                    tmp = tmp_r.next()
                    dma(xt.ap, xmid[tt * 128:(tt + 1) * 128, :], reads=[xmid_tt[tt]], writes=[xt.res])
                    for cch in range(2):
                        for f in range(22):
                            PE(lambda e, f=f, cch=cch, j=j, aT=aT: e.matmul(bankF(cch), aT.ap[:, f, 128 * j:128 * j + 128], wdown_sb.ap[:, f, 512 * cch:512 * cch + 512], start=(f == 0), stop=(f == 21)),
                               r=[aT.res, wdown_sb.res], w=[psres[cch]], g=("dmm", l, tt, cch))
                    V(lambda e, tmp=tmp: e.tensor_tensor(tmp.ap, psF[:, 0:1024], g2.ap, ALU.mult), r=[psres[0], psres[1], g2.res], w=[tmp.res])
                    xo = xo_r.next()
                    V(lambda e, xo=xo, xt=xt, tmp=tmp: e.tensor_tensor(xo.ap, tmp.ap, xt.ap, ALU.add), r=[tmp.res, xt.res], w=[xo.res])
                    dma(out[tt * 128:(tt + 1) * 128, :], xo.ap, reads=[xo.res], writes=[xres_tt[tt]])

            pending_down = None

            for w_ in range(4):
                c0 = 512 * w_
                aT = aT_r.next()
                hw = hw_r.next()
                lo = max(c0 - 1, 0); hi = min(c0 + 513, S)
                j0 = lo - (c0 - 1)
                if j0 > 0:
                    GP(lambda e, hw=hw: e.memset(hw.ap[:, :, 0:1], 0.0), w=[hw.res], g=("hwl", l, b, w_))
                if hi < c0 + 513:
                    GP(lambda e, hw=hw: e.memset(hw.ap[:, :, 513:514], 0.0), w=[hw.res], g=("hwl", l, b, w_))
                dma(hw.ap[:, :, j0:j0 + (hi - lo)], h2TD[b, :, lo:hi].rearrange("(k p) t -> p k t", p=128), reads=[dres[("h2TD", b)]], writes=[hw.res], wgroup=("hwl", l, b, w_))
                for f in range(22):
                    wu = wup_r.next()
                    dma(wu.ap[:, 0, :, :].rearrange("p k n -> p (k n)"), wup_b[f], reads=[dres["wup_b"]], writes=[wu.res], wgroup=("wul", l, b, w_, f))
                    dma(wu.ap[:, 1, :, :].rearrange("p k n -> p (k n)"), wup_b[22 + f], reads=[dres["wup_b"]], writes=[wu.res], wgroup=("wul", l, b, w_, f))
                    zs = zsel % 2
                    zsel += 1
                    zb = [4 * zs, 4 * zs + 2]
                    zaps = []
                    for vg in range(2):
                        b0 = zb[vg]
                        zap = psF[:, 512 * b0:512 * b0 + 1024] if b0 < 6 else psB_f
                        zaps.append(zap)
                        for k in range(8):
                            PE(lambda e, k=k, vg=vg, zap=zap, wu=wu, hw=hw: e.matmul(zap[:, 0:512], wu.ap[:, vg, k, :], hw.ap[:, k, 0:512], start=(k == 0), stop=(k == 7)),
                               r=[wu.res, hw.res], w=[psres[b0]], g=("zmm", l, b, w_, f, vg, 0))
                        for k in range(8):
                            PE(lambda e, k=k, vg=vg, zap=zap, wu=wu, hw=hw: e.matmul(zap[:, 512:514], wu.ap[:, vg, k, :], hw.ap[:, k, 512:514], start=(k == 0), stop=(k == 7)),
                               r=[wu.res, hw.res], w=[psres[b0 + 1]], g=("zmm", l, b, w_, f, vg, 1))
                    tv = tv_r.next(); tg = tg_r.next()
                    for vg, tt_ in ((0, tv), (1, tg)):
                        fi = f + 22 * vg
                        zap = zaps[vg]
                        rs = [psres[zb[vg]], psres[zb[vg] + 1]]
                        ACT(lambda e, zap=zap, tt_=tt_, fi=fi: e.activation(tt_.ap, zap[:, 1:513], AF.Identity, bias=cb.ap[:, fi:fi + 1], scale=cw.ap[:, 1, fi:fi + 1]),
                            r=rs + [cw.res, cb.res], w=[tt_.res])
                        V(lambda e, zap=zap, tt_=tt_, fi=fi: e.scalar_tensor_tensor(tt_.ap, zap[:, 0:512], cw.ap[:, 0, fi:fi + 1], tt_.ap, ALU.mult, ALU.add), r=rs + [cw.res, tt_.res], w=[tt_.res])
                        V(lambda e, zap=zap, tt_=tt_, fi=fi: e.scalar_tensor_tensor(tt_.ap, zap[:, 2:514], cw.ap[:, 2, fi:fi + 1], tt_.ap, ALU.mult, ALU.add), r=rs + [cw.res, tt_.res], w=[tt_.res])
                    if w_ == 0 and b == 0 and f == 0:
                        dbg(f"zc{l}", tv.ap, [tv.res])
                    if pend is not None:
                        finish_pair(*pend)
                    pend = (tv, tg, f, w_, aT)
                    if f == 2 and pending_down is not None:
                        down(*pending_down)
                        pending_down = None
                if pend is not None:
                    finish_pair(*pend)
                    pend = None
                pending_down = (w_, aT)
            if pending_down is not None:
                down(*pending_down)
                pending_down = None
            A.reset(m)
            P.barrier()

        for b in range(NB):
            phase_proj(b)
            if stop_after == "proj":
                break
            phase_attn(b, "mla")
            phase_attn(b, "dil")
        if stop_after == "proj":
            break
        phase_s5()
        for b in range(NB):
            phase_out(b)
        for b in range(NB):
            phase_ffn(b)
        A.reset(m_layer)

    P.emit(st)
    st.close()
    return nc, P, A


def host_constants():
    ident = np.eye(128, dtype=np.float32).astype(ml_dtypes.bfloat16)
    k = np.arange(128)[:, None]
    xx = np.arange(MASKW)[None, :]
    d = k - xx + 1920
    w = (np.abs(d) <= 64).astype(np.float32) + ((d % 4 == 0) & (np.abs(d) <= 256)) + ((d % 16 == 0) & (np.abs(d) <= 1024))
    mask = w.astype(ml_dtypes.bfloat16)
    misc = np.zeros((128, 256), np.float32)
    misc[:, 0:16] = 128.0 * np.arange(16)[None, :]
    misc[:, 16:144] = np.arange(128)[None, :]
    for j in range(8):
        misc[16 * j:16 * j + 16, 144 + j] = 1.0
    misc[:, 160:176] = (1.0 / (10000.0 ** (np.arange(0, 32, 2, dtype=np.float32) / 32.0))).astype(np.float32)[None, :]
    misc[:, 192:224] = (1.0 / (10000.0 ** (np.arange(0, 64, 2, dtype=np.float32) / 64.0))).astype(np.float32)[None, :]
    iota = np.tile(np.arange(1024, dtype=np.float32)[None, :], (128, 1))
    return ident, mask, misc, iota


_CACHE = {}


def kernel(**inputs):
    n = 8
    if "nc" not in _CACHE:
        _CACHE["nc"] = build_program()[0]
    nc = _CACHE["nc"]
    ident, mask, misc, iota = host_constants()
    x = np.ascontiguousarray(np.asarray(inputs["x"], dtype=np.float32))
    c = np.asarray(inputs["c"], dtype=np.float32)
    pos = np.asarray(inputs["positions"], dtype=np.int32)
    shared = {k: np.ascontiguousarray(np.asarray(v)) for k, v in inputs.items() if k not in ("x", "c", "positions")}
    shared.update({"k_ident": ident, "k_mask": mask, "k_misc": misc, "k_iota": iota})
    in_maps = []
    for i in range(n):
        mp = dict(shared)
        mp["x"] = x[NB * i:NB * (i + 1)].reshape(T, D)
        mp["c"] = np.ascontiguousarray(c[NB * i:NB * (i + 1)])
        mp["positions"] = np.ascontiguousarray(pos[NB * i:NB * (i + 1)])
        in_maps.append(mp)
    res = run_bass_kernel_spmd(nc, in_maps, core_ids=list(range(n)))
    outs = [np.asarray(r["out"]).reshape(NB, S, D) for r in res.results]
    return np.concatenate(outs, axis=0).astype(np.float32)
```

```python
import math
from contextlib import ExitStack

import ml_dtypes
import numpy as np
import concourse.bass as bass
import concourse.mybir as mybir
from concourse.bass_utils import run_bass_kernel_spmd

F32 = mybir.dt.float32
BF16 = mybir.dt.bfloat16
I32 = mybir.dt.int32
AF = mybir.ActivationFunctionType
ALU = mybir.AluOpType
AX = mybir.AxisListType

ENGS = ("pe", "act", "dve", "pool", "sp")
EPOCH = 20000
NDMASEM = 14


class Res:
    __slots__ = ("name", "writers", "readers", "wgroup")

    def __init__(self, name):
        self.name = name
        self.writers = []
        self.readers = []
        self.wgroup = None


class Op:
    __slots__ = ("idx", "eng", "fn", "dma", "deps", "lidx", "waits", "signal", "needed", "slot", "prev_slot_op")

    def __init__(self):
        self.waits = []
        self.signal = None
        self.needed = False


class Prog:
    def __init__(self, nc):
        self.nc = nc
        self.ops = []
        self.eng_ops = {e: [] for e in ENGS}
        self.dma_rr = {e: 0 for e in ENGS}
        self.dma_slot_last = {}
        self.nres = 0
        self.cur_barrier = set()

    def res(self, name=None):
        self.nres += 1
        return Res(name or f"r{self.nres}")

    def barrier(self):
        tails = set()
        for e in ENGS:
            if self.eng_ops[e]:
                tails.add(self.eng_ops[e][-1].idx)
        for o in self.dma_slot_last.values():
            tails.add(o.idx)
        self.cur_barrier = tails

    def op(self, eng, fn, reads=(), writes=(), dma=False, wgroup=None):
        o = Op()
        o.idx = len(self.ops)
        o.eng = eng
        o.fn = fn
        o.dma = dma
        deps = set(self.cur_barrier)
        for r in reads:
            deps.update(r.writers)
        for w in writes:
            if not (wgroup is not None and w.wgroup == wgroup):
                deps.update(w.writers)
            deps.update(w.readers)
        o.deps = deps
        o.lidx = len(self.eng_ops[eng])
        o.slot = None
        o.prev_slot_op = None
        if dma:
            s = self.dma_rr[eng]
            self.dma_rr[eng] = (s + 1) % NDMASEM
            o.slot = s
            o.prev_slot_op = self.dma_slot_last.get((eng, s))
            self.dma_slot_last[(eng, s)] = o
        self.ops.append(o)
        self.eng_ops[eng].append(o)
        for r in reads:
            r.readers.append(o.idx)
        for w in writes:
            if wgroup is not None and w.wgroup == wgroup:
                w.writers.append(o.idx)
            else:
                w.writers = [o.idx]
                w.readers = []
                w.wgroup = wgroup
        return o

    def finalize(self):
        ops = self.ops
        tails = set()
        for e in ENGS:
            if self.eng_ops[e]:
                tails.add(self.eng_ops[e][-1].idx)
        for o in self.dma_slot_last.values():
            tails.add(o.idx)
        fin = Op()
        fin.idx = len(ops)
        fin.eng = "sp"
        fin.fn = None
        fin.dma = False
        fin.deps = tails
        fin.lidx = len(self.eng_ops["sp"])
        fin.slot = None
        fin.prev_slot_op = None
        ops.append(fin)
        self.eng_ops["sp"].append(fin)

        seen = {e: {f: -1 for f in ENGS} for e in ENGS}
        seen_dma = {e: set() for e in ENGS}
        for o in ops:
            e = o.eng
            dl = []
            if o.dma and o.prev_slot_op is not None:
                dl.append(o.prev_slot_op)
            for d in sorted(o.deps):
                dl.append(ops[d])
            for d in dl:
                if d.dma:
                    if d.idx in seen_dma[e]:
                        continue
                    seen_dma[e].add(d.idx)
                    d.needed = True
                    o.waits.append(d)
                else:
                    f = d.eng
                    if f == e:
                        if e in ("pe", "sp"):
                            continue
                        if d.lidx < o.lidx - 3:
                            continue
                    if d.lidx <= seen[e][f]:
                        continue
                    seen[e][f] = d.lidx
                    d.needed = True
                    o.waits.append(d)
        cnt = {e: 0 for e in ENGS}
        dma_tot = {}
        for o in ops:
            if o.dma:
                k = (o.eng, o.slot)
                dma_tot[k] = dma_tot.get(k, 0) + 16
                o.signal = ("d", o.eng, o.slot, dma_tot[k])
            elif o.needed:
                c = cnt[o.eng]
                cnt[o.eng] = c + 1
                o.signal = ("c", o.eng, c // EPOCH, c % EPOCH + 1)
        self.n_epochs = {e: max(1, (cnt[e] + EPOCH - 1) // EPOCH) for e in ENGS}

    def emit(self, stack):
        nc = self.nc
        self.finalize()
        sems = {}
        for e in ENGS:
            if e == "sp":
                continue
            for ep in range(self.n_epochs[e]):
                sems[("c", e, ep)] = stack.enter_context(nc.semaphore(f"s_{e}_{ep}"))
        for e in ENGS:
            if any(k[0] == e for k in self.dma_slot_last):
                for s in range(NDMASEM):
                    sems[("d", e, s)] = stack.enter_context(nc.semaphore(f"d_{e}_{s}"))
        block = stack.enter_context(nc.Block())

        def run(engname):
            def body(eng):
                for o in self.eng_ops[engname]:
                    for d in o.waits:
                        sg = d.signal
                        eng.wait_ge(sems[sg[:3]], sg[3])
                    if o.fn is None:
                        continue
                    inst = o.fn(eng)
                    if o.signal is not None:
                        sg = o.signal
                        inst.then_inc(sems[sg[:3]], 16 if sg[0] == "d" else 1)
            return body

        block.tensor(run("pe"))
        block.scalar(run("act"))
        block.vector(run("dve"))
        block.gpsimd(run("pool"))
        block.sync(run("sp"))


D = 1024
S = 2048
NB = 2
T = NB * S
NTT = T // 128
DEPTH = 4
HM, NOPE, ROPE_M, VM, QKM = 6, 64, 32, 64, 96
QR, KVR = 192, 128
HD, DD = 6, 64
G, GC, NST = 16, 16, 64
CH = 256
INW = 1760
FF = 2816
EPS = 1e-6
OFF_CQ, OFF_CKV, OFF_KR, OFF_DQ, OFF_DK, OFF_DV, OFF_UC = 0, 192, 320, 352, 736, 1120, 1504
MASKW = 3968
TWO_PI_S = 6.2831845
INV2PI = 1.0 / (2.0 * math.pi)
HALF_PI_S = 1.5707960
MAGIC = 12582912.0


class Tile:
    __slots__ = ("ap", "res")

    def __init__(self, ap, res):
        self.ap = ap
        self.res = res


class Arena:
    def __init__(self, P, arena_ap, words):
        self.P = P
        self.a = arena_ap
        self.words = words
        self.top = 0
        self.peak = 0

    def mark(self):
        return self.top

    def reset(self, m):
        self.top = m

    def alloc(self, name, shape, dtype):
        n = 1
        for s in shape[1:]:
            n *= s
        nw = n if dtype in (F32, I32) else (n + 1) // 2
        nw = (nw + 7) // 8 * 8
        assert self.top + nw <= self.words, f"SBUF arena overflow at {name}: {self.top}+{nw}>{self.words}"
        ap = self.a[0:shape[0], self.top:self.top + nw]
        self.top += nw
        self.peak = max(self.peak, self.top)
        if dtype != F32:
            ap = ap.bitcast(dtype)
        ap = ap[:, 0:n]
        if len(shape) == 3:
            ap = ap.rearrange("p (a b) -> p a b", a=shape[1])
        elif len(shape) == 4:
            ap = ap.rearrange("p (a b c) -> p a b c", a=shape[1], b=shape[2])
        return Tile(ap, self.P.res(name))

    def ring(self, name, shape, dtype, n):
        return Ring([self.alloc(f"{name}{i}", shape, dtype) for i in range(n)])


class Ring:
    def __init__(self, tiles):
        self.tiles = tiles
        self.i = 0

    def next(self):
        t = self.tiles[self.i % len(self.tiles)]
        self.i += 1
        return t


def build_program(depth=DEPTH, debug=None, stop_after=None):
    debug = debug or {}
    nc = bass.Bass("TRN2", target_bir_lowering=False)
    dt_in = lambda name, shape, dt=F32: nc.dram_tensor(name, list(shape), dt, kind="ExternalInput").ap()
    dt_scr = lambda name, shape, dt: nc.dram_tensor(name, list(shape), dt, kind="Internal").ap()
    L = DEPTH
    x_in = dt_in("x", [T, D])
    c_in = dt_in("c", [NB, D])
    pos_in = dt_in("positions", [NB, S], I32)
    w_mod = dt_in("w_mod", [L, D, 6 * D]); b_mod = dt_in("b_mod", [L, 6 * D])
    norm1 = dt_in("norm1", [L, D]); w_in = dt_in("w_in", [L, D, INW])
    mla_q_norm = dt_in("mla_q_norm", [L, QR]); mla_w_uq = dt_in("mla_w_uq", [L, QR, HM * QKM])
    mla_kv_norm = dt_in("mla_kv_norm", [L, KVR]); mla_w_ukv = dt_in("mla_w_ukv", [L, KVR, HM * 128])
    mla_qk_gain = dt_in("mla_qk_gain", [L, 2, QKM]); dil_qk_gain = dt_in("dil_qk_gain", [L, 2, DD])
    ssm_a_re = dt_in("ssm_a_re", [L, 2, G, NST]); ssm_a_im = dt_in("ssm_a_im", [L, 2, G, NST])
    ssm_log_dt = dt_in("ssm_log_dt", [L, 2, G])
    ssm_b_re = dt_in("ssm_b_re", [L, 2, G, NST, GC]); ssm_b_im = dt_in("ssm_b_im", [L, 2, G, NST, GC])
    ssm_c_re = dt_in("ssm_c_re", [L, 2, G, GC, NST]); ssm_c_im = dt_in("ssm_c_im", [L, 2, G, GC, NST])
    ssm_d = dt_in("ssm_d", [L, CH]); ssm_w_glu = dt_in("ssm_w_glu", [L, CH, 2 * CH]); ssm_b_glu = dt_in("ssm_b_glu", [L, 2 * CH])
    mix_norm = dt_in("mix_norm", [L, D]); w_out = dt_in("w_out", [L, D, D]); norm2 = dt_in("norm2", [L, D])
    ffn_w_up = dt_in("ffn_w_up", [L, D, 2 * FF]); ffn_conv_w = dt_in("ffn_conv_w", [L, 3, 2 * FF])
    ffn_conv_b = dt_in("ffn_conv_b", [L, 2 * FF]); ffn_w_down = dt_in("ffn_w_down", [L, FF, D])
    k_ident = dt_in("k_ident", [128, 128], BF16)
    k_mask = dt_in("k_mask", [128, MASKW], BF16)
    k_misc = dt_in("k_misc", [128, 256])
    k_iota = dt_in("k_iota", [128, 1024])
    out = nc.dram_tensor("out", [T, D], F32, kind="ExternalOutput").ap()

    xmid = dt_scr("xmid", [T, D], F32)
    modD = dt_scr("modD", [NB, 6 * D], F32)
    qmT = dt_scr("qmT", [NB, HM, QKM, S], BF16); kmT = dt_scr("kmT", [NB, HM, QKM, S], BF16)
    vmD = dt_scr("vmD", [NB, S, HM * 65], BF16)
    qdT = dt_scr("qdT", [NB, 3, 128, S], BF16); kdT = dt_scr("kdT", [NB, 3, 128, S], BF16)
    vdD = dt_scr("vdD", [NB, S, HD * 65], BF16)
    ucT = dt_scr("ucT", [NB, CH, S], BF16)
    uctok = dt_scr("uctok", [NB, S, CH], F32)
    mixedD = dt_scr("mixedD", [NB, S, D], BF16)
    h2TD = dt_scr("h2TD", [NB, D, S], BF16)
    win_b = dt_scr("win_b", [D, INW], BF16); wuq_b = dt_scr("wuq_b", [QR, HM * QKM], BF16)
    wukv_b = dt_scr("wukv_b", [KVR, HM * 128], BF16); wglu_b = dt_scr("wglu_b", [CH, 2 * CH], BF16)
    wout_b = dt_scr("wout_b", [D, D], BF16); wup_b = dt_scr("wup_b", [44, 128, 1024], BF16)
    wdown_b = dt_scr("wdown_b", [FF, D], BF16)
    dbg_out = {k: nc.dram_tensor("dbg_" + k, list(shp), dt_, kind="ExternalOutput").ap() for k, (shp, dt_) in debug.items()}

    st = ExitStack()
    AW = 53000
    arena_t = st.enter_context(nc.sbuf_tensor("arena", [128, AW], F32))
    psF = st.enter_context(nc.psum_tensor("psF", [128, 3072], F32))
    psBt = st.enter_context(nc.psum_tensor("psB", [128, 2048], BF16))
    P = Prog(nc)
    A = Arena(P, arena_t, AW)
    psres = [P.res(f"psum{i}") for i in range(8)]
    psB = psBt[:, :]
    psB_f = psBt[:, :].bitcast(F32)

    def bankF(i):
        if i < 6:
            return psF[:, 512 * i:512 * (i + 1)]
        return psB_f[:, 512 * (i - 6):512 * (i - 5)]

    dres = {k: P.res("D_" + k) for k in ["xin", "xmid", "out", "modD", "qmT", "kmT", "vmD", "qdT", "kdT", "vdD", "ucT",
                                          "uctok", "mixedD", "h2TD", "win_b", "wuq_b", "wukv_b", "wglu_b", "wout_b",
                                          "wup_b", "wdown_b", "dbg"]}
    for k in ["qmT", "kmT", "vmD", "qdT", "kdT", "vdD", "ucT", "uctok", "mixedD", "h2TD"]:
        for b in range(NB):
            dres[(k, b)] = P.res(f"D_{k}_{b}")
    xres_tt = [P.res(f"D_x_{i}") for i in range(NTT)]
    xmid_tt = [P.res(f"D_xm_{i}") for i in range(NTT)]

    def dma(out_ap, in_ap, reads=(), writes=(), wgroup=None, eng="sp", ncont=False):
        if ncont:
            f = lambda e: e.dma_start(out=out_ap, in_=in_ap, allow_slow_non_contiguous=True)
        else:
            f = lambda e: e.dma_start(out=out_ap, in_=in_ap)
        return P.op(eng, f, reads=reads, writes=writes, dma=True, wgroup=wgroup)

    def dbg(name, ap, res_list):
        if name in dbg_out:
            dma(dbg_out[name], ap, reads=res_list, writes=[dres["dbg"]], wgroup="dbg")

    V = lambda fn, r=(), w=(), g=None: P.op("dve", fn, reads=r, writes=w, wgroup=g)
    ACT = lambda fn, r=(), w=(), g=None: P.op("act", fn, reads=r, writes=w, wgroup=g)
    GP = lambda fn, r=(), w=(), g=None: P.op("pool", fn, reads=r, writes=w, wgroup=g)
    PE = lambda fn, r=(), w=(), g=None: P.op("pe", fn, reads=r, writes=w, wgroup=g)

    ident = A.alloc("ident", [128, 128], BF16)
    maskS = A.alloc("maskS", [128, MASKW], BF16)
    misc = A.alloc("misc", [128, 256], F32)
    dma(ident.ap, k_ident, writes=[ident.res])
    dma(maskS.ap, k_mask, writes=[maskS.res])
    dma(misc.ap, k_misc, writes=[misc.res])
    iota = A.alloc("iota", [128, 1024], F32)
    dma(iota.ap, k_iota, writes=[iota.res])
    cosM = A.alloc("cosM", [128, NTT, 16], F32); sinM = A.alloc("sinM", [128, NTT, 16], F32)
    cosD = A.alloc("cosD", [128, NTT, 32], F32); sinD = A.alloc("sinD", [128, NTT, 32], F32)
    cT = A.alloc("cT", [128, NB, 8], F32)

    def rsqrt_to(dst_tile, src_ap, scale, n_reads, tmp_tile):
        ACT(lambda e: e.activation(tmp_tile.ap, src_ap, AF.Sqrt, bias=EPS, scale=scale), r=n_reads, w=[tmp_tile.res])
        V(lambda e: e.reciprocal(dst_tile.ap, tmp_tile.ap), r=[tmp_tile.res], w=[dst_tile.res])

    def sincos_scratch(n):
        return (A.alloc("yi", [128, n], I32), A.alloc("yf", [128, n], F32), A.alloc("yc", [128, n], F32))

    def sincos(dst_sin, dst_cos, y_tile, n, scr):
        yi = Tile(scr[0].ap[:, 0:n], scr[0].res); yf = Tile(scr[1].ap[:, 0:n], scr[1].res); yc = Tile(scr[2].ap[:, 0:n], scr[2].res)
        yflat = y_tile.ap
        for dst, off in ((dst_sin, 0.0), (dst_cos, 0.25)):
            if off != 0.0:
                V(lambda e, off=off: e.tensor_scalar(yc.ap, yflat, off, None, ALU.add), r=[y_tile.res], w=[yc.res])
                src = yc
            else:
                src = y_tile
            V(lambda e, src=src: e.tensor_scalar(yf.ap, src.ap, MAGIC, MAGIC, ALU.add, ALU.subtract), r=[src.res], w=[yf.res])
            V(lambda e, src=src: e.tensor_sub(yf.ap, src.ap, yf.ap), r=[src.res, yf.res], w=[yf.res])
            ACT(lambda e, dst=dst: e.activation(dst.ap, yf.ap, AF.Sin, bias=0.0, scale=TWO_PI_S), r=[yf.res], w=[dst.res])

    m_init = A.mark()
    posi = A.alloc("posi", [128, NTT], I32); posf = A.alloc("posf", [128, NTT], F32)
    dma(posi.ap, pos_in.rearrange("b (t p) -> p (b t)", p=128), writes=[posi.res], ncont=True)
    V(lambda e: e.tensor_copy(posf.ap, posi.ap), r=[posi.res], w=[posf.res])
    angM = A.alloc("angM", [128, NTT * 16], F32); angD = A.alloc("angD", [128, NTT * 32], F32)
    for j in range(NTT):
        V(lambda e, j=j: e.tensor_scalar(angM.ap[:, 16 * j:16 * j + 16], misc.ap[:, 160:176], posf.ap[:, j:j + 1], INV2PI, ALU.mult, ALU.mult),
          r=[misc.res, posf.res], w=[angM.res], g="angM")
        V(lambda e, j=j: e.tensor_scalar(angD.ap[:, 32 * j:32 * j + 32], misc.ap[:, 192:224], posf.ap[:, j:j + 1], INV2PI, ALU.mult, ALU.mult),
          r=[misc.res, posf.res], w=[angD.res], g="angD")
    sM = Tile(sinM.ap.rearrange("p a b -> p (a b)"), sinM.res); cM = Tile(cosM.ap.rearrange("p a b -> p (a b)"), cosM.res)
    sDt = Tile(sinD.ap.rearrange("p a b -> p (a b)"), sinD.res); cDt = Tile(cosD.ap.rearrange("p a b -> p (a b)"), cosD.res)
    scr0 = sincos_scratch(NTT * 32)
    sincos(sM, cM, angM, NTT * 16, scr0)
    sincos(sDt, cDt, angD, NTT * 32, scr0)
    dbg("posf", posf.ap, [posf.res]); dbg("angM", angM.ap, [angM.res]); dbg("sinM", sM.ap, [sinM.res]); dbg("cosM", cM.ap, [cosM.res])
    craw = A.alloc("craw", [128, NB, 8], F32)
    for b_ in range(NB):
        dma(craw.ap[:, b_, :], c_in[b_].rearrange("(k p) -> p k", p=128), writes=[craw.res], ncont=True, wgroup="craw")
    ACT(lambda e: e.activation(cT.ap, craw.ap, AF.Silu), r=[craw.res], w=[cT.res])
    P.barrier()
    A.reset(m_init)

    norm1B = A.alloc("norm1B", [128, D], F32); norm2B = A.alloc("norm2B", [128, D], F32)
    mixB = A.alloc("mixB", [128, D], F32)
    qgB = A.alloc("qgB", [128, QR], F32); kvgB = A.alloc("kvgB", [128, KVR], F32)
    gM = A.alloc("gM", [128, 2, QKM], F32); gD = A.alloc("gD", [128, 2, DD], F32)
    dB = A.alloc("dB", [128, CH], F32); bgluB = A.alloc("bgluB", [128, 2 * CH], F32)
    cw = A.alloc("cw", [128, 3, 44], F32); cb = A.alloc("cb", [128, 44], F32)
    modT = [A.alloc(f"modT{i}", [128, D], F32) for i in range(3)]
    stat = A.alloc("stat", [128, 64], F32)
    statres = [P.res(f"stat{i}") for i in range(16)]
    m_layer = A.mark()

    bc = lambda ap1d: ap1d.partition_broadcast(128)

    for l in range(depth):
        xin_ap = x_in if l == 0 else out
        last = (l == depth - 1)
        dma(norm1B.ap, bc(norm1[l]), writes=[norm1B.res]); dma(norm2B.ap, bc(norm2[l]), writes=[norm2B.res])
        dma(mixB.ap, bc(mix_norm[l]), writes=[mixB.res])
        dma(qgB.ap, bc(mla_q_norm[l]), writes=[qgB.res]); dma(kvgB.ap, bc(mla_kv_norm[l]), writes=[kvgB.res])
        dma(gM.ap.rearrange("p a b -> p (a b)"), bc(mla_qk_gain[l].rearrange("a b -> (a b)")), writes=[gM.res])
        dma(gD.ap.rearrange("p a b -> p (a b)"), bc(dil_qk_gain[l].rearrange("a b -> (a b)")), writes=[gD.res])
        dma(dB.ap, bc(ssm_d[l]), writes=[dB.res]); dma(bgluB.ap, bc(ssm_b_glu[l]), writes=[bgluB.res])
        dma(cw.ap, ffn_conv_w[l].rearrange("j (f p) -> p j f", p=128), writes=[cw.res], ncont=True)
        dma(cb.ap, ffn_conv_b[l].rearrange("(f p) -> p f", p=128), writes=[cb.res], ncont=True)

        m = A.mark()
        CHK = 4096
        stg_f = A.ring("stgf", [128, CHK], F32, 2); stg_b = A.ring("stgb", [128, CHK], BF16, 2)
        ci = 0
        for (src, dst, key, rows, cols) in ((w_in[l], win_b, "win_b", D, INW), (mla_w_uq[l], wuq_b, "wuq_b", QR, HM * QKM),
                                            (mla_w_ukv[l], wukv_b, "wukv_b", KVR, HM * 128), (ssm_w_glu[l], wglu_b, "wglu_b", CH, 2 * CH),
                                            (w_out[l], wout_b, "wout_b", D, D),
                                            (ffn_w_down[l], wdown_b, "wdown_b", FF, D)):
            n = rows * cols
            per = n // 128
            assert per * 128 == n
            sflat = src.rearrange("r c -> (r c)").rearrange("(p x) -> p x", p=128)
            dflat = dst.rearrange("r c -> (r c)").rearrange("(p x) -> p x", p=128)
            o0 = 0
            while o0 < per:
                w_ = min(CHK, per - o0)
                sf = stg_f.next(); sb = stg_b.next()
                dma(sf.ap[:, 0:w_], sflat[:, o0:o0 + w_], writes=[sf.res])
                eng = ("dve", "act")[ci % 2]
                ci += 1
                if eng == "act":
                    ACT(lambda e, sf=sf, sb=sb, w_=w_: e.copy(sb.ap[:, 0:w_], sf.ap[:, 0:w_]), r=[sf.res], w=[sb.res])
                else:
                    P.op(eng, lambda e, sf=sf, sb=sb, w_=w_: e.tensor_copy(sb.ap[:, 0:w_], sf.ap[:, 0:w_]), reads=[sf.res], writes=[sb.res])
                dma(dflat[:, o0:o0 + w_], sb.ap[:, 0:w_], reads=[sb.res], writes=[dres[key]], wgroup="wcast")
                o0 += w_
        for c in range(11):
            sf = stg_f.next(); sb = stg_b.next()
            dma(sf.ap.rearrange("p (k n) -> p k n", k=8), ffn_w_up[l][:, 512 * c:512 * c + 512].rearrange("(k p) n -> p k n", p=128), writes=[sf.res])
            src4 = sf.ap.rearrange("p (k f n) -> p k f n", k=8, f=4)
            dst4 = sb.ap.rearrange("p (f k n) -> p k f n", f=4, k=8)
            eng = ("dve", "act")[ci % 2]
            ci += 1
            if eng == "act":
                ACT(lambda e, src4=src4, dst4=dst4: e.copy(dst4, src4), r=[sf.res], w=[sb.res])
            else:
                P.op(eng, lambda e, src4=src4, dst4=dst4: e.tensor_copy(dst4, src4), reads=[sf.res], writes=[sb.res])
            dma(wup_b[4 * c:4 * c + 4].rearrange("f p x -> p f x"), sb.ap.rearrange("p (f x) -> p f x", f=4), reads=[sb.res], writes=[dres["wup_b"]], wgroup="wcast")
        P.barrier()
        A.reset(m)

        m = A.mark()
        wm = A.ring("wm", [128, 8, 512], F32, 2)
        modS = A.alloc("modS", [NB, 6 * D], F32); bmS = A.alloc("bmS", [NB, 6 * D], F32)
        dma(bmS.ap, b_mod[l].partition_broadcast(NB), writes=[bmS.res])
        for cc in range(12):
            wt = wm.next()
            dma(wt.ap, w_mod[l][:, 512 * cc:512 * (cc + 1)].rearrange("(k p) n -> p k n", p=128), writes=[wt.res])
            bk = cc % 2
            for k in range(8):
                PE(lambda e, k=k, wt=wt, bk=bk: e.matmul(bankF(bk)[0:NB, :], cT.ap[:, :, k], wt.ap[:, k, :], start=(k == 0), stop=(k == 7)),
                   r=[cT.res, wt.res], w=[psres[bk]], g=("modmm", l, cc))
            V(lambda e, cc=cc, bk=bk: e.tensor_add(modS.ap[:, 512 * cc:512 * (cc + 1)], bankF(bk)[0:NB, :], bmS.ap[:, 512 * cc:512 * (cc + 1)]),
              r=[psres[bk], bmS.res], w=[modS.res], g="modS")
        dma(modD, modS.ap, reads=[modS.res], writes=[dres["modD"]])
        dbg(f"mod{l}", modS.ap, [modS.res])
        dbg("cT", cT.ap.rearrange("p a b -> p (a b)"), [cT.res])
        A.reset(m)
        P.barrier()
        if stop_after == "mod":
            break

        def load_mod(tile, b, idx, plus_one_times=None):
            dma(tile.ap, modD[b, idx * D:(idx + 1) * D].partition_broadcast(128), reads=[dres["modD"]], writes=[tile.res])
            if plus_one_times is not None:
                V(lambda e: e.scalar_tensor_tensor(tile.ap, tile.ap, 1.0, plus_one_times.ap, ALU.add, ALU.mult),
                  r=[tile.res, plus_one_times.res], w=[tile.res])

        def phase_proj(b):
            m = A.mark()
            G1, SH1 = modT[0], modT[1]
            load_mod(G1, b, 1, norm1B)
            load_mod(SH1, b, 0)
            win_sb = A.alloc("win_sb", [128, 8, INW], BF16)
            wuq_sb = A.alloc("wuq_sb", [128, 2, HM * QKM], BF16)
            wukv_sb = A.alloc("wukv_sb", [128, HM * 128], BF16)
            dma(win_sb.ap, win_b.rearrange("(k p) n -> p k n", p=128), reads=[dres["win_b"]], writes=[win_sb.res])
            dma(wuq_sb.ap[:, 0, :], wuq_b[0:128, :], reads=[dres["wuq_b"]], writes=[wuq_sb.res], wgroup="wuq")
            dma(wuq_sb.ap[0:64, 1, :], wuq_b[128:192, :], reads=[dres["wuq_b"]], writes=[wuq_sb.res], wgroup="wuq")
            dma(wukv_sb.ap, wukv_b, reads=[dres["wukv_b"]], writes=[wukv_sb.res])
            xt_r = A.ring("xt", [128, D], F32, 2)
            junk = A.alloc("junk", [128, D], BF16)
            hf_r = A.ring("hf", [128, D], F32, 2)
            hb_r = A.ring("hb", [128, D], BF16, 2)
            hT_r = A.ring("hT", [128, 8, 128], BF16, 2)
            u_sb_r = A.ring("u_sb", [128, INW], F32, 2)
            sq_r = A.ring("sq", [128, 768], F32, 2)
            cn_r = A.ring("cn", [128, 320], BF16, 2)
            cTs_r = A.ring("cTs", [128, 3, 128], BF16, 2)
            q_sb_r = A.ring("q_sb", [128, HM, QKM], F32, 2)
            kv_sb_r = A.ring("kv_sb", [128, HM, 128], F32, 2)
            qn_r = A.ring("qn", [128, HM, QKM], F32, 2)
            qbf_r = A.ring("qbf", [128, HM, QKM], BF16, 2)
            kbf_r = A.ring("kbf", [128, HM, QKM], BF16, 2)
            kr_r = A.ring("kr", [128, 4, 32], F32, 2)
            vbf_r = A.ring("vbf", [128, HM, 65], BF16, 2)
            vdbf_r = A.ring("vdbf", [128, HD, 65], BF16, 2)
            for t_ in vbf_r.tiles + vdbf_r.tiles:
                GP(lambda e, t_=t_: e.memset(t_.ap, 1.0), w=[t_.res])
            qkT_r = A.ring("qkT", [128, 2, HM, 128], BF16, 2)
            dn_r = A.ring("dn", [128, 2, HD, DD], F32, 2)
            dbf_r = A.ring("dbf", [128, 2, HD, DD], BF16, 2)
            rt_r = A.ring("rt", [128, 2, HD, 32], F32, 2)
            rt2_r = A.ring("rt2", [128, 2, HD, 32], F32, 2)
            dT_r = A.ring("dT", [128, 2, 3, 128], BF16, 2)
            ucb_r = A.ring("ucb", [128, CH], BF16, 2)
            ucT_r = A.ring("ucTs", [128, 2, 128], BF16, 2)
            S_ = lambda i, n=1: stat.ap[:, 4 * i:4 * i + n]

            def tile_gen(tl):
                tt = b * (S // 128) + tl
                t0 = tl * 128
                hf = hf_r.next(); hb = hb_r.next(); u_sb = u_sb_r.next(); sq = sq_r.next(); cn = cn_r.next(); cTs = cTs_r.next()
                q_sb = q_sb_r.next(); kv_sb = kv_sb_r.next(); qn = qn_r.next(); qbf = qbf_r.next(); kbf = kbf_r.next(); kr = kr_r.next()
                dn = dn_r.next(); dbf = dbf_r.next(); rt = rt_r.next(); rt2 = rt2_r.next(); ucb = ucb_r.next()
                xt = xt_r.next()
                dma(xt.ap, xin_ap[tt * 128:(tt + 1) * 128, :], reads=[xres_tt[tt]], writes=[xt.res])
                ACT(lambda e, xt=xt: e.activation(junk.ap, xt.ap, AF.Square, accum_out=S_(0)), r=[xt.res], w=[junk.res, statres[0]])
                ACT(lambda e: e.activation(S_(1), S_(0), AF.Sqrt, bias=EPS, scale=1.0 / D), r=[statres[0]], w=[statres[1]])
                V(lambda e: e.reciprocal(S_(2), S_(1)), r=[statres[1]], w=[statres[2]])
                yield
                V(lambda e, xt=xt: e.scalar_tensor_tensor(hf.ap, xt.ap, S_(2), G1.ap, ALU.mult, ALU.mult), r=[xt.res, statres[2], G1.res], w=[hf.res])
                V(lambda e: e.tensor_tensor(hb.ap, hf.ap, SH1.ap, ALU.add), r=[hf.res, SH1.res], w=[hb.res])
                if tl == 0 and b == 0:
                    dbg(f"h{l}", hb.ap, [hb.res])
                yield
                for k in range(8):
                    PE(lambda e, k=k: e.transpose(psB[:, 128 * k:128 * (k + 1)], hb.ap[:, 128 * k:128 * (k + 1)], ident.ap),
                       r=[hb.res, ident.res], w=[psres[6]], g=("hTt", l, tt))
                hT = hT_r.next()
                ACT(lambda e, hT=hT: e.copy(hT.ap.rearrange("p a b -> p (a b)"), psB[:, 0:1024]), r=[psres[6]], w=[hT.res])
                for ci_, (c0, c1) in enumerate(((0, 512), (512, 1024), (1024, 1536), (1536, INW))):
                    for k in range(8):
                        PE(lambda e, k=k, c0=c0, c1=c1, hT=hT: e.matmul(psF[:, c0:c1], hT.ap[:, k, :], win_sb.ap[:, k, c0:c1], start=(k == 0), stop=(k == 7)),
                           r=[hT.res, win_sb.res], w=[psres[ci_]], g=("umm", l, tt, ci_))
                yield
                ACT(lambda e: e.copy(u_sb.ap[:, 0:1024], psF[:, 0:1024]), r=[psres[0], psres[1]], w=[u_sb.res], g=("uev", l, tt))
                V(lambda e: e.tensor_copy(u_sb.ap[:, 1024:INW], psF[:, 1024:INW]), r=[psres[2], psres[3]], w=[u_sb.res], g=("uev", l, tt))
                if tl == 0 and b == 0:
                    dbg(f"u{l}", u_sb.ap, [u_sb.res])
                yield
                ACT(lambda e: e.activation(sq.ap[:, 0:QR], u_sb.ap[:, 0:QR], AF.Square, accum_out=S_(3)), r=[u_sb.res], w=[sq.res, statres[3]])
                ACT(lambda e: e.activation(sq.ap[:, 0:KVR], u_sb.ap[:, OFF_CKV:OFF_CKV + KVR], AF.Square, accum_out=S_(4)), r=[u_sb.res], w=[sq.res, statres[4]])
                ACT(lambda e: e.activation(S_(5), S_(3), AF.Sqrt, bias=EPS, scale=1.0 / QR), r=[statres[3]], w=[statres[5]])
                ACT(lambda e: e.activation(S_(6), S_(4), AF.Sqrt, bias=EPS, scale=1.0 / KVR), r=[statres[4]], w=[statres[6]])
                V(lambda e: e.reciprocal(S_(5), S_(5)), r=[statres[5]], w=[statres[5]])
                V(lambda e: e.reciprocal(S_(6), S_(6)), r=[statres[6]], w=[statres[6]])
                V(lambda e: e.scalar_tensor_tensor(cn.ap[:, 0:QR], u_sb.ap[:, 0:QR], S_(5), qgB.ap, ALU.mult, ALU.mult),
                  r=[u_sb.res, statres[5], qgB.res], w=[cn.res], g=("cn", l, tt))
                V(lambda e: e.scalar_tensor_tensor(cn.ap[:, QR:QR + KVR], u_sb.ap[:, OFF_CKV:OFF_CKV + KVR], S_(6), kvgB.ap, ALU.mult, ALU.mult),
                  r=[u_sb.res, statres[6], kvgB.res], w=[cn.res], g=("cn", l, tt))
                yield
                PE(lambda e: e.transpose(psB[:, 1024:1152], cn.ap[:, 0:128], ident.ap), r=[cn.res, ident.res], w=[psres[7]], g=("cTt", l, tt))
                PE(lambda e: e.transpose(psB[0:64, 1152:1280], cn.ap[:, 128:192], ident.ap), r=[cn.res, ident.res], w=[psres[7]], g=("cTt", l, tt))
                PE(lambda e: e.transpose(psB[:, 1280:1408], cn.ap[:, 192:320], ident.ap), r=[cn.res, ident.res], w=[psres[7]], g=("cTt", l, tt))
                ACT(lambda e: e.copy(cTs.ap.rearrange("p a b -> p (a b)"), psB[:, 1024:1408]), r=[psres[7]], w=[cTs.res])
                yield
                PE(lambda e: e.matmul(psF[:, 2048:2560], cTs.ap[:, 0, :], wuq_sb.ap[:, 0, 0:512], start=True, stop=False), r=[cTs.res, wuq_sb.res], w=[psres[4]], g=("qp", l, tt))
                PE(lambda e: e.matmul(psF[:, 2048:2560], cTs.ap[0:64, 1, :], wuq_sb.ap[0:64, 1, 0:512], start=False, stop=True), r=[cTs.res, wuq_sb.res], w=[psres[4]], g=("qp", l, tt))
                PE(lambda e: e.matmul(psF[:, 2560:2624], cTs.ap[:, 0, :], wuq_sb.ap[:, 0, 512:576], start=True, stop=False), r=[cTs.res, wuq_sb.res], w=[psres[5]], g=("qp2", l, tt))
                PE(lambda e: e.matmul(psF[:, 2560:2624], cTs.ap[0:64, 1, :], wuq_sb.ap[0:64, 1, 512:576], start=False, stop=True), r=[cTs.res, wuq_sb.res], w=[psres[5]], g=("qp2", l, tt))
                ACT(lambda e: e.copy(q_sb.ap.rearrange("p a b -> p (a b)"), psF[:, 2048:2624]), r=[psres[4], psres[5]], w=[q_sb.res])
                yield
                PE(lambda e: e.matmul(psF[:, 2048:2560], cTs.ap[:, 2, :], wukv_sb.ap[:, 0:512], start=True, stop=True), r=[cTs.res, wukv_sb.res], w=[psres[4]])
                PE(lambda e: e.matmul(psF[:, 2560:2816], cTs.ap[:, 2, :], wukv_sb.ap[:, 512:768], start=True, stop=True), r=[cTs.res, wukv_sb.res], w=[psres[5]])
                V(lambda e: e.tensor_copy(kv_sb.ap.rearrange("p a b -> p (a b)"), psF[:, 2048:2816]), r=[psres[4], psres[5]], w=[kv_sb.res])
                yield
                sq3 = sq.ap[:, 0:HM * QKM].rearrange("p (a b) -> p a b", a=HM)
                V(lambda e: e.tensor_tensor(sq3, q_sb.ap, q_sb.ap, ALU.mult), r=[q_sb.res], w=[sq.res])
                V(lambda e: e.tensor_reduce(S_(7, 6), sq3, AX.X, ALU.add), r=[sq.res], w=[statres[7]])
                ACT(lambda e: e.activation(S_(7, 6), S_(7, 6), AF.Sqrt, bias=EPS, scale=1.0 / QKM), r=[statres[7]], w=[statres[7]])
                V(lambda e: e.reciprocal(S_(7, 6), S_(7, 6)), r=[statres[7]], w=[statres[7]])
                V(lambda e: e.tensor_tensor(qn.ap, q_sb.ap, S_(7, 6).unsqueeze(2).to_broadcast([128, HM, QKM]), ALU.mult), r=[q_sb.res, statres[7]], w=[qn.res])
                V(lambda e: e.tensor_tensor(qn.ap, qn.ap, gM.ap[:, 0:1, :].to_broadcast([128, HM, QKM]), ALU.mult), r=[qn.res, gM.res], w=[qn.res])
                yield
                cs_m = cosM.ap[:, tt:tt + 1, :].to_broadcast([128, HM, 16]); sn_m = sinM.ap[:, tt:tt + 1, :].to_broadcast([128, HM, 16])
                qa = qn.ap[:, :, 64:80]; qb = qn.ap[:, :, 80:96]
                r1 = rt.ap[:, 0, :, 0:16]; r2 = rt.ap[:, 0, :, 16:32]; r3 = rt.ap[:, 1, :, 0:16]; r4 = rt.ap[:, 1, :, 16:32]
                ACT(lambda e: e.copy(qbf.ap[:, :, 0:64], qn.ap[:, :, 0:64]), r=[qn.res], w=[qbf.res], g=("qbf", l, tt))
                V(lambda e, cs_m=cs_m: e.tensor_tensor(r1, qa, cs_m, ALU.mult), r=[qn.res, cosM.res], w=[rt.res], g=("rt", l, tt, 0))
                V(lambda e, sn_m=sn_m: e.tensor_tensor(r2, qb, sn_m, ALU.mult), r=[qn.res, sinM.res], w=[rt.res], g=("rt", l, tt, 0))
                V(lambda e, sn_m=sn_m: e.tensor_tensor(r3, qa, sn_m, ALU.mult), r=[qn.res, sinM.res], w=[rt.res], g=("rt", l, tt, 0))
                V(lambda e, cs_m=cs_m: e.tensor_tensor(r4, qb, cs_m, ALU.mult), r=[qn.res, cosM.res], w=[rt.res], g=("rt", l, tt, 0))
                V(lambda e: e.tensor_tensor(qbf.ap[:, :, 64:80], r1, r2, ALU.subtract), r=[rt.res], w=[qbf.res], g=("qbf", l, tt))
                V(lambda e: e.tensor_tensor(qbf.ap[:, :, 80:96], r3, r4, ALU.add), r=[rt.res], w=[qbf.res], g=("qbf", l, tt))
                yield
                sqk = sq.ap[:, 0:HM * 64].rearrange("p (a b) -> p a b", a=HM)
                V(lambda e: e.tensor_tensor(sqk, kv_sb.ap[:, :, 0:64], kv_sb.ap[:, :, 0:64], ALU.mult), r=[kv_sb.res], w=[sq.res])
                V(lambda e: e.tensor_reduce(S_(9, 6), sqk, AX.X, ALU.add), r=[sq.res], w=[statres[9]])
                ACT(lambda e: e.activation(kr.ap[:, 3, :], u_sb.ap[:, OFF_KR:OFF_KR + 32], AF.Square, accum_out=S_(11)), r=[u_sb.res], w=[kr.res, statres[11]], g=("kr", l, tt))
                V(lambda e: e.tensor_scalar(S_(9, 6), S_(9, 6), S_(11), None, ALU.add), r=[statres[9], statres[11]], w=[statres[9]])
                ACT(lambda e: e.activation(S_(9, 6), S_(9, 6), AF.Sqrt, bias=EPS, scale=1.0 / QKM), r=[statres[9]], w=[statres[9]])
                V(lambda e: e.reciprocal(S_(9, 6), S_(9, 6)), r=[statres[9]], w=[statres[9]])
                V(lambda e: e.tensor_tensor(qn.ap[:, :, 0:64], kv_sb.ap[:, :, 0:64], S_(9, 6).unsqueeze(2).to_broadcast([128, HM, 64]), ALU.mult),
                  r=[kv_sb.res, statres[9], qbf.res, rt.res], w=[qn.res])
                V(lambda e: e.tensor_tensor(kbf.ap[:, :, 0:64], qn.ap[:, :, 0:64], gM.ap[:, 1:2, 0:64].to_broadcast([128, HM, 64]), ALU.mult),
                  r=[qn.res, gM.res], w=[kbf.res], g=("kbf", l, tt))
                yield
                V(lambda e: e.tensor_tensor(kr.ap[:, 0, :], u_sb.ap[:, OFF_KR:OFF_KR + 32], gM.ap[:, 1, 64:96], ALU.mult), r=[u_sb.res, gM.res], w=[kr.res], g=("kr", l, tt))
                ka = kr.ap[:, 0, 0:16]; kb_ = kr.ap[:, 0, 16:32]
                c1 = cosM.ap[:, tt, :]; s1 = sinM.ap[:, tt, :]
                V(lambda e, c1=c1: e.tensor_tensor(kr.ap[:, 1, 0:16], ka, c1, ALU.mult), r=[kr.res, cosM.res], w=[kr.res])
                V(lambda e, s1=s1: e.tensor_tensor(kr.ap[:, 1, 16:32], kb_, s1, ALU.mult), r=[kr.res, sinM.res], w=[kr.res])
                V(lambda e, s1=s1: e.tensor_tensor(kr.ap[:, 2, 0:16], ka, s1, ALU.mult), r=[kr.res, sinM.res], w=[kr.res])
                V(lambda e, c1=c1: e.tensor_tensor(kr.ap[:, 2, 16:32], kb_, c1, ALU.mult), r=[kr.res, cosM.res], w=[kr.res])
                V(lambda e: e.tensor_tensor(kr.ap[:, 3, 0:16], kr.ap[:, 1, 0:16], kr.ap[:, 1, 16:32], ALU.subtract), r=[kr.res], w=[kr.res])
                V(lambda e: e.tensor_tensor(kr.ap[:, 3, 16:32], kr.ap[:, 2, 0:16], kr.ap[:, 2, 16:32], ALU.add), r=[kr.res], w=[kr.res])
                V(lambda e: e.tensor_tensor(kbf.ap[:, :, 64:96], kr.ap[:, 3:4, :].to_broadcast([128, HM, 32]), S_(9, 6).unsqueeze(2).to_broadcast([128, HM, 32]), ALU.mult),
                  r=[kr.res, statres[9]], w=[kbf.res], g=("kbf", l, tt))
                vbf = vbf_r.next()
                ACT(lambda e, vbf=vbf: e.copy(vbf.ap[:, :, 0:64], kv_sb.ap[:, :, 64:128]), r=[kv_sb.res], w=[vbf.res])
                dma(vmD[b, t0:t0 + 128, :], vbf.ap.rearrange("p a b -> p (a b)"), reads=[vbf.res], writes=[dres[("vmD", b)]], wgroup="vm")
                if tl == 0 and b == 0:
                    dbg(f"qbf{l}", qbf.ap.rearrange("p a b -> p (a b)"), [qbf.res])
                    dbg(f"kbf{l}", kbf.ap.rearrange("p a b -> p (a b)"), [kbf.res])
                yield
                for h in range(HM):
                    PE(lambda e, h=h: e.transpose(psB[0:QKM, 128 * h:128 * (h + 1)], qbf.ap[:, h, :], ident.ap), r=[qbf.res, ident.res], w=[psres[6]], g=("qTt", l, tt))
                    PE(lambda e, h=h: e.transpose(psB[0:QKM, 1024 + 128 * h:1024 + 128 * (h + 1)], kbf.ap[:, h, :], ident.ap), r=[kbf.res, ident.res], w=[psres[7]], g=("kTt", l, tt))
                qkT = qkT_r.next()
                ACT(lambda e, qkT=qkT: e.copy(qkT.ap[0:QKM, 0, :, :].rearrange("p a b -> p (a b)"), psB[0:QKM, 0:768]), r=[psres[6]], w=[qkT.res], g=("qkT", l, tt))
                V(lambda e, qkT=qkT: e.tensor_copy(qkT.ap[0:QKM, 1, :, :].rearrange("p a b -> p (a b)"), psB[0:QKM, 1024:1792]), r=[psres[7]], w=[qkT.res], g=("qkT", l, tt))
                dma(qmT[b, :, :, t0:t0 + 128].rearrange("h d t -> d h t"), qkT.ap[0:QKM, 0, :, :], reads=[qkT.res], writes=[dres[("qmT", b)]], wgroup="qm")
                dma(kmT[b, :, :, t0:t0 + 128].rearrange("h d t -> d h t"), qkT.ap[0:QKM, 1, :, :], reads=[qkT.res], writes=[dres[("kmT", b)]], wgroup="km")
                yield
                dqk = u_sb.ap[:, OFF_DQ:OFF_DQ + 768].rearrange("p (s h d) -> p s h d", s=2, h=HD)
                sq4 = sq.ap[:, 0:768].rearrange("p (s h d) -> p s h d", s=2, h=HD)
                V(lambda e: e.tensor_tensor(sq.ap[:, 0:768], u_sb.ap[:, OFF_DQ:OFF_DQ + 768], u_sb.ap[:, OFF_DQ:OFF_DQ + 768], ALU.mult), r=[u_sb.res], w=[sq.res])
                V(lambda e: e.tensor_reduce(S_(12, 12), sq.ap[:, 0:768].rearrange("p (a b) -> p a b", a=12), AX.X, ALU.add), r=[sq.res], w=[statres[12]])
                ACT(lambda e: e.activation(S_(12, 12), S_(12, 12), AF.Sqrt, bias=EPS, scale=1.0 / DD), r=[statres[12]], w=[statres[12]])
                V(lambda e: e.reciprocal(S_(12, 12), S_(12, 12)), r=[statres[12]], w=[statres[12]])
                dn3 = dn.ap.rearrange("p s h d -> p (s h) d")
                V(lambda e: e.tensor_tensor(dn3, u_sb.ap[:, OFF_DQ:OFF_DQ + 768].rearrange("p (a b) -> p a b", a=12), S_(12, 12).unsqueeze(2).to_broadcast([128, 12, DD]), ALU.mult),
                  r=[u_sb.res, statres[12]], w=[dn.res])
                yield
                for s_ in range(2):
                    V(lambda e, s_=s_: e.tensor_tensor(dn.ap[:, s_, :, :], dn.ap[:, s_, :, :], gD.ap[:, s_:s_ + 1, :].to_broadcast([128, HD, DD]), ALU.mult),
                      r=[dn.res, gD.res], w=[dn.res])
                yield
                cs_d = cosD.ap[:, tt:tt + 1, :].to_broadcast([128, 12, 32]); sn_d = sinD.ap[:, tt:tt + 1, :].to_broadcast([128, 12, 32])
                da = dn3[:, :, 0:32]; db = dn3[:, :, 32:64]
                rt3 = rt.ap.rearrange("p s h d -> p (s h) d"); rt23 = rt2.ap.rearrange("p s h d -> p (s h) d")
                dbf3 = dbf.ap.rearrange("p s h d -> p (s h) d")
                V(lambda e, cs_d=cs_d: e.tensor_tensor(rt3, da, cs_d, ALU.mult), r=[dn.res, cosD.res, qbf.res], w=[rt.res])
                V(lambda e, sn_d=sn_d: e.tensor_tensor(rt23, db, sn_d, ALU.mult), r=[dn.res, sinD.res], w=[rt2.res])
                V(lambda e: e.tensor_tensor(dbf3[:, :, 0:32], rt3, rt23, ALU.subtract), r=[rt.res, rt2.res], w=[dbf.res], g=("dbf", l, tt))
                yield
                V(lambda e, sn_d=sn_d: e.tensor_tensor(rt3, da, sn_d, ALU.mult), r=[dn.res, sinD.res, dbf.res], w=[rt.res])
                V(lambda e, cs_d=cs_d: e.tensor_tensor(rt23, db, cs_d, ALU.mult), r=[dn.res, cosD.res, dbf.res], w=[rt2.res])
                V(lambda e: e.tensor_tensor(dbf3[:, :, 32:64], rt3, rt23, ALU.add), r=[rt.res, rt2.res], w=[dbf.res], g=("dbf", l, tt))
                vdbf = vdbf_r.next()
                ACT(lambda e, vdbf=vdbf: e.copy(vdbf.ap[:, :, 0:64], u_sb.ap[:, OFF_DV:OFF_DV + 384].rearrange("p (a b) -> p a b", a=HD)), r=[u_sb.res], w=[vdbf.res])
                dma(vdD[b, t0:t0 + 128, :], vdbf.ap.rearrange("p a b -> p (a b)"), reads=[vdbf.res], writes=[dres[("vdD", b)]], wgroup="vd")
                if tl == 0 and b == 0:
                    dbg(f"dbf{l}", dbf.ap.rearrange("p s h d -> p (s h d)"), [dbf.res])
                yield
                dbf2 = dbf.ap.rearrange("p s h d -> p s (h d)")
                for s_ in range(2):
                    for j in range(3):
                        PE(lambda e, s_=s_, j=j: e.transpose(psB[:, 1024 * s_ + 128 * j:1024 * s_ + 128 * (j + 1)], dbf2[:, s_, 128 * j:128 * (j + 1)], ident.ap),
                           r=[dbf.res, ident.res], w=[psres[6 + s_]], g=("dTt", l, tt, s_))
                dT = dT_r.next()
                ACT(lambda e, dT=dT: e.copy(dT.ap[:, 0, :, :].rearrange("p a b -> p (a b)"), psB[:, 0:384]), r=[psres[6]], w=[dT.res], g=("dT", l, tt))
                V(lambda e, dT=dT: e.tensor_copy(dT.ap[:, 1, :, :].rearrange("p a b -> p (a b)"), psB[:, 1024:1408]), r=[psres[7]], w=[dT.res], g=("dT", l, tt))
                dma(qdT[b, :, :, t0:t0 + 128].rearrange("j d t -> d j t"), dT.ap[:, 0, :, :], reads=[dT.res], writes=[dres[("qdT", b)]], wgroup="qd")
                dma(kdT[b, :, :, t0:t0 + 128].rearrange("j d t -> d j t"), dT.ap[:, 1, :, :], reads=[dT.res], writes=[dres[("kdT", b)]], wgroup="kd")
                yield
                dma(uctok[b, t0:t0 + 128, :], u_sb.ap[:, OFF_UC:OFF_UC + CH], reads=[u_sb.res], writes=[dres[("uctok", b)]], wgroup="uct")
                ACT(lambda e: e.copy(ucb.ap, u_sb.ap[:, OFF_UC:OFF_UC + CH]), r=[u_sb.res], w=[ucb.res])
                for hh in range(2):
                    PE(lambda e, hh=hh: e.transpose(psB[:, 512 + 128 * hh:512 + 128 * (hh + 1)], ucb.ap[:, 128 * hh:128 * (hh + 1)], ident.ap),
                       r=[ucb.res, ident.res], w=[psres[6]], g=("ucTt", l, tt))
                ucTs = ucT_r.next()
                ACT(lambda e, ucTs=ucTs: e.copy(ucTs.ap.rearrange("p a b -> p (a b)"), psB[:, 512:768]), r=[psres[6]], w=[ucTs.res])
                dma(ucT[b, :, t0:t0 + 128].rearrange("(hh p) t -> p hh t", p=128), ucTs.ap, reads=[ucTs.res], writes=[dres[("ucT", b)]], wgroup="uc")
            active = []
            nxt = 0
            ntile = S // 128
            while active or nxt < ntile:
                if len(active) < 2 and nxt < ntile:
                    active.append(tile_gen(nxt))
                    nxt += 1
                for g_ in list(active):
                    try:
                        next(g_)
                    except StopIteration:
                        active.remove(g_)
            A.reset(m)
            P.barrier()

        def phase_attn(b, kind):
            m = A.mark()
            if kind == "mla":
                KD, scale, off = QKM, QKM ** -0.5, 0
                qsrc, ksrc, vsrc, rq, rk_, rv = qmT, kmT, vmD, dres[("qmT", b)], dres[("kmT", b)], dres[("vmD", b)]
                qall = A.alloc("qall", [128, HM, S], BF16); kall = A.alloc("kall", [128, HM, S], BF16)
                dma(qall.ap[0:QKM], qsrc[b].rearrange("h d t -> d h t"), reads=[rq], writes=[qall.res])
                dma(kall.ap[0:QKM], ksrc[b].rearrange("h d t -> d h t"), reads=[rk_], writes=[kall.res])
                qv = lambda h, c0, c1: qall.ap[0:QKM, h, c0:c1]
                kv_ = lambda h, c0, c1: kall.ap[0:QKM, h, c0:c1]
            else:
                KD, scale, off = DD, DD ** -0.5, 384
                qsrc, ksrc, vsrc, rq, rk_, rv = qdT, kdT, vdD, dres[("qdT", b)], dres[("kdT", b)], dres[("vdD", b)]
                qall = A.alloc("qall", [128, 3, S], BF16); kall = A.alloc("kall", [128, 3, S], BF16)
                dma(qall.ap, qsrc[b].rearrange("j d t -> d j t"), reads=[rq], writes=[qall.res])
                dma(kall.ap, ksrc[b].rearrange("j d t -> d j t"), reads=[rk_], writes=[kall.res])
                qv = lambda h, c0, c1: qall.ap[64 * (h % 2):64 * (h % 2) + 64, h // 2, c0:c1]
                kv_ = lambda h, c0, c1: kall.ap[64 * (h % 2):64 * (h % 2) + 64, h // 2, c0:c1]
            Vs = A.alloc("Vs", [128, 16, 6 * 65], BF16)
            dma(Vs.ap, vsrc[b].rearrange("(t p) c -> p t c", p=128), reads=[rv], writes=[Vs.res])
            pT_r = A.ring("pT", [128, 1024], BF16, 4)
            pTm_r = A.ring("pTm", [128, 1024], BF16, 4)
            osb_r = A.ring("osb", [128, 4, 65], F32, 2)
            oa = A.alloc("oa", [128, 4, 384], F32)
            rden = A.alloc("rden", [128, 4], F32)
            mx_r = A.ring("mx", [128, 4, 384], BF16, 2)
            junk2 = A.alloc("junk2", [128, 384], BF16)
            steps = []
            for qc in range(4):
                for h in range(6):
                    kts = []
                    for kt in range(16):
                        if kind == "dil":
                            dmin = 128 * kt - 512 * qc - 511
                            dmax = 128 * kt + 127 - 512 * qc
                            if dmin > 1024 or dmax < -1024:
                                continue
                        kts.append(kt)
                    prs = [kts[i:i + 2] for i in range(0, len(kts), 2)]
                    for pi, pr in enumerate(prs):
                        steps.append((qc, h, pr, pi == 0, pi == len(prs) - 1))

            SPAIR = (0, 2, 6)

            def spair_ap(pb, ncol):
                return psF[:, 512 * pb:512 * pb + ncol] if pb < 6 else psB_f[:, 0:ncol]

            def emit_S(si):
                qc, h, pr, _, _ = steps[si]
                pb = SPAIR[si % 3]
                for i_, kt in enumerate(pr):
                    PE(lambda e, bk=pb + i_, h=h, kt=kt, qc=qc: e.matmul(bankF(bk), kv_(h, 128 * kt, 128 * kt + 128), qv(h, 512 * qc, 512 * qc + 512), start=True, stop=True),
                       r=[qall.res, kall.res], w=[psres[pb + i_]])

            S_ = lambda i, n=1: stat.ap[:, 4 * i:4 * i + n]
            ocnt = 0
            emit_S(0)
            emit_S(1)
            for si, (qc, h, pr, first, last) in enumerate(steps):
                if si + 2 < len(steps):
                    emit_S(si + 2)
                pb = SPAIR[si % 3]
                npr = len(pr)
                ob = 4 + (ocnt % 2)
                pT = pT_r.next()
                ACT(lambda e, pb=pb, pT=pT, npr=npr: e.activation(pT.ap[:, 0:512 * npr], spair_ap(pb, 512 * npr), AF.Exp, bias=0.0, scale=scale),
                    r=[psres[pb + i_] for i_ in range(npr)], w=[pT.res])
                if kind == "dil":
                    pm = pTm_r.next()
                    for i_, kt in enumerate(pr):
                        x0 = 1920 - 128 * (kt - 4 * qc)
                        eng = "dve"
                        P.op(eng, lambda e, pm=pm, pT=pT, x0=x0, i_=i_: e.tensor_tensor(pm.ap[:, 512 * i_:512 * i_ + 512], pT.ap[:, 512 * i_:512 * i_ + 512], maskS.ap[:, x0:x0 + 512], ALU.mult),
                             reads=[pT.res, maskS.res], writes=[pm.res], wgroup=("pm", l, b, si))
                    pT = pm
                for i_, kt in enumerate(pr):
                    for j in range(4):
                        PE(lambda e, j=j, pT=pT, kt=kt, h=h, ob=ob, i_=i_, st_=(first and i_ == 0 and j == 0), sp_=(last and i_ == npr - 1 and j == 3):
                           e.matmul(bankF(ob)[:, 65 * j:65 * j + 65], pT.ap[:, 512 * i_ + 128 * j:512 * i_ + 128 * j + 128], Vs.ap[:, kt, 65 * h:65 * h + 65], start=st_, stop=sp_),
                           r=[pT.res, Vs.res], w=[psres[ob]], g=("pv", l, b, kind, qc, h))
                if last:
                    ocnt += 1
                    osb = osb_r.next()
                    ACT(lambda e, osb=osb, ob=ob: e.copy(osb.ap.rearrange("p a b -> p (a b)"), bankF(ob)[:, 0:260]), r=[psres[ob]], w=[osb.res])
                    V(lambda e, osb=osb: e.reciprocal(rden.ap, osb.ap[:, :, 64]), r=[osb.res], w=[rden.res])
                    V(lambda e, osb=osb, h=h: e.tensor_tensor(oa.ap[:, :, 64 * h:64 * h + 64], osb.ap[:, :, 0:64], rden.ap.unsqueeze(2).to_broadcast([128, 4, 64]), ALU.mult),
                      r=[osb.res, rden.res], w=[oa.res], g=("oa", l, b, kind, qc))
                    if h == 5:
                        for j in range(4):
                            ACT(lambda e, j=j: e.activation(junk2.ap, oa.ap[:, j, :], AF.Square, accum_out=stat.ap[:, j:j + 1]),
                                r=[oa.res], w=[junk2.res, statres[0]], g=("oass", l, b, kind, qc))
                        ACT(lambda e: e.activation(stat.ap[:, 4:8], stat.ap[:, 0:4], AF.Sqrt, bias=EPS, scale=1.0 / 384), r=[statres[0]], w=[statres[1]])
                        V(lambda e: e.reciprocal(stat.ap[:, 4:8], stat.ap[:, 4:8]), r=[statres[1]], w=[statres[1]])
                        mx = mx_r.next()
                        V(lambda e: e.tensor_tensor(oa.ap, oa.ap, stat.ap[:, 4:8].unsqueeze(2).to_broadcast([128, 4, 384]), ALU.mult), r=[oa.res, statres[1]], w=[oa.res])
                        GP(lambda e, mx=mx: e.tensor_tensor(mx.ap, oa.ap, mixB.ap[:, off:off + 384].unsqueeze(1).to_broadcast([128, 4, 384]), ALU.mult), r=[oa.res, mixB.res], w=[mx.res])
                        dma(mixedD[b, 512 * qc:512 * qc + 512, off:off + 384].rearrange("(j p) c -> p j c", p=128), mx.ap, reads=[mx.res], writes=[dres[("mixedD", b)]], wgroup="mixw")
            A.reset(m)
            P.barrier()

        def phase_s5():
            m = A.mark()
            rr = A.alloc("rr", [128, 32], F32); th = A.alloc("th", [128, 32], F32); thb = A.alloc("thb", [128, 32], F32)
            LB = A.alloc("LB", [128, 32, 128], BF16); LBs = A.alloc("LBs", [128, 32, 128], BF16)
            W1T = A.alloc("W1T", [128, 4, 128], BF16); W2T = A.alloc("W2T", [128, 4, 128], BF16)
            scr5 = sincos_scratch(144)
            m_keep = A.mark()
            lre = A.alloc("lre", [128, 32], F32); lim = A.alloc("lim", [128, 32], F32); ldt = A.alloc("ldt", [128, 32], F32)
            for half in range(2):
                dma(lre.ap[64 * half:64 * half + 64, :], ssm_a_re[l].rearrange("d g p -> p (d g)"), writes=[lre.res], wgroup="lre", ncont=True)
                dma(lim.ap[64 * half:64 * half + 64, :], ssm_a_im[l].rearrange("d g p -> p (d g)"), writes=[lim.res], wgroup="lim", ncont=True)
            dma(ldt.ap, ssm_log_dt[l].rearrange("d g -> (d g)").partition_broadcast(128), writes=[ldt.res])
            dtt = A.alloc("dtt", [128, 32], F32)
            ACT(lambda e: e.activation(dtt.ap, ldt.ap, AF.Exp), r=[ldt.res], w=[dtt.res])
            V(lambda e: e.tensor_mul(rr.ap, lre.ap, dtt.ap), r=[lre.res, dtt.res], w=[rr.res])
            ACT(lambda e: e.activation(rr.ap, rr.ap, AF.Exp), r=[rr.res], w=[rr.res])
            V(lambda e: e.tensor_mul(th.ap, lim.ap, dtt.ap), r=[lim.res, dtt.res], w=[th.res])
            V(lambda e: e.tensor_scalar(th.ap, th.ap, INV2PI, None, ALU.mult), r=[th.res], w=[th.res])
            V(lambda e: e.tensor_scalar(thb.ap, th.ap, 1024.0, None, ALU.mult), r=[th.res], w=[thb.res])
            sn0 = A.alloc("sn0", [128, 32], F32); cs0 = A.alloc("cs0", [128, 32], F32)
            sincos(sn0, cs0, th, 32, scr5)
            kre = A.alloc("kre", [128, 32], F32); kim = A.alloc("kim", [128, 32], F32)
            t_a = A.alloc("t_a", [128, 32], F32); t_b = A.alloc("t_b", [128, 32], F32); den = A.alloc("den", [128, 32], F32)
            V(lambda e: e.tensor_mul(cs0.ap, cs0.ap, rr.ap), r=[cs0.res, rr.res], w=[cs0.res])
            V(lambda e: e.tensor_scalar(cs0.ap, cs0.ap, -1.0, None, ALU.add), r=[cs0.res], w=[cs0.res])
            V(lambda e: e.tensor_mul(sn0.ap, sn0.ap, rr.ap), r=[sn0.res, rr.res], w=[sn0.res])
            V(lambda e: e.tensor_mul(t_a.ap, lre.ap, lre.ap), r=[lre.res], w=[t_a.res])
            V(lambda e: e.tensor_mul(t_b.ap, lim.ap, lim.ap), r=[lim.res], w=[t_b.res])
            V(lambda e: e.tensor_add(den.ap, t_a.ap, t_b.ap), r=[t_a.res, t_b.res], w=[den.res])
            V(lambda e: e.reciprocal(den.ap, den.ap), r=[den.res], w=[den.res])
            V(lambda e: e.tensor_mul(t_a.ap, cs0.ap, lre.ap), r=[cs0.res, lre.res, den.res], w=[t_a.res])
            V(lambda e: e.tensor_mul(t_b.ap, sn0.ap, lim.ap), r=[sn0.res, lim.res], w=[t_b.res])
            V(lambda e: e.tensor_add(kre.ap, t_a.ap, t_b.ap), r=[t_a.res, t_b.res], w=[kre.res])
            V(lambda e: e.tensor_mul(kre.ap, kre.ap, den.ap), r=[kre.res, den.res], w=[kre.res])
            V(lambda e: e.tensor_mul(t_a.ap, sn0.ap, lre.ap), r=[sn0.res, lre.res, kre.res], w=[t_a.res])
            V(lambda e: e.tensor_mul(t_b.ap, cs0.ap, lim.ap), r=[cs0.res, lim.res, kre.res], w=[t_b.res])
            V(lambda e: e.tensor_sub(kim.ap, t_a.ap, t_b.ap), r=[t_a.res, t_b.res], w=[kim.res])
            V(lambda e: e.tensor_mul(kim.ap, kim.ap, den.ap), r=[kim.res, den.res], w=[kim.res])
            bre = A.alloc("bre", [64, 32, GC], F32); bim = A.alloc("bim", [64, 32, GC], F32)
            dma(bre.ap, ssm_b_re[l].rearrange("d g p c -> p (d g) c"), writes=[bre.res])
            dma(bim.ap, ssm_b_im[l].rearrange("d g p c -> p (d g) c"), writes=[bim.res])
            Bre = A.alloc("Bre", [64, 32, GC], BF16); Bim = A.alloc("Bim", [64, 32, GC], BF16); Bren = A.alloc("Bren", [64, 32, GC], BF16)
            tb1 = A.alloc("tb1", [64, 32, GC], F32); tb2 = A.alloc("tb2", [64, 32, GC], F32)
            kre_b = kre.ap[0:64, :].unsqueeze(2).to_broadcast([64, 32, GC]); kim_b = kim.ap[0:64, :].unsqueeze(2).to_broadcast([64, 32, GC])
            V(lambda e: e.tensor_tensor(tb1.ap, bre.ap, kre_b, ALU.mult), r=[bre.res, kre.res], w=[tb1.res])
            V(lambda e: e.tensor_tensor(tb2.ap, bim.ap, kim_b, ALU.mult), r=[bim.res, kim.res], w=[tb2.res])
            V(lambda e: e.tensor_tensor(Bre.ap, tb1.ap, tb2.ap, ALU.subtract), r=[tb1.res, tb2.res], w=[Bre.res])
            V(lambda e: e.tensor_tensor(Bren.ap, tb2.ap, tb1.ap, ALU.subtract), r=[tb1.res, tb2.res], w=[Bren.res])
            V(lambda e: e.tensor_tensor(tb1.ap, bim.ap, kre_b, ALU.mult), r=[bim.res, kre.res, Bre.res, Bren.res], w=[tb1.res])
            V(lambda e: e.tensor_tensor(tb2.ap, bre.ap, kim_b, ALU.mult), r=[bre.res, kim.res, Bre.res, Bren.res], w=[tb2.res])
            V(lambda e: e.tensor_tensor(Bim.ap, tb1.ap, tb2.ap, ALU.add), r=[tb1.res, tb2.res], w=[Bim.res])
            for blk in range(4):
                sl = lambda t_, blk=blk: t_.ap[:, 8 * blk:8 * blk + 8, :].rearrange("p a b -> p (a b)")
                s_re, s_im, s_ren = sl(Bre), sl(Bim), sl(Bren)
                PE(lambda e, a_=s_re: e.transpose(psB[:, 0:64], a_, ident.ap[0:64, 0:64]), r=[Bre.res, ident.res], w=[psres[6]], g=("Bt", l, blk))
                PE(lambda e, a_=s_im: e.transpose(psB[:, 64:128], a_, ident.ap[0:64, 0:64]), r=[Bim.res, ident.res], w=[psres[6]], g=("Bt", l, blk))
                PE(lambda e, a_=s_im: e.transpose(psB[:, 128:192], a_, ident.ap[0:64, 0:64]), r=[Bim.res, ident.res], w=[psres[6]], g=("Bt", l, blk))
                PE(lambda e, a_=s_ren: e.transpose(psB[:, 192:256], a_, ident.ap[0:64, 0:64]), r=[Bren.res, ident.res], w=[psres[6]], g=("Bt", l, blk))
                for g8 in range(8):
                    gd = blk * 8 + g8
                    V(lambda e, gd=gd, g8=g8: e.tensor_scalar(LB.ap[:, gd, :], psB[:, 0:128], misc.ap[:, 144 + g8:145 + g8], None, ALU.mult), r=[psres[6], misc.res], w=[LB.res], g=("LB", l))
                    V(lambda e, gd=gd, g8=g8: e.tensor_scalar(LBs.ap[:, gd, :], psB[:, 128:256], misc.ap[:, 144 + g8:145 + g8], None, ALU.mult), r=[psres[6], misc.res], w=[LBs.res], g=("LBs", l))
            Cin = A.alloc("Cin", [128, 4, 128], F32)
            dma(Cin.ap[:, :, 0:64], ssm_c_re[l].rearrange("d g c p -> (d g c) p").rearrange("(k q) p -> q k p", q=128), writes=[Cin.res], wgroup="cin")
            dma(Cin.ap[:, :, 64:128], ssm_c_im[l].rearrange("d g c p -> (d g c) p").rearrange("(k q) p -> q k p", q=128), writes=[Cin.res], wgroup="cin")
            W1s = A.alloc("W1s", [128, 4, 128], BF16); W2s = A.alloc("W2s", [128, 4, 128], BF16)
            V(lambda e: e.tensor_copy(W1s.ap[:, :, 0:64], Cin.ap[:, :, 0:64]), r=[Cin.res], w=[W1s.res], g="w1s")
            V(lambda e: e.tensor_scalar(W1s.ap[:, :, 64:128], Cin.ap[:, :, 64:128], -1.0, None, ALU.mult), r=[Cin.res], w=[W1s.res], g="w1s")
            V(lambda e: e.tensor_scalar(W2s.ap[:, :, 0:64], Cin.ap[:, :, 64:128], -1.0, None, ALU.mult), r=[Cin.res], w=[W2s.res], g="w2s")
            V(lambda e: e.tensor_scalar(W2s.ap[:, :, 64:128], Cin.ap[:, :, 0:64], -1.0, None, ALU.mult), r=[Cin.res], w=[W2s.res], g="w2s")
            for blk in range(4):
                PE(lambda e, blk=blk: e.transpose(psB[:, 1024:1152], W1s.ap[:, blk, :], ident.ap), r=[W1s.res, ident.res], w=[psres[7]], g=("Wt", l, blk))
                PE(lambda e, blk=blk: e.transpose(psB[:, 1152:1280], W2s.ap[:, blk, :], ident.ap), r=[W2s.res, ident.res], w=[psres[7]], g=("Wt", l, blk))
                V(lambda e, blk=blk: e.tensor_copy(W1T.ap[:, blk, :], psB[:, 1024:1152]), r=[psres[7]], w=[W1T.res], g="W1T")
                V(lambda e, blk=blk: e.tensor_copy(W2T.ap[:, blk, :], psB[:, 1152:1280]), r=[psres[7]], w=[W2T.res], g="W2T")
            P.barrier()
            A.reset(m_keep)
            m_main = A.mark()

            def rev(ap2d, c0, n):
                a = ap2d[:, c0:c0 + n]
                return bass.AP(a.tensor, a.offset + (n - 1) * a.ap[-1][0], [list(a.ap[0]), [-a.ap[-1][0], n]])

            yalls = [A.alloc(f"yall{b}", [128, 16, CH], F32) for b in range(NB)]
            m_y = A.mark()
            uT = A.alloc("uT", [128, NB, 2, S], BF16)
            for b in range(NB):
                dma(uT.ap[:, b, :, :], ucT[b].rearrange("(hh p) t -> p hh t", p=128), reads=[dres[("ucT", b)]], writes=[uT.res], wgroup="uTl")
            ytab = A.alloc("ytab", [128, 1024], F32); ftab = A.alloc("ftab", [128, 1024], F32)
            cosTs = [A.alloc(f"cosT{d_}", [128, 16, 128], F32) for d_ in range(2)]
            sinTs = [A.alloc(f"sinT{d_}", [128, 16, 128], F32) for d_ in range(2)]
            z_r = A.ring("z", [128, S], F32, 2)
            t1_r = A.ring("t1", [128, 512], F32, 2); t2_r = A.ring("t2", [128, 512], F32, 2)
            P12 = [[A.alloc(f"P{i}{d_}", [128, S], BF16) for i in range(2)] for d_ in range(2)]
            ycnt = 0
            for g in range(G):
                half = g // 8
                for d_ in range(2):
                    gd = d_ * 16 + g
                    cosT, sinT = cosTs[d_], sinTs[d_]
                    cosf = cosT.ap.rearrange("p a b -> p (a b)"); sinf = sinT.ap.rearrange("p a b -> p (a b)")
                    for hb_ in range(2):
                        c0_ = 1024 * hb_
                        if hb_ == 0:
                            ACT(lambda e, gd=gd: e.activation(ytab.ap, iota.ap, AF.Identity, bias=0.0, scale=th.ap[:, gd:gd + 1]), r=[iota.res, th.res], w=[ytab.res])
                        else:
                            ACT(lambda e, gd=gd: e.activation(ytab.ap, iota.ap, AF.Identity, bias=thb.ap[:, gd:gd + 1], scale=th.ap[:, gd:gd + 1]), r=[iota.res, th.res, thb.res], w=[ytab.res])
                        V(lambda e: e.tensor_scalar(ftab.ap, ytab.ap, MAGIC, MAGIC, ALU.add, ALU.subtract), r=[ytab.res], w=[ftab.res])
                        V(lambda e: e.tensor_sub(ftab.ap, ytab.ap, ftab.ap), r=[ytab.res, ftab.res], w=[ftab.res])
                        ACT(lambda e, sinf=sinf, c0_=c0_: e.activation(sinf[:, c0_:c0_ + 1024], ftab.ap, AF.Sin, bias=0.0, scale=TWO_PI_S), r=[ftab.res], w=[sinT.res], g=("sinT", l, gd))
                        ACT(lambda e: e.activation(ytab.ap, ftab.ap, AF.Abs), r=[ftab.res], w=[ytab.res])
                        ACT(lambda e, cosf=cosf, c0_=c0_: e.activation(cosf[:, c0_:c0_ + 1024], ytab.ap, AF.Sin, bias=HALF_PI_S, scale=-TWO_PI_S), r=[ytab.res], w=[cosT.res], g=("cosT", l, gd))
                units = [(b, d_) for b in range(NB) for d_ in range(2)]

                def stage_A(b, d_, g=g, half=half):
                    gd = d_ * 16 + g
                    cosT, sinT = cosTs[d_], sinTs[d_]
                    cosTf = cosT.ap.rearrange("p a b -> p (a b)"); sinTf = sinT.ap.rearrange("p a b -> p (a b)")
                    z = z_r.next()
                    for c in range(4):
                        cn_ = c if d_ == 0 else 3 - c
                        PE(lambda e, gd=gd, cn_=cn_, b=b: e.matmul(bankF(0), LB.ap[:, gd, :], uT.ap[:, b, half, 512 * cn_:512 * cn_ + 512], start=True, stop=True),
                           r=[LB.res, uT.res], w=[psres[0]])
                        PE(lambda e, gd=gd, cn_=cn_, b=b: e.matmul(bankF(1), LBs.ap[:, gd, :], uT.ap[:, b, half, 512 * cn_:512 * cn_ + 512], start=True, stop=True),
                           r=[LBs.res, uT.res], w=[psres[1]])
                        t1 = t1_r.next(); t2 = t2_r.next()
                        v0 = bankF(0) if d_ == 0 else rev(bankF(0), 0, 512)
                        v1 = bankF(1) if d_ == 0 else rev(bankF(1), 0, 512)
                        V(lambda e, t1=t1, v0=v0, c=c, cosTf=cosTf: e.tensor_tensor(t1.ap, v0, cosTf[:, 512 * c:512 * c + 512], ALU.mult), r=[psres[0], cosT.res], w=[t1.res])
                        V(lambda e, t2=t2, v1=v1, c=c, sinTf=sinTf: e.tensor_tensor(t2.ap, v1, sinTf[:, 512 * c:512 * c + 512], ALU.mult), r=[psres[1], sinT.res], w=[t2.res])
                        GP(lambda e, t1=t1, t2=t2, c=c, z=z: e.tensor_tensor(z.ap[:, 512 * c:512 * c + 512], t1.ap, t2.ap, ALU.add), r=[t1.res, t2.res], w=[z.res], g=("z", l, g, b, d_))
                    return z

                def stage_B(b, d_, z, g=g, half=half):
                    gd = d_ * 16 + g
                    cosT, sinT = cosTs[d_], sinTs[d_]
                    cosTf = cosT.ap.rearrange("p a b -> p (a b)"); sinTf = sinT.ap.rearrange("p a b -> p (a b)")
                    V(lambda e, gd=gd, z=z: e.tensor_tensor_scan(z.ap, rr.ap[:, gd:gd + 1].to_broadcast([128, S]), z.ap, 0.0, ALU.mult, ALU.add), r=[z.res, rr.res], w=[z.res])
                    p1, p2 = P12[d_]
                    o1 = p1.ap if d_ == 0 else rev(p1.ap, 0, S)
                    o2 = p2.ap if d_ == 0 else rev(p2.ap, 0, S)
                    V(lambda e, o1=o1, cosTf=cosTf, z=z: e.tensor_tensor(o1, z.ap, cosTf, ALU.mult), r=[z.res, cosT.res], w=[p1.res])
                    GP(lambda e, o2=o2, sinTf=sinTf, z=z: e.tensor_tensor(o2, z.ap, sinTf, ALU.mult), r=[z.res, sinT.res], w=[p2.res])
                    if b == 0 and g == 0:
                        dbg(f"P1_{l}_{d_}", p1.ap, [p1.res])

                def y_mm(b, g=g, half=half):
                    nonlocal ycnt
                    yall = yalls[b]
                    yb_ = 2 + ycnt % 2
                    ycnt += 1
                    for tl in range(16):
                        for d_ in range(2):
                            blk = d_ * 2 + half
                            for i in range(2):
                                Wt = (W1T, W2T)[i]
                                pp = P12[d_][i]
                                PE(lambda e, tl=tl, i=i, d_=d_, Wt=Wt, blk=blk, yb_=yb_, pp=pp, g=g: e.matmul(bankF(yb_)[:, 16 * tl:16 * tl + 16], pp.ap[:, 128 * tl:128 * tl + 128],
                                                                                       Wt.ap[:, blk, 16 * (g % 8):16 * (g % 8) + 16], start=(d_ == 0 and i == 0), stop=(d_ == 1 and i == 1)),
                                   r=[pp.res, Wt.res], w=[psres[yb_]], g=("ymm", l, g, b, tl))
                    ACT(lambda e, yb_=yb_, yall=yall, g=g: e.copy(yall.ap[:, :, 16 * g:16 * g + 16], bankF(yb_)[:, 0:256].rearrange("p (a b) -> p a b", a=16)), r=[psres[yb_]], w=[yall.res], g=("yall", l, b))

                zs_ = {}
                zs_[0] = stage_A(*units[0])
                for ui, (b, d_) in enumerate(units):
                    if ui + 1 < len(units):
                        zs_[ui + 1] = stage_A(*units[ui + 1])
                    stage_B(b, d_, zs_[ui])
                    if d_ == 1:
                        y_mm(b)
            P.barrier()
            for b in range(NB):
                A.reset(m_y)
                s5_epilogue(b, yalls[b])
                P.barrier()
            A.reset(m)
            P.barrier()

        def s5_epilogue(b, yall):
            m2 = A.mark()
            wglu_sb = A.alloc("wglu_sb", [128, 2, 2 * CH], BF16)
            dma(wglu_sb.ap, wglu_b.rearrange("(k p) n -> p k n", p=128), reads=[dres["wglu_b"]], writes=[wglu_sb.res])
            uct = A.alloc("uct", [128, 16, CH], F32)
            dma(uct.ap, uctok[b].rearrange("(t p) c -> p t c", p=128), reads=[dres[("uctok", b)]], writes=[uct.res])
            yw = A.alloc("yw", [128, 16, CH], F32)
            ybf = A.alloc("ybf", [128, 16, CH], BF16)
            V(lambda e: e.tensor_tensor(uct.ap, uct.ap, dB.ap.unsqueeze(1).to_broadcast([128, 16, CH]), ALU.mult), r=[uct.res, dB.res], w=[uct.res])
            V(lambda e: e.tensor_tensor(yw.ap, yall.ap, uct.ap, ALU.add), r=[yall.res, uct.res], w=[yw.res])
            if b == 0:
                dbg(f"ypre{l}", yw.ap[:, 0, :], [yw.res])
            GP(lambda e: e.tensor_tensor(uct.ap, yw.ap, yw.ap, ALU.mult), r=[yw.res], w=[uct.res])
            V(lambda e: e.tensor_scalar(uct.ap, uct.ap, 0.044715, 1.0, ALU.mult, ALU.add), r=[uct.res], w=[uct.res])
            V(lambda e: e.tensor_tensor(uct.ap, uct.ap, yw.ap, ALU.mult), r=[uct.res, yw.res], w=[uct.res])
            ACT(lambda e: e.activation(uct.ap.rearrange("p a b -> p (a b)"), uct.ap.rearrange("p a b -> p (a b)"), AF.Sigmoid, bias=0.0, scale=1.5957691216057308), r=[uct.res], w=[uct.res])
            V(lambda e: e.tensor_tensor(ybf.ap, uct.ap, yw.ap, ALU.mult), r=[uct.res, yw.res], w=[ybf.res])
            yT_r = A.ring("yTs", [128, 2, 128], BF16, 2)
            glu = A.alloc("glu", [128, 2 * CH], F32); sg = A.alloc("sg", [128, CH], F32); oc = A.alloc("oc", [128, CH], F32)
            junk3 = A.alloc("junk3", [128, CH], BF16)
            ocb_r = A.ring("ocb", [128, CH], BF16, 2)
            for tl in range(16):
                for hh in range(2):
                    PE(lambda e, tl=tl, hh=hh: e.transpose(psB[:, 128 * hh:128 * hh + 128], ybf.ap[:, tl, 128 * hh:128 * hh + 128], ident.ap), r=[ybf.res, ident.res], w=[psres[6]], g=("yTt", l, b, tl))
                yT = yT_r.next()
                ACT(lambda e, yT=yT: e.copy(yT.ap.rearrange("p a b -> p (a b)"), psB[:, 0:256]), r=[psres[6]], w=[yT.res])
                for hh in range(2):
                    PE(lambda e, yT=yT, hh=hh: e.matmul(bankF(4), yT.ap[:, hh, :], wglu_sb.ap[:, hh, :], start=(hh == 0), stop=(hh == 1)), r=[yT.res, wglu_sb.res], w=[psres[4]], g=("glumm", l, b, tl))
                V(lambda e: e.tensor_tensor(glu.ap, bankF(4), bgluB.ap, ALU.add), r=[psres[4], bgluB.res], w=[glu.res])
                ACT(lambda e: e.activation(sg.ap, glu.ap[:, CH:2 * CH], AF.Sigmoid), r=[glu.res], w=[sg.res])
                V(lambda e: e.tensor_tensor(oc.ap, glu.ap[:, 0:CH], sg.ap, ALU.mult), r=[glu.res, sg.res], w=[oc.res])
                ACT(lambda e: e.activation(junk3.ap, oc.ap, AF.Square, accum_out=stat.ap[:, 0:1]), r=[oc.res], w=[junk3.res, statres[0]])
                ACT(lambda e: e.activation(stat.ap[:, 4:5], stat.ap[:, 0:1], AF.Sqrt, bias=EPS, scale=1.0 / CH), r=[statres[0]], w=[statres[1]])
                V(lambda e: e.reciprocal(stat.ap[:, 4:5], stat.ap[:, 4:5]), r=[statres[1]], w=[statres[1]])
                ocb = ocb_r.next()
                V(lambda e, ocb=ocb: e.scalar_tensor_tensor(ocb.ap, oc.ap, stat.ap[:, 4:5], mixB.ap[:, 768:1024], ALU.mult, ALU.mult), r=[oc.res, statres[1], mixB.res], w=[ocb.res])
                dma(mixedD[b, 128 * tl:128 * tl + 128, 768:1024], ocb.ap, reads=[ocb.res], writes=[dres[("mixedD", b)]], wgroup="mixw")
            A.reset(m2)

        def phase_out(b):
            m = A.mark()
            g1, G2, SH2 = modT[0], modT[1], modT[2]
            load_mod(g1, b, 2)
            load_mod(G2, b, 4, norm2B)
            load_mod(SH2, b, 3)
            wout_sb = A.alloc("wout_sb", [128, 8, D], BF16)
            dma(wout_sb.ap, wout_b.rearrange("(k p) n -> p k n", p=128), reads=[dres["wout_b"]], writes=[wout_sb.res])
            mx_r = A.ring("mxin", [128, D], BF16, 2)
            mT_r = A.ring("mT", [128, 8, 128], BF16, 2)
            xt_r = A.ring("xt6", [128, D], F32, 2)
            tmp_r = A.ring("tmp6", [128, D], F32, 2)
            xn_r = A.ring("xn", [128, D], F32, 2)
            junk = A.alloc("junk6", [128, D], BF16)
            hf_r = A.ring("hf6", [128, D], F32, 2)
            hb_r = A.ring("hb6", [128, D], BF16, 2)
            h2T_r = A.ring("h2Ts", [128, 8, 128], BF16, 2)
            def tile_gen(tl):
                tt = b * (S // 128) + tl
                t0 = tl * 128
                tmp = tmp_r.next(); hf = hf_r.next(); hb = hb_r.next()
                mx = mx_r.next(); xt = xt_r.next()
                dma(mx.ap, mixedD[b, t0:t0 + 128, :], reads=[dres[("mixedD", b)]], writes=[mx.res])
                dma(xt.ap, xin_ap[tt * 128:(tt + 1) * 128, :], reads=[xres_tt[tt]], writes=[xt.res])
                if tl == 0 and b == 0:
                    dbg(f"mixed{l}", mx.ap, [mx.res])
                for k in range(8):
                    PE(lambda e, k=k, mx=mx: e.transpose(psB[:, 128 * k:128 * (k + 1)], mx.ap[:, 128 * k:128 * (k + 1)], ident.ap), r=[mx.res, ident.res], w=[psres[6]], g=("mTt", l, tt))
                yield
                mT = mT_r.next()
                ACT(lambda e, mT=mT: e.copy(mT.ap.rearrange("p a b -> p (a b)"), psB[:, 0:1024]), r=[psres[6]], w=[mT.res])
                yield
                for cch in range(2):
                    for k in range(8):
                        PE(lambda e, k=k, cch=cch, mT=mT: e.matmul(bankF(cch), mT.ap[:, k, :], wout_sb.ap[:, k, 512 * cch:512 * cch + 512], start=(k == 0), stop=(k == 7)),
                           r=[mT.res, wout_sb.res], w=[psres[cch]], g=("omm", l, tt, cch))
                yield
                V(lambda e: e.tensor_tensor(tmp.ap, psF[:, 0:1024], g1.ap, ALU.mult), r=[psres[0], psres[1], g1.res], w=[tmp.res])
                xn = xn_r.next()
                V(lambda e, xn=xn, xt=xt: e.tensor_tensor(xn.ap, tmp.ap, xt.ap, ALU.add), r=[tmp.res, xt.res], w=[xn.res])
                dma(xmid[tt * 128:(tt + 1) * 128, :], xn.ap, reads=[xn.res], writes=[xmid_tt[tt]])
                yield
                ACT(lambda e, xn=xn: e.activation(junk.ap, xn.ap, AF.Square, accum_out=stat.ap[:, 0:1]), r=[xn.res], w=[junk.res, statres[0]])
                ACT(lambda e: e.activation(stat.ap[:, 4:5], stat.ap[:, 0:1], AF.Sqrt, bias=EPS, scale=1.0 / D), r=[statres[0]], w=[statres[1]])
                V(lambda e: e.reciprocal(stat.ap[:, 8:9], stat.ap[:, 4:5]), r=[statres[1]], w=[statres[2]])
                V(lambda e, xn=xn: e.scalar_tensor_tensor(hf.ap, xn.ap, stat.ap[:, 8:9], G2.ap, ALU.mult, ALU.mult), r=[xn.res, statres[2], G2.res], w=[hf.res])
                V(lambda e: e.tensor_tensor(hb.ap, hf.ap, SH2.ap, ALU.add), r=[hf.res, SH2.res], w=[hb.res])
                yield
                for k in range(8):
                    PE(lambda e, k=k: e.transpose(psB[:, 1024 + 128 * k:1024 + 128 * (k + 1)], hb.ap[:, 128 * k:128 * (k + 1)], ident.ap), r=[hb.res, ident.res], w=[psres[7]], g=("h2Tt", l, tt))
                yield
                h2T = h2T_r.next()
                ACT(lambda e, h2T=h2T: e.copy(h2T.ap.rearrange("p a b -> p (a b)"), psB[:, 1024:2048]), r=[psres[7]], w=[h2T.res])
                dma(h2TD[b, :, t0:t0 + 128].rearrange("(k p) t -> p k t", p=128), h2T.ap, reads=[h2T.res], writes=[dres[("h2TD", b)]], wgroup="h2w")
            active = []
            nxt = 0
            ntile = S // 128
            while active or nxt < ntile:
                if len(active) < 2 and nxt < ntile:
                    active.append(tile_gen(nxt))
                    nxt += 1
                for g_ in list(active):
                    try:
                        next(g_)
                    except StopIteration:
                        active.remove(g_)
            A.reset(m)
            P.barrier()

        def phase_ffn(b):
            m = A.mark()
            g2 = modT[0]
            load_mod(g2, b, 5)
            wdown_sb = A.alloc("wdown_sb", [128, 22, D], BF16)
            dma(wdown_sb.ap, wdown_b.rearrange("(f p) n -> p f n", p=128), reads=[dres["wdown_b"]], writes=[wdown_sb.res])
            hw_r = A.ring("hw", [128, 8, 514], BF16, 2)
            wup_r = A.ring("wup", [128, 2, 8, 128], BF16, 3)
            tv_r = A.ring("tv", [128, 512], F32, 3); tg_r = A.ring("tg", [128, 512], F32, 3)
            aT = A.alloc("aT", [128, 22, 512], BF16)
            xt_r = A.ring("xt7", [128, D], F32, 2)
            tmp = A.alloc("tmp7", [128, D], F32)
            xo_r = A.ring("xo", [128, D], F32, 2)
            zsel = 0
            pend = None

            def finish_pair(tv, tg, f, w_):
                ACT(lambda e, tg=tg: e.activation(tg.ap, tg.ap, AF.Silu), r=[tg.res], w=[tg.res])
                GP(lambda e, tv=tv, tg=tg, f=f: e.tensor_tensor(aT.ap[:, f, :], tv.ap, tg.ap, ALU.mult), r=[tv.res, tg.res], w=[aT.res], g=("aT", l, b, w_))

            for w_ in range(4):
                c0 = 512 * w_
                hw = hw_r.next()
                lo = max(c0 - 1, 0); hi = min(c0 + 513, S)
                j0 = lo - (c0 - 1)
                if j0 > 0:
                    GP(lambda e, hw=hw: e.memset(hw.ap[:, :, 0:1], 0.0), w=[hw.res], g=("hwl", l, b, w_))
                if hi < c0 + 513:
                    GP(lambda e, hw=hw: e.memset(hw.ap[:, :, 513:514], 0.0), w=[hw.res], g=("hwl", l, b, w_))
                dma(hw.ap[:, :, j0:j0 + (hi - lo)], h2TD[b, :, lo:hi].rearrange("(k p) t -> p k t", p=128), reads=[dres[("h2TD", b)]], writes=[hw.res], wgroup=("hwl", l, b, w_))
                for f in range(22):
                    wu = wup_r.next()
                    dma(wu.ap[:, 0, :, :].rearrange("p k n -> p (k n)"), wup_b[f], reads=[dres["wup_b"]], writes=[wu.res], wgroup=("wul", l, b, w_, f))
                    dma(wu.ap[:, 1, :, :].rearrange("p k n -> p (k n)"), wup_b[22 + f], reads=[dres["wup_b"]], writes=[wu.res], wgroup=("wul", l, b, w_, f))
                    zs = zsel % 2
                    zsel += 1
                    zb = [4 * zs, 4 * zs + 2]
                    zaps = []
                    for vg in range(2):
                        b0 = zb[vg]
                        zap = psF[:, 512 * b0:512 * b0 + 1024] if b0 < 6 else psB_f
                        zaps.append(zap)
                        for k in range(8):
                            PE(lambda e, k=k, vg=vg, zap=zap, wu=wu, hw=hw: e.matmul(zap[:, 0:512], wu.ap[:, vg, k, :], hw.ap[:, k, 0:512], start=(k == 0), stop=(k == 7)),
                               r=[wu.res, hw.res], w=[psres[b0]], g=("zmm", l, b, w_, f, vg, 0))
                        for k in range(8):
                            PE(lambda e, k=k, vg=vg, zap=zap, wu=wu, hw=hw: e.matmul(zap[:, 512:514], wu.ap[:, vg, k, :], hw.ap[:, k, 512:514], start=(k == 0), stop=(k == 7)),
                               r=[wu.res, hw.res], w=[psres[b0 + 1]], g=("zmm", l, b, w_, f, vg, 1))
                    tv = tv_r.next(); tg = tg_r.next()
                    for vg, tt_ in ((0, tv), (1, tg)):
                        fi = f + 22 * vg
                        zap = zaps[vg]
                        rs = [psres[zb[vg]], psres[zb[vg] + 1]]
                        ACT(lambda e, zap=zap, tt_=tt_, fi=fi: e.activation(tt_.ap, zap[:, 1:513], AF.Identity, bias=cb.ap[:, fi:fi + 1], scale=cw.ap[:, 1, fi:fi + 1]),
                            r=rs + [cw.res, cb.res], w=[tt_.res])
                        V(lambda e, zap=zap, tt_=tt_, fi=fi: e.scalar_tensor_tensor(tt_.ap, zap[:, 0:512], cw.ap[:, 0, fi:fi + 1], tt_.ap, ALU.mult, ALU.add), r=rs + [cw.res, tt_.res], w=[tt_.res])
                        V(lambda e, zap=zap, tt_=tt_, fi=fi: e.scalar_tensor_tensor(tt_.ap, zap[:, 2:514], cw.ap[:, 2, fi:fi + 1], tt_.ap, ALU.mult, ALU.add), r=rs + [cw.res, tt_.res], w=[tt_.res])
                    if w_ == 0 and b == 0 and f == 0:
                        dbg(f"zc{l}", tv.ap, [tv.res])
                    if pend is not None:
                        finish_pair(*pend)
                    pend = (tv, tg, f, w_)
                if pend is not None:
                    finish_pair(*pend)
                    pend = None
                for j in range(4):
                    tt = b * 16 + w_ * 4 + j
                    xt = xt_r.next()
                    dma(xt.ap, xmid[tt * 128:(tt + 1) * 128, :], reads=[xmid_tt[tt]], writes=[xt.res])
                    for cch in range(2):
                        for f in range(22):
                            PE(lambda e, f=f, cch=cch, j=j: e.matmul(bankF(cch), aT.ap[:, f, 128 * j:128 * j + 128], wdown_sb.ap[:, f, 512 * cch:512 * cch + 512], start=(f == 0), stop=(f == 21)),
                               r=[aT.res, wdown_sb.res], w=[psres[cch]], g=("dmm", l, tt, cch))
                    V(lambda e: e.tensor_tensor(tmp.ap, psF[:, 0:1024], g2.ap, ALU.mult), r=[psres[0], psres[1], g2.res], w=[tmp.res])
                    xo = xo_r.next()
                    V(lambda e, xo=xo, xt=xt: e.tensor_tensor(xo.ap, tmp.ap, xt.ap, ALU.add), r=[tmp.res, xt.res], w=[xo.res])
                    dma(out[tt * 128:(tt + 1) * 128, :], xo.ap, reads=[xo.res], writes=[xres_tt[tt]])
            A.reset(m)
            P.barrier()

        for b in range(NB):
            phase_proj(b)
            if stop_after == "proj":
                break
            phase_attn(b, "mla")
            phase_attn(b, "dil")
        if stop_after == "proj":
            break
        phase_s5()
        for b in range(NB):
            phase_out(b)
        for b in range(NB):
            phase_ffn(b)
        A.reset(m_layer)

    P.emit(st)
    st.close()
    return nc, P, A


def host_constants():
    ident = np.eye(128, dtype=np.float32).astype(ml_dtypes.bfloat16)
    k = np.arange(128)[:, None]
    xx = np.arange(MASKW)[None, :]
    d = k - xx + 1920
    w = (np.abs(d) <= 64).astype(np.float32) + ((d % 4 == 0) & (np.abs(d) <= 256)) + ((d % 16 == 0) & (np.abs(d) <= 1024))
    mask = w.astype(ml_dtypes.bfloat16)
    misc = np.zeros((128, 256), np.float32)
    misc[:, 0:16] = 128.0 * np.arange(16)[None, :]
    misc[:, 16:144] = np.arange(128)[None, :]
    for j in range(8):
        misc[16 * j:16 * j + 16, 144 + j] = 1.0
    misc[:, 160:176] = (1.0 / (10000.0 ** (np.arange(0, 32, 2, dtype=np.float32) / 32.0))).astype(np.float32)[None, :]
    misc[:, 192:224] = (1.0 / (10000.0 ** (np.arange(0, 64, 2, dtype=np.float32) / 64.0))).astype(np.float32)[None, :]
    iota = np.tile(np.arange(1024, dtype=np.float32)[None, :], (128, 1))
    return ident, mask, misc, iota


_CACHE = {}


def kernel(**inputs):
    n = 8
    if "nc" not in _CACHE:
        _CACHE["nc"] = build_program()[0]
    nc = _CACHE["nc"]
    ident, mask, misc, iota = host_constants()
    x = np.ascontiguousarray(np.asarray(inputs["x"], dtype=np.float32))
    c = np.asarray(inputs["c"], dtype=np.float32)
    pos = np.asarray(inputs["positions"], dtype=np.int32)
    shared = {k: np.ascontiguousarray(np.asarray(v)) for k, v in inputs.items() if k not in ("x", "c", "positions")}
    shared.update({"k_ident": ident, "k_mask": mask, "k_misc": misc, "k_iota": iota})
    in_maps = []
    for i in range(n):
        mp = dict(shared)
        mp["x"] = x[NB * i:NB * (i + 1)].reshape(T, D)
        mp["c"] = np.ascontiguousarray(c[NB * i:NB * (i + 1)])
        mp["positions"] = np.ascontiguousarray(pos[NB * i:NB * (i + 1)])
        in_maps.append(mp)
    res = run_bass_kernel_spmd(nc, in_maps, core_ids=list(range(n)))
    outs = [np.asarray(r["out"]).reshape(NB, S, D) for r in res.results]
    return np.concatenate(outs, axis=0).astype(np.float32)
```

```python
import math
from contextlib import ExitStack

import ml_dtypes
import numpy as np
import concourse.bass as bass
import concourse.mybir as mybir
from concourse.bass_utils import run_bass_kernel_spmd

F32 = mybir.dt.float32
BF16 = mybir.dt.bfloat16
I32 = mybir.dt.int32
AF = mybir.ActivationFunctionType
ALU = mybir.AluOpType
AX = mybir.AxisListType

ENGS = ("pe", "act", "dve", "pool", "sp")
EPOCH = 20000
NDMASEM = 14


class Res:
    __slots__ = ("name", "writers", "readers", "wgroup")

    def __init__(self, name):
        self.name = name
        self.writers = []
        self.readers = []
        self.wgroup = None


class Op:
    __slots__ = ("idx", "eng", "fn", "dma", "deps", "lidx", "waits", "signal", "needed", "slot", "prev_slot_op")

    def __init__(self):
        self.waits = []
        self.signal = None
        self.needed = False


class Prog:
    def __init__(self, nc):
        self.nc = nc
        self.ops = []
        self.eng_ops = {e: [] for e in ENGS}
        self.dma_rr = {e: 0 for e in ENGS}
        self.dma_slot_last = {}
        self.nres = 0
        self.cur_barrier = set()

    def res(self, name=None):
        self.nres += 1
        return Res(name or f"r{self.nres}")

    def barrier(self):
        tails = set()
        for e in ENGS:
            if self.eng_ops[e]:
                tails.add(self.eng_ops[e][-1].idx)
        for o in self.dma_slot_last.values():
            tails.add(o.idx)
        self.cur_barrier = tails

    def op(self, eng, fn, reads=(), writes=(), dma=False, wgroup=None):
        o = Op()
        o.idx = len(self.ops)
        o.eng = eng
        o.fn = fn
        o.dma = dma
        deps = set(self.cur_barrier)
        for r in reads:
            deps.update(r.writers)
        for w in writes:
            if not (wgroup is not None and w.wgroup == wgroup):
                deps.update(w.writers)
            deps.update(w.readers)
        o.deps = deps
        o.lidx = len(self.eng_ops[eng])
        o.slot = None
        o.prev_slot_op = None
        if dma:
            s = self.dma_rr[eng]
            self.dma_rr[eng] = (s + 1) % NDMASEM
            o.slot = s
            o.prev_slot_op = self.dma_slot_last.get((eng, s))
            self.dma_slot_last[(eng, s)] = o
        self.ops.append(o)
        self.eng_ops[eng].append(o)
        for r in reads:
            r.readers.append(o.idx)
        for w in writes:
            if wgroup is not None and w.wgroup == wgroup:
                w.writers.append(o.idx)
            else:
                w.writers = [o.idx]
                w.readers = []
                w.wgroup = wgroup
        return o

    def finalize(self):
        ops = self.ops
        tails = set()
        for e in ENGS:
            if self.eng_ops[e]:
                tails.add(self.eng_ops[e][-1].idx)
        for o in self.dma_slot_last.values():
            tails.add(o.idx)
        fin = Op()
        fin.idx = len(ops)
        fin.eng = "sp"
        fin.fn = None
        fin.dma = False
        fin.deps = tails
        fin.lidx = len(self.eng_ops["sp"])
        fin.slot = None
        fin.prev_slot_op = None
        ops.append(fin)
        self.eng_ops["sp"].append(fin)

        seen = {e: {f: -1 for f in ENGS} for e in ENGS}
        seen_dma = {e: set() for e in ENGS}
        for o in ops:
            e = o.eng
            dl = []
            if o.dma and o.prev_slot_op is not None:
                dl.append(o.prev_slot_op)
            for d in sorted(o.deps):
                dl.append(ops[d])
            for d in dl:
                if d.dma:
                    if d.idx in seen_dma[e]:
                        continue
                    seen_dma[e].add(d.idx)
                    d.needed = True
                    o.waits.append(d)
                else:
                    f = d.eng
                    if f == e:
                        if e in ("pe", "sp"):
                            continue
                        if d.lidx < o.lidx - 3:
                            continue
                    if d.lidx <= seen[e][f]:
                        continue
                    seen[e][f] = d.lidx
                    d.needed = True
                    o.waits.append(d)
        cnt = {e: 0 for e in ENGS}
        dma_tot = {}
        for o in ops:
            if o.dma:
                k = (o.eng, o.slot)
                dma_tot[k] = dma_tot.get(k, 0) + 16
                o.signal = ("d", o.eng, o.slot, dma_tot[k])
            elif o.needed:
                c = cnt[o.eng]
                cnt[o.eng] = c + 1
                o.signal = ("c", o.eng, c // EPOCH, c % EPOCH + 1)
        self.n_epochs = {e: max(1, (cnt[e] + EPOCH - 1) // EPOCH) for e in ENGS}

    def emit(self, stack):
        nc = self.nc
        self.finalize()
        sems = {}
        for e in ENGS:
            if e == "sp":
                continue
            for ep in range(self.n_epochs[e]):
                sems[("c", e, ep)] = stack.enter_context(nc.semaphore(f"s_{e}_{ep}"))
        for e in ENGS:
            if any(k[0] == e for k in self.dma_slot_last):
                for s in range(NDMASEM):
                    sems[("d", e, s)] = stack.enter_context(nc.semaphore(f"d_{e}_{s}"))
        block = stack.enter_context(nc.Block())

        def run(engname):
            def body(eng):
                for o in self.eng_ops[engname]:
                    for d in o.waits:
                        sg = d.signal
                        eng.wait_ge(sems[sg[:3]], sg[3])
                    if o.fn is None:
                        continue
                    inst = o.fn(eng)
                    if o.signal is not None:
                        sg = o.signal
                        inst.then_inc(sems[sg[:3]], 16 if sg[0] == "d" else 1)
            return body

        block.tensor(run("pe"))
        block.scalar(run("act"))
        block.vector(run("dve"))
        block.gpsimd(run("pool"))
        block.sync(run("sp"))


D = 1024
S = 2048
NB = 2
T = NB * S
NTT = T // 128
DEPTH = 4
HM, NOPE, ROPE_M, VM, QKM = 6, 64, 32, 64, 96
QR, KVR = 192, 128
HD, DD = 6, 64
G, GC, NST = 16, 16, 64
CH = 256
INW = 1760
FF = 2816
EPS = 1e-6
OFF_CQ, OFF_CKV, OFF_KR, OFF_DQ, OFF_DK, OFF_DV, OFF_UC = 0, 192, 320, 352, 736, 1120, 1504
MASKW = 3968
TWO_PI_S = 6.2831845
INV2PI = 1.0 / (2.0 * math.pi)
HALF_PI_S = 1.5707960
MAGIC = 12582912.0


class Tile:
    __slots__ = ("ap", "res")

    def __init__(self, ap, res):
        self.ap = ap
        self.res = res


class Arena:
    def __init__(self, P, arena_ap, words):
        self.P = P
        self.a = arena_ap
        self.words = words
        self.top = 0
        self.peak = 0

    def mark(self):
        return self.top

    def reset(self, m):
        self.top = m

    def alloc(self, name, shape, dtype):
        n = 1
        for s in shape[1:]:
            n *= s
        nw = n if dtype in (F32, I32) else (n + 1) // 2
        nw = (nw + 7) // 8 * 8
        assert self.top + nw <= self.words, f"SBUF arena overflow at {name}: {self.top}+{nw}>{self.words}"
        ap = self.a[0:shape[0], self.top:self.top + nw]
        self.top += nw
        self.peak = max(self.peak, self.top)
        if dtype != F32:
            ap = ap.bitcast(dtype)
        ap = ap[:, 0:n]
        if len(shape) == 3:
            ap = ap.rearrange("p (a b) -> p a b", a=shape[1])
        elif len(shape) == 4:
            ap = ap.rearrange("p (a b c) -> p a b c", a=shape[1], b=shape[2])
        return Tile(ap, self.P.res(name))

    def ring(self, name, shape, dtype, n):
        return Ring([self.alloc(f"{name}{i}", shape, dtype) for i in range(n)])


class Ring:
    def __init__(self, tiles):
        self.tiles = tiles
        self.i = 0

    def next(self):
        t = self.tiles[self.i % len(self.tiles)]
        self.i += 1
        return t


def build_program(depth=DEPTH, debug=None, stop_after=None):
    debug = debug or {}
    nc = bass.Bass("TRN2", target_bir_lowering=False)
    dt_in = lambda name, shape, dt=F32: nc.dram_tensor(name, list(shape), dt, kind="ExternalInput").ap()
    dt_scr = lambda name, shape, dt: nc.dram_tensor(name, list(shape), dt, kind="Internal").ap()
    L = DEPTH
    x_in = dt_in("x", [T, D])
    c_in = dt_in("c", [NB, D])
    pos_in = dt_in("positions", [NB, S], I32)
    w_mod = dt_in("w_mod", [L, D, 6 * D]); b_mod = dt_in("b_mod", [L, 6 * D])
    norm1 = dt_in("norm1", [L, D]); w_in = dt_in("w_in", [L, D, INW])
    mla_q_norm = dt_in("mla_q_norm", [L, QR]); mla_w_uq = dt_in("mla_w_uq", [L, QR, HM * QKM])
    mla_kv_norm = dt_in("mla_kv_norm", [L, KVR]); mla_w_ukv = dt_in("mla_w_ukv", [L, KVR, HM * 128])
    mla_qk_gain = dt_in("mla_qk_gain", [L, 2, QKM]); dil_qk_gain = dt_in("dil_qk_gain", [L, 2, DD])
    ssm_a_re = dt_in("ssm_a_re", [L, 2, G, NST]); ssm_a_im = dt_in("ssm_a_im", [L, 2, G, NST])
    ssm_log_dt = dt_in("ssm_log_dt", [L, 2, G])
    ssm_b_re = dt_in("ssm_b_re", [L, 2, G, NST, GC]); ssm_b_im = dt_in("ssm_b_im", [L, 2, G, NST, GC])
    ssm_c_re = dt_in("ssm_c_re", [L, 2, G, GC, NST]); ssm_c_im = dt_in("ssm_c_im", [L, 2, G, GC, NST])
    ssm_d = dt_in("ssm_d", [L, CH]); ssm_w_glu = dt_in("ssm_w_glu", [L, CH, 2 * CH]); ssm_b_glu = dt_in("ssm_b_glu", [L, 2 * CH])
    mix_norm = dt_in("mix_norm", [L, D]); w_out = dt_in("w_out", [L, D, D]); norm2 = dt_in("norm2", [L, D])
    ffn_w_up = dt_in("ffn_w_up", [L, D, 2 * FF]); ffn_conv_w = dt_in("ffn_conv_w", [L, 3, 2 * FF])
    ffn_conv_b = dt_in("ffn_conv_b", [L, 2 * FF]); ffn_w_down = dt_in("ffn_w_down", [L, FF, D])
    k_ident = dt_in("k_ident", [128, 128], BF16)
    k_mask = dt_in("k_mask", [128, MASKW], BF16)
    k_misc = dt_in("k_misc", [128, 256])
    k_iota = dt_in("k_iota", [128, 1024])
    out = nc.dram_tensor("out", [T, D], F32, kind="ExternalOutput").ap()

    xmid = dt_scr("xmid", [T, D], F32)
    modD = dt_scr("modD", [NB, 6 * D], F32)
    qmT = dt_scr("qmT", [NB, HM, QKM, S], BF16); kmT = dt_scr("kmT", [NB, HM, QKM, S], BF16)
    vmD = dt_scr("vmD", [NB, S, HM * 65], BF16)
    qdT = dt_scr("qdT", [NB, 3, 128, S], BF16); kdT = dt_scr("kdT", [NB, 3, 128, S], BF16)
    vdD = dt_scr("vdD", [NB, S, HD * 65], BF16)
    ucT = dt_scr("ucT", [NB, CH, S], BF16)
    uctok = dt_scr("uctok", [NB, S, CH], F32)
    mixedD = dt_scr("mixedD", [NB, S, D], BF16)
    h2TD = dt_scr("h2TD", [NB, D, S], BF16)
    win_b = dt_scr("win_b", [D, INW], BF16); wuq_b = dt_scr("wuq_b", [QR, HM * QKM], BF16)
    wukv_b = dt_scr("wukv_b", [KVR, HM * 128], BF16); wglu_b = dt_scr("wglu_b", [CH, 2 * CH], BF16)
    wout_b = dt_scr("wout_b", [D, D], BF16); wup_b = dt_scr("wup_b", [44, 128, 1024], BF16)
    wdown_b = dt_scr("wdown_b", [FF, D], BF16)
    dbg_out = {k: nc.dram_tensor("dbg_" + k, list(shp), dt_, kind="ExternalOutput").ap() for k, (shp, dt_) in debug.items()}

    st = ExitStack()
    AW = 53000
    arena_t = st.enter_context(nc.sbuf_tensor("arena", [128, AW], F32))
    psF = st.enter_context(nc.psum_tensor("psF", [128, 3072], F32))
    psBt = st.enter_context(nc.psum_tensor("psB", [128, 2048], BF16))
    P = Prog(nc)
    A = Arena(P, arena_t, AW)
    psres = [P.res(f"psum{i}") for i in range(8)]
    psB = psBt[:, :]
    psB_f = psBt[:, :].bitcast(F32)

    def bankF(i):
        if i < 6:
            return psF[:, 512 * i:512 * (i + 1)]
        return psB_f[:, 512 * (i - 6):512 * (i - 5)]

    dres = {k: P.res("D_" + k) for k in ["xin", "xmid", "out", "modD", "qmT", "kmT", "vmD", "qdT", "kdT", "vdD", "ucT",
                                          "uctok", "mixedD", "h2TD", "win_b", "wuq_b", "wukv_b", "wglu_b", "wout_b",
                                          "wup_b", "wdown_b", "dbg"]}
    for k in ["qmT", "kmT", "vmD", "qdT", "kdT", "vdD", "ucT", "uctok", "mixedD", "h2TD"]:
        for b in range(NB):
            dres[(k, b)] = P.res(f"D_{k}_{b}")
    xres_tt = [P.res(f"D_x_{i}") for i in range(NTT)]
    xmid_tt = [P.res(f"D_xm_{i}") for i in range(NTT)]

    def dma(out_ap, in_ap, reads=(), writes=(), wgroup=None, eng="sp", ncont=False):
        if ncont:
            f = lambda e: e.dma_start(out=out_ap, in_=in_ap, allow_slow_non_contiguous=True)
        else:
            f = lambda e: e.dma_start(out=out_ap, in_=in_ap)
        return P.op(eng, f, reads=reads, writes=writes, dma=True, wgroup=wgroup)

    def dbg(name, ap, res_list):
        if name in dbg_out:
            dma(dbg_out[name], ap, reads=res_list, writes=[dres["dbg"]], wgroup="dbg")

    V = lambda fn, r=(), w=(), g=None: P.op("dve", fn, reads=r, writes=w, wgroup=g)
    ACT = lambda fn, r=(), w=(), g=None: P.op("act", fn, reads=r, writes=w, wgroup=g)
    GP = lambda fn, r=(), w=(), g=None: P.op("pool", fn, reads=r, writes=w, wgroup=g)
    PE = lambda fn, r=(), w=(), g=None: P.op("pe", fn, reads=r, writes=w, wgroup=g)

    ident = A.alloc("ident", [128, 128], BF16)
    maskS = A.alloc("maskS", [128, MASKW], BF16)
    misc = A.alloc("misc", [128, 256], F32)
    dma(ident.ap, k_ident, writes=[ident.res])
    dma(maskS.ap, k_mask, writes=[maskS.res])
    dma(misc.ap, k_misc, writes=[misc.res])
    iota = A.alloc("iota", [128, 1024], F32)
    dma(iota.ap, k_iota, writes=[iota.res])
    cosM = A.alloc("cosM", [128, NTT, 16], F32); sinM = A.alloc("sinM", [128, NTT, 16], F32)
    cosD = A.alloc("cosD", [128, NTT, 32], F32); sinD = A.alloc("sinD", [128, NTT, 32], F32)
    cT = A.alloc("cT", [128, NB, 8], F32)

    def rsqrt_to(dst_tile, src_ap, scale, n_reads, tmp_tile):
        ACT(lambda e: e.activation(tmp_tile.ap, src_ap, AF.Sqrt, bias=EPS, scale=scale), r=n_reads, w=[tmp_tile.res])
        V(lambda e: e.reciprocal(dst_tile.ap, tmp_tile.ap), r=[tmp_tile.res], w=[dst_tile.res])

    def sincos_scratch(n):
        return (A.alloc("yi", [128, n], I32), A.alloc("yf", [128, n], F32), A.alloc("yc", [128, n], F32))

    def sincos(dst_sin, dst_cos, y_tile, n, scr):
        yi = Tile(scr[0].ap[:, 0:n], scr[0].res); yf = Tile(scr[1].ap[:, 0:n], scr[1].res); yc = Tile(scr[2].ap[:, 0:n], scr[2].res)
        yflat = y_tile.ap
        for dst, off in ((dst_sin, 0.0), (dst_cos, 0.25)):
            if off != 0.0:
                V(lambda e, off=off: e.tensor_scalar(yc.ap, yflat, off, None, ALU.add), r=[y_tile.res], w=[yc.res])
                src = yc
            else:
                src = y_tile
            V(lambda e, src=src: e.tensor_scalar(yf.ap, src.ap, MAGIC, MAGIC, ALU.add, ALU.subtract), r=[src.res], w=[yf.res])
            V(lambda e, src=src: e.tensor_sub(yf.ap, src.ap, yf.ap), r=[src.res, yf.res], w=[yf.res])
            ACT(lambda e, dst=dst: e.activation(dst.ap, yf.ap, AF.Sin, bias=0.0, scale=TWO_PI_S), r=[yf.res], w=[dst.res])

    m_init = A.mark()
    posi = A.alloc("posi", [128, NTT], I32); posf = A.alloc("posf", [128, NTT], F32)
    dma(posi.ap, pos_in.rearrange("b (t p) -> p (b t)", p=128), writes=[posi.res], ncont=True)
    V(lambda e: e.tensor_copy(posf.ap, posi.ap), r=[posi.res], w=[posf.res])
    angM = A.alloc("angM", [128, NTT * 16], F32); angD = A.alloc("angD", [128, NTT * 32], F32)
    for j in range(NTT):
        V(lambda e, j=j: e.tensor_scalar(angM.ap[:, 16 * j:16 * j + 16], misc.ap[:, 160:176], posf.ap[:, j:j + 1], INV2PI, ALU.mult, ALU.mult),
          r=[misc.res, posf.res], w=[angM.res], g="angM")
        V(lambda e, j=j: e.tensor_scalar(angD.ap[:, 32 * j:32 * j + 32], misc.ap[:, 192:224], posf.ap[:, j:j + 1], INV2PI, ALU.mult, ALU.mult),
          r=[misc.res, posf.res], w=[angD.res], g="angD")
    sM = Tile(sinM.ap.rearrange("p a b -> p (a b)"), sinM.res); cM = Tile(cosM.ap.rearrange("p a b -> p (a b)"), cosM.res)
    sDt = Tile(sinD.ap.rearrange("p a b -> p (a b)"), sinD.res); cDt = Tile(cosD.ap.rearrange("p a b -> p (a b)"), cosD.res)
    scr0 = sincos_scratch(NTT * 32)
    sincos(sM, cM, angM, NTT * 16, scr0)
    sincos(sDt, cDt, angD, NTT * 32, scr0)
    dbg("posf", posf.ap, [posf.res]); dbg("angM", angM.ap, [angM.res]); dbg("sinM", sM.ap, [sinM.res]); dbg("cosM", cM.ap, [cosM.res])
    craw = A.alloc("craw", [128, NB, 8], F32)
    for b_ in range(NB):
        dma(craw.ap[:, b_, :], c_in[b_].rearrange("(k p) -> p k", p=128), writes=[craw.res], ncont=True, wgroup="craw")
    ACT(lambda e: e.activation(cT.ap, craw.ap, AF.Silu), r=[craw.res], w=[cT.res])
    P.barrier()
    A.reset(m_init)

    norm1B = A.alloc("norm1B", [128, D], F32); norm2B = A.alloc("norm2B", [128, D], F32)
    mixB = A.alloc("mixB", [128, D], F32)
    qgB = A.alloc("qgB", [128, QR], F32); kvgB = A.alloc("kvgB", [128, KVR], F32)
    gM = A.alloc("gM", [128, 2, QKM], F32); gD = A.alloc("gD", [128, 2, DD], F32)
    dB = A.alloc("dB", [128, CH], F32); bgluB = A.alloc("bgluB", [128, 2 * CH], F32)
    cw = A.alloc("cw", [128, 3, 44], F32); cb = A.alloc("cb", [128, 44], F32)
    modT = [A.alloc(f"modT{i}", [128, D], F32) for i in range(3)]
    stat = A.alloc("stat", [128, 64], F32)
    statres = [P.res(f"stat{i}") for i in range(16)]
    m_layer = A.mark()

    bc = lambda ap1d: ap1d.partition_broadcast(128)

    for l in range(depth):
        xin_ap = x_in if l == 0 else out
        last = (l == depth - 1)
        dma(norm1B.ap, bc(norm1[l]), writes=[norm1B.res]); dma(norm2B.ap, bc(norm2[l]), writes=[norm2B.res])
        dma(mixB.ap, bc(mix_norm[l]), writes=[mixB.res])
        dma(qgB.ap, bc(mla_q_norm[l]), writes=[qgB.res]); dma(kvgB.ap, bc(mla_kv_norm[l]), writes=[kvgB.res])
        dma(gM.ap.rearrange("p a b -> p (a b)"), bc(mla_qk_gain[l].rearrange("a b -> (a b)")), writes=[gM.res])
        dma(gD.ap.rearrange("p a b -> p (a b)"), bc(dil_qk_gain[l].rearrange("a b -> (a b)")), writes=[gD.res])
        dma(dB.ap, bc(ssm_d[l]), writes=[dB.res]); dma(bgluB.ap, bc(ssm_b_glu[l]), writes=[bgluB.res])
        dma(cw.ap, ffn_conv_w[l].rearrange("j (f p) -> p j f", p=128), writes=[cw.res], ncont=True)
        dma(cb.ap, ffn_conv_b[l].rearrange("(f p) -> p f", p=128), writes=[cb.res], ncont=True)

        m = A.mark()
        CHK = 4096
        stg_f = A.ring("stgf", [128, CHK], F32, 2); stg_b = A.ring("stgb", [128, CHK], BF16, 2)
        ci = 0
        for (src, dst, key, rows, cols) in ((w_in[l], win_b, "win_b", D, INW), (mla_w_uq[l], wuq_b, "wuq_b", QR, HM * QKM),
                                            (mla_w_ukv[l], wukv_b, "wukv_b", KVR, HM * 128), (ssm_w_glu[l], wglu_b, "wglu_b", CH, 2 * CH),
                                            (w_out[l], wout_b, "wout_b", D, D),
                                            (ffn_w_down[l], wdown_b, "wdown_b", FF, D)):
            n = rows * cols
            per = n // 128
            assert per * 128 == n
            sflat = src.rearrange("r c -> (r c)").rearrange("(p x) -> p x", p=128)
            dflat = dst.rearrange("r c -> (r c)").rearrange("(p x) -> p x", p=128)
            o0 = 0
            while o0 < per:
                w_ = min(CHK, per - o0)
                sf = stg_f.next(); sb = stg_b.next()
                dma(sf.ap[:, 0:w_], sflat[:, o0:o0 + w_], writes=[sf.res])
                eng = ("dve", "act")[ci % 2]
                ci += 1
                if eng == "act":
                    ACT(lambda e, sf=sf, sb=sb, w_=w_: e.copy(sb.ap[:, 0:w_], sf.ap[:, 0:w_]), r=[sf.res], w=[sb.res])
                else:
                    P.op(eng, lambda e, sf=sf, sb=sb, w_=w_: e.tensor_copy(sb.ap[:, 0:w_], sf.ap[:, 0:w_]), reads=[sf.res], writes=[sb.res])
                dma(dflat[:, o0:o0 + w_], sb.ap[:, 0:w_], reads=[sb.res], writes=[dres[key]], wgroup="wcast", eng="pool")
                o0 += w_
        for c in range(11):
            sf = stg_f.next(); sb = stg_b.next()
            dma(sf.ap.rearrange("p (k n) -> p k n", k=8), ffn_w_up[l][:, 512 * c:512 * c + 512].rearrange("(k p) n -> p k n", p=128), writes=[sf.res])
            src4 = sf.ap.rearrange("p (k f n) -> p k f n", k=8, f=4)
            dst4 = sb.ap.rearrange("p (f k n) -> p k f n", f=4, k=8)
            eng = ("dve", "act")[ci % 2]
            ci += 1
            if eng == "act":
                ACT(lambda e, src4=src4, dst4=dst4: e.copy(dst4, src4), r=[sf.res], w=[sb.res])
            else:
                P.op(eng, lambda e, src4=src4, dst4=dst4: e.tensor_copy(dst4, src4), reads=[sf.res], writes=[sb.res])
            dma(wup_b[4 * c:4 * c + 4].rearrange("f p x -> p f x"), sb.ap.rearrange("p (f x) -> p f x", f=4), reads=[sb.res], writes=[dres["wup_b"]], wgroup="wcast", eng="pool")
        P.barrier()
        A.reset(m)

        m = A.mark()
        wm = A.ring("wm", [128, 8, 512], F32, 2)
        modS = A.alloc("modS", [NB, 6 * D], F32); bmS = A.alloc("bmS", [NB, 6 * D], F32)
        dma(bmS.ap, b_mod[l].partition_broadcast(NB), writes=[bmS.res])
        for cc in range(12):
            wt = wm.next()
            dma(wt.ap, w_mod[l][:, 512 * cc:512 * (cc + 1)].rearrange("(k p) n -> p k n", p=128), writes=[wt.res], eng=("sp" if cc % 2 == 0 else "pool"))
            bk = cc % 2
            for k in range(8):
                PE(lambda e, k=k, wt=wt, bk=bk: e.matmul(bankF(bk)[0:NB, :], cT.ap[:, :, k], wt.ap[:, k, :], start=(k == 0), stop=(k == 7)),
                   r=[cT.res, wt.res], w=[psres[bk]], g=("modmm", l, cc))
            V(lambda e, cc=cc, bk=bk: e.tensor_add(modS.ap[:, 512 * cc:512 * (cc + 1)], bankF(bk)[0:NB, :], bmS.ap[:, 512 * cc:512 * (cc + 1)]),
              r=[psres[bk], bmS.res], w=[modS.res], g="modS")
        dma(modD, modS.ap, reads=[modS.res], writes=[dres["modD"]])
        dbg(f"mod{l}", modS.ap, [modS.res])
        dbg("cT", cT.ap.rearrange("p a b -> p (a b)"), [cT.res])
        A.reset(m)
        P.barrier()
        if stop_after == "mod":
            break

        def load_mod(tile, b, idx, plus_one_times=None):
            dma(tile.ap, modD[b, idx * D:(idx + 1) * D].partition_broadcast(128), reads=[dres["modD"]], writes=[tile.res])
            if plus_one_times is not None:
                V(lambda e: e.scalar_tensor_tensor(tile.ap, tile.ap, 1.0, plus_one_times.ap, ALU.add, ALU.mult),
                  r=[tile.res, plus_one_times.res], w=[tile.res])

        def phase_proj(b):
            m = A.mark()
            G1, SH1 = modT[0], modT[1]
            load_mod(G1, b, 1, norm1B)
            load_mod(SH1, b, 0)
            win_sb = A.alloc("win_sb", [128, 8, INW], BF16)
            wuq_sb = A.alloc("wuq_sb", [128, 2, HM * QKM], BF16)
            wukv_sb = A.alloc("wukv_sb", [128, HM * 128], BF16)
            dma(win_sb.ap, win_b.rearrange("(k p) n -> p k n", p=128), reads=[dres["win_b"]], writes=[win_sb.res])
            dma(wuq_sb.ap[:, 0, :], wuq_b[0:128, :], reads=[dres["wuq_b"]], writes=[wuq_sb.res], wgroup="wuq")
            dma(wuq_sb.ap[0:64, 1, :], wuq_b[128:192, :], reads=[dres["wuq_b"]], writes=[wuq_sb.res], wgroup="wuq")
            dma(wukv_sb.ap, wukv_b, reads=[dres["wukv_b"]], writes=[wukv_sb.res])
            xt_r = A.ring("xt", [128, D], F32, 2)
            junk = A.alloc("junk", [128, D], BF16)
            hf_r = A.ring("hf", [128, D], F32, 2)
            hb_r = A.ring("hb", [128, D], BF16, 2)
            hT_r = A.ring("hT", [128, 8, 128], BF16, 2)
            u_sb_r = A.ring("u_sb", [128, INW], F32, 2)
            sq_r = A.ring("sq", [128, 768], F32, 2)
            cn_r = A.ring("cn", [128, 320], BF16, 2)
            cTs_r = A.ring("cTs", [128, 3, 128], BF16, 2)
            q_sb_r = A.ring("q_sb", [128, HM, QKM], F32, 2)
            kv_sb_r = A.ring("kv_sb", [128, HM, 128], F32, 2)
            qn_r = A.ring("qn", [128, HM, QKM], F32, 2)
            qbf_r = A.ring("qbf", [128, HM, QKM], BF16, 2)
            kbf_r = A.ring("kbf", [128, HM, QKM], BF16, 2)
            kr_r = A.ring("kr", [128, 4, 32], F32, 2)
            vbf_r = A.ring("vbf", [128, HM, 65], BF16, 2)
            vdbf_r = A.ring("vdbf", [128, HD, 65], BF16, 2)
            for t_ in vbf_r.tiles + vdbf_r.tiles:
                GP(lambda e, t_=t_: e.memset(t_.ap, 1.0), w=[t_.res])
            qkT_r = A.ring("qkT", [128, 2, HM, 128], BF16, 2)
            dn_r = A.ring("dn", [128, 2, HD, DD], F32, 2)
            dbf_r = A.ring("dbf", [128, 2, HD, DD], BF16, 2)
            rt_r = A.ring("rt", [128, 2, HD, 32], F32, 2)
            rt2_r = A.ring("rt2", [128, 2, HD, 32], F32, 2)
            dT_r = A.ring("dT", [128, 2, 3, 128], BF16, 2)
            ucb_r = A.ring("ucb", [128, CH], BF16, 2)
            ucT_r = A.ring("ucTs", [128, 2, 128], BF16, 2)
            S_ = lambda i, n=1: stat.ap[:, 4 * i:4 * i + n]

            def tile_gen(tl):
                tt = b * (S // 128) + tl
                t0 = tl * 128
                hf = hf_r.next(); hb = hb_r.next(); u_sb = u_sb_r.next(); sq = sq_r.next(); cn = cn_r.next(); cTs = cTs_r.next()
                q_sb = q_sb_r.next(); kv_sb = kv_sb_r.next(); qn = qn_r.next(); qbf = qbf_r.next(); kbf = kbf_r.next(); kr = kr_r.next()
                dn = dn_r.next(); dbf = dbf_r.next(); rt = rt_r.next(); rt2 = rt2_r.next(); ucb = ucb_r.next()
                xt = xt_r.next()
                dma(xt.ap, xin_ap[tt * 128:(tt + 1) * 128, :], reads=[xres_tt[tt]], writes=[xt.res])
                ACT(lambda e, xt=xt: e.activation(junk.ap, xt.ap, AF.Square, accum_out=S_(0)), r=[xt.res], w=[junk.res, statres[0]])
                ACT(lambda e: e.activation(S_(1), S_(0), AF.Sqrt, bias=EPS, scale=1.0 / D), r=[statres[0]], w=[statres[1]])
                V(lambda e: e.reciprocal(S_(2), S_(1)), r=[statres[1]], w=[statres[2]])
                V(lambda e, xt=xt: e.scalar_tensor_tensor(hf.ap, xt.ap, S_(2), G1.ap, ALU.mult, ALU.mult), r=[xt.res, statres[2], G1.res], w=[hf.res])
                V(lambda e: e.tensor_tensor(hb.ap, hf.ap, SH1.ap, ALU.add), r=[hf.res, SH1.res], w=[hb.res])
                if tl == 0 and b == 0:
                    dbg(f"h{l}", hb.ap, [hb.res])
                yield
                for k in range(8):
                    PE(lambda e, k=k: e.transpose(psB[:, 128 * k:128 * (k + 1)], hb.ap[:, 128 * k:128 * (k + 1)], ident.ap),
                       r=[hb.res, ident.res], w=[psres[6]], g=("hTt", l, tt))
                hT = hT_r.next()
                ACT(lambda e, hT=hT: e.copy(hT.ap.rearrange("p a b -> p (a b)"), psB[:, 0:1024]), r=[psres[6]], w=[hT.res])
                for ci_, (c0, c1) in enumerate(((0, 512), (512, 1024), (1024, 1536), (1536, INW))):
                    for k in range(8):
                        PE(lambda e, k=k, c0=c0, c1=c1, hT=hT: e.matmul(psF[:, c0:c1], hT.ap[:, k, :], win_sb.ap[:, k, c0:c1], start=(k == 0), stop=(k == 7)),
                           r=[hT.res, win_sb.res], w=[psres[ci_]], g=("umm", l, tt, ci_))
                yield
                ACT(lambda e: e.copy(u_sb.ap[:, 0:1024], psF[:, 0:1024]), r=[psres[0], psres[1]], w=[u_sb.res], g=("uev", l, tt))
                V(lambda e: e.tensor_copy(u_sb.ap[:, 1024:INW], psF[:, 1024:INW]), r=[psres[2], psres[3]], w=[u_sb.res], g=("uev", l, tt))
                if tl == 0 and b == 0:
                    dbg(f"u{l}", u_sb.ap, [u_sb.res])
                yield
                ACT(lambda e: e.activation(sq.ap[:, 0:QR], u_sb.ap[:, 0:QR], AF.Square, accum_out=S_(3)), r=[u_sb.res], w=[sq.res, statres[3]])
                ACT(lambda e: e.activation(sq.ap[:, 0:KVR], u_sb.ap[:, OFF_CKV:OFF_CKV + KVR], AF.Square, accum_out=S_(4)), r=[u_sb.res], w=[sq.res, statres[4]])
                ACT(lambda e: e.activation(S_(5), S_(3), AF.Sqrt, bias=EPS, scale=1.0 / QR), r=[statres[3]], w=[statres[5]])
                ACT(lambda e: e.activation(S_(6), S_(4), AF.Sqrt, bias=EPS, scale=1.0 / KVR), r=[statres[4]], w=[statres[6]])
                V(lambda e: e.reciprocal(S_(5), S_(5)), r=[statres[5]], w=[statres[5]])
                V(lambda e: e.reciprocal(S_(6), S_(6)), r=[statres[6]], w=[statres[6]])
                V(lambda e: e.scalar_tensor_tensor(cn.ap[:, 0:QR], u_sb.ap[:, 0:QR], S_(5), qgB.ap, ALU.mult, ALU.mult),
                  r=[u_sb.res, statres[5], qgB.res], w=[cn.res], g=("cn", l, tt))
                V(lambda e: e.scalar_tensor_tensor(cn.ap[:, QR:QR + KVR], u_sb.ap[:, OFF_CKV:OFF_CKV + KVR], S_(6), kvgB.ap, ALU.mult, ALU.mult),
                  r=[u_sb.res, statres[6], kvgB.res], w=[cn.res], g=("cn", l, tt))
                yield
                PE(lambda e: e.transpose(psB[:, 1024:1152], cn.ap[:, 0:128], ident.ap), r=[cn.res, ident.res], w=[psres[7]], g=("cTt", l, tt))
                PE(lambda e: e.transpose(psB[0:64, 1152:1280], cn.ap[:, 128:192], ident.ap), r=[cn.res, ident.res], w=[psres[7]], g=("cTt", l, tt))
                PE(lambda e: e.transpose(psB[:, 1280:1408], cn.ap[:, 192:320], ident.ap), r=[cn.res, ident.res], w=[psres[7]], g=("cTt", l, tt))
                ACT(lambda e: e.copy(cTs.ap.rearrange("p a b -> p (a b)"), psB[:, 1024:1408]), r=[psres[7]], w=[cTs.res])
                yield
                PE(lambda e: e.matmul(psF[:, 2048:2560], cTs.ap[:, 0, :], wuq_sb.ap[:, 0, 0:512], start=True, stop=False), r=[cTs.res, wuq_sb.res], w=[psres[4]], g=("qp", l, tt))
                PE(lambda e: e.matmul(psF[:, 2048:2560], cTs.ap[0:64, 1, :], wuq_sb.ap[0:64, 1, 0:512], start=False, stop=True), r=[cTs.res, wuq_sb.res], w=[psres[4]], g=("qp", l, tt))
                PE(lambda e: e.matmul(psF[:, 2560:2624], cTs.ap[:, 0, :], wuq_sb.ap[:, 0, 512:576], start=True, stop=False), r=[cTs.res, wuq_sb.res], w=[psres[5]], g=("qp2", l, tt))
                PE(lambda e: e.matmul(psF[:, 2560:2624], cTs.ap[0:64, 1, :], wuq_sb.ap[0:64, 1, 512:576], start=False, stop=True), r=[cTs.res, wuq_sb.res], w=[psres[5]], g=("qp2", l, tt))
                ACT(lambda e: e.copy(q_sb.ap.rearrange("p a b -> p (a b)"), psF[:, 2048:2624]), r=[psres[4], psres[5]], w=[q_sb.res])
                PE(lambda e: e.matmul(psF[:, 2048:2560], cTs.ap[:, 2, :], wukv_sb.ap[:, 0:512], start=True, stop=True), r=[cTs.res, wukv_sb.res], w=[psres[4]])
                PE(lambda e: e.matmul(psF[:, 2560:2816], cTs.ap[:, 2, :], wukv_sb.ap[:, 512:768], start=True, stop=True), r=[cTs.res, wukv_sb.res], w=[psres[5]])
                V(lambda e: e.tensor_copy(kv_sb.ap.rearrange("p a b -> p (a b)"), psF[:, 2048:2816]), r=[psres[4], psres[5]], w=[kv_sb.res])
                yield
                sq3 = sq.ap[:, 0:HM * QKM].rearrange("p (a b) -> p a b", a=HM)
                V(lambda e: e.tensor_tensor(sq3, q_sb.ap, q_sb.ap, ALU.mult), r=[q_sb.res], w=[sq.res])
                V(lambda e: e.tensor_reduce(S_(7, 6), sq3, AX.X, ALU.add), r=[sq.res], w=[statres[7]])
                ACT(lambda e: e.activation(S_(7, 6), S_(7, 6), AF.Sqrt, bias=EPS, scale=1.0 / QKM), r=[statres[7]], w=[statres[7]])
                V(lambda e: e.reciprocal(S_(7, 6), S_(7, 6)), r=[statres[7]], w=[statres[7]])
                V(lambda e: e.tensor_tensor(qn.ap, q_sb.ap, S_(7, 6).unsqueeze(2).to_broadcast([128, HM, QKM]), ALU.mult), r=[q_sb.res, statres[7]], w=[qn.res])
                V(lambda e: e.tensor_tensor(qn.ap, qn.ap, gM.ap[:, 0:1, :].to_broadcast([128, HM, QKM]), ALU.mult), r=[qn.res, gM.res], w=[qn.res])
                cs_m = cosM.ap[:, tt:tt + 1, :].to_broadcast([128, HM, 16]); sn_m = sinM.ap[:, tt:tt + 1, :].to_broadcast([128, HM, 16])
                qa = qn.ap[:, :, 64:80]; qb = qn.ap[:, :, 80:96]
                r1 = rt.ap[:, 0, :, 0:16]; r2 = rt.ap[:, 0, :, 16:32]; r3 = rt.ap[:, 1, :, 0:16]; r4 = rt.ap[:, 1, :, 16:32]
                ACT(lambda e: e.copy(qbf.ap[:, :, 0:64], qn.ap[:, :, 0:64]), r=[qn.res], w=[qbf.res], g=("qbf", l, tt))
                V(lambda e, cs_m=cs_m: e.tensor_tensor(r1, qa, cs_m, ALU.mult), r=[qn.res, cosM.res], w=[rt.res], g=("rt", l, tt, 0))
                V(lambda e, sn_m=sn_m: e.tensor_tensor(r2, qb, sn_m, ALU.mult), r=[qn.res, sinM.res], w=[rt.res], g=("rt", l, tt, 0))
                V(lambda e, sn_m=sn_m: e.tensor_tensor(r3, qa, sn_m, ALU.mult), r=[qn.res, sinM.res], w=[rt.res], g=("rt", l, tt, 0))
                V(lambda e, cs_m=cs_m: e.tensor_tensor(r4, qb, cs_m, ALU.mult), r=[qn.res, cosM.res], w=[rt.res], g=("rt", l, tt, 0))
                V(lambda e: e.tensor_tensor(qbf.ap[:, :, 64:80], r1, r2, ALU.subtract), r=[rt.res], w=[qbf.res], g=("qbf", l, tt))
                V(lambda e: e.tensor_tensor(qbf.ap[:, :, 80:96], r3, r4, ALU.add), r=[rt.res], w=[qbf.res], g=("qbf", l, tt))
                yield
                sqk = sq.ap[:, 0:HM * 64].rearrange("p (a b) -> p a b", a=HM)
                V(lambda e: e.tensor_tensor(sqk, kv_sb.ap[:, :, 0:64], kv_sb.ap[:, :, 0:64], ALU.mult), r=[kv_sb.res], w=[sq.res])
                V(lambda e: e.tensor_reduce(S_(9, 6), sqk, AX.X, ALU.add), r=[sq.res], w=[statres[9]])
                ACT(lambda e: e.activation(kr.ap[:, 3, :], u_sb.ap[:, OFF_KR:OFF_KR + 32], AF.Square, accum_out=S_(11)), r=[u_sb.res], w=[kr.res, statres[11]], g=("kr", l, tt))
                V(lambda e: e.tensor_scalar(S_(9, 6), S_(9, 6), S_(11), None, ALU.add), r=[statres[9], statres[11]], w=[statres[9]])
                ACT(lambda e: e.activation(S_(9, 6), S_(9, 6), AF.Sqrt, bias=EPS, scale=1.0 / QKM), r=[statres[9]], w=[statres[9]])
                V(lambda e: e.reciprocal(S_(9, 6), S_(9, 6)), r=[statres[9]], w=[statres[9]])
                V(lambda e: e.tensor_tensor(qn.ap[:, :, 0:64], kv_sb.ap[:, :, 0:64], S_(9, 6).unsqueeze(2).to_broadcast([128, HM, 64]), ALU.mult),
                  r=[kv_sb.res, statres[9], qbf.res, rt.res], w=[qn.res])
                V(lambda e: e.tensor_tensor(kbf.ap[:, :, 0:64], qn.ap[:, :, 0:64], gM.ap[:, 1:2, 0:64].to_broadcast([128, HM, 64]), ALU.mult),
                  r=[qn.res, gM.res], w=[kbf.res], g=("kbf", l, tt))
                yield
                V(lambda e: e.tensor_tensor(kr.ap[:, 0, :], u_sb.ap[:, OFF_KR:OFF_KR + 32], gM.ap[:, 1, 64:96], ALU.mult), r=[u_sb.res, gM.res], w=[kr.res], g=("kr", l, tt))
                ka = kr.ap[:, 0, 0:16]; kb_ = kr.ap[:, 0, 16:32]
                c1 = cosM.ap[:, tt, :]; s1 = sinM.ap[:, tt, :]
                V(lambda e, c1=c1: e.tensor_tensor(kr.ap[:, 1, 0:16], ka, c1, ALU.mult), r=[kr.res, cosM.res], w=[kr.res])
                V(lambda e, s1=s1: e.tensor_tensor(kr.ap[:, 1, 16:32], kb_, s1, ALU.mult), r=[kr.res, sinM.res], w=[kr.res])
                V(lambda e, s1=s1: e.tensor_tensor(kr.ap[:, 2, 0:16], ka, s1, ALU.mult), r=[kr.res, sinM.res], w=[kr.res])
                V(lambda e, c1=c1: e.tensor_tensor(kr.ap[:, 2, 16:32], kb_, c1, ALU.mult), r=[kr.res, cosM.res], w=[kr.res])
                V(lambda e: e.tensor_tensor(kr.ap[:, 3, 0:16], kr.ap[:, 1, 0:16], kr.ap[:, 1, 16:32], ALU.subtract), r=[kr.res], w=[kr.res])
                V(lambda e: e.tensor_tensor(kr.ap[:, 3, 16:32], kr.ap[:, 2, 0:16], kr.ap[:, 2, 16:32], ALU.add), r=[kr.res], w=[kr.res])
                V(lambda e: e.tensor_tensor(kbf.ap[:, :, 64:96], kr.ap[:, 3:4, :].to_broadcast([128, HM, 32]), S_(9, 6).unsqueeze(2).to_broadcast([128, HM, 32]), ALU.mult),
                  r=[kr.res, statres[9]], w=[kbf.res], g=("kbf", l, tt))
                vbf = vbf_r.next()
                ACT(lambda e, vbf=vbf: e.copy(vbf.ap[:, :, 0:64], kv_sb.ap[:, :, 64:128]), r=[kv_sb.res], w=[vbf.res])
                dma(vmD[b, t0:t0 + 128, :], vbf.ap.rearrange("p a b -> p (a b)"), reads=[vbf.res], writes=[dres[("vmD", b)]], wgroup="vm")
                if tl == 0 and b == 0:
                    dbg(f"qbf{l}", qbf.ap.rearrange("p a b -> p (a b)"), [qbf.res])
                    dbg(f"kbf{l}", kbf.ap.rearrange("p a b -> p (a b)"), [kbf.res])
                yield
                for h in range(HM):
                    PE(lambda e, h=h: e.transpose(psB[0:QKM, 128 * h:128 * (h + 1)], qbf.ap[:, h, :], ident.ap), r=[qbf.res, ident.res], w=[psres[6]], g=("qTt", l, tt))
                    PE(lambda e, h=h: e.transpose(psB[0:QKM, 1024 + 128 * h:1024 + 128 * (h + 1)], kbf.ap[:, h, :], ident.ap), r=[kbf.res, ident.res], w=[psres[7]], g=("kTt", l, tt))
                qkT = qkT_r.next()
                ACT(lambda e, qkT=qkT: e.copy(qkT.ap[0:QKM, 0, :, :].rearrange("p a b -> p (a b)"), psB[0:QKM, 0:768]), r=[psres[6]], w=[qkT.res], g=("qkT", l, tt))
                V(lambda e, qkT=qkT: e.tensor_copy(qkT.ap[0:QKM, 1, :, :].rearrange("p a b -> p (a b)"), psB[0:QKM, 1024:1792]), r=[psres[7]], w=[qkT.res], g=("qkT", l, tt))
                dma(qmT[b, :, :, t0:t0 + 128].rearrange("h d t -> d h t"), qkT.ap[0:QKM, 0, :, :], reads=[qkT.res], writes=[dres[("qmT", b)]], wgroup="qm")
                dma(kmT[b, :, :, t0:t0 + 128].rearrange("h d t -> d h t"), qkT.ap[0:QKM, 1, :, :], reads=[qkT.res], writes=[dres[("kmT", b)]], wgroup="km")
                yield
                dqk = u_sb.ap[:, OFF_DQ:OFF_DQ + 768].rearrange("p (s h d) -> p s h d", s=2, h=HD)
                sq4 = sq.ap[:, 0:768].rearrange("p (s h d) -> p s h d", s=2, h=HD)
                V(lambda e: e.tensor_tensor(sq.ap[:, 0:768], u_sb.ap[:, OFF_DQ:OFF_DQ + 768], u_sb.ap[:, OFF_DQ:OFF_DQ + 768], ALU.mult), r=[u_sb.res], w=[sq.res])
                V(lambda e: e.tensor_reduce(S_(12, 12), sq.ap[:, 0:768].rearrange("p (a b) -> p a b", a=12), AX.X, ALU.add), r=[sq.res], w=[statres[12]])
                ACT(lambda e: e.activation(S_(12, 12), S_(12, 12), AF.Sqrt, bias=EPS, scale=1.0 / DD), r=[statres[12]], w=[statres[12]])
                V(lambda e: e.reciprocal(S_(12, 12), S_(12, 12)), r=[statres[12]], w=[statres[12]])
                dn3 = dn.ap.rearrange("p s h d -> p (s h) d")
                V(lambda e: e.tensor_tensor(dn3, u_sb.ap[:, OFF_DQ:OFF_DQ + 768].rearrange("p (a b) -> p a b", a=12), S_(12, 12).unsqueeze(2).to_broadcast([128, 12, DD]), ALU.mult),
                  r=[u_sb.res, statres[12]], w=[dn.res])
                for s_ in range(2):
                    V(lambda e, s_=s_: e.tensor_tensor(dn.ap[:, s_, :, :], dn.ap[:, s_, :, :], gD.ap[:, s_:s_ + 1, :].to_broadcast([128, HD, DD]), ALU.mult),
                      r=[dn.res, gD.res], w=[dn.res])
                yield
                cs_d = cosD.ap[:, tt:tt + 1, :].to_broadcast([128, 12, 32]); sn_d = sinD.ap[:, tt:tt + 1, :].to_broadcast([128, 12, 32])
                da = dn3[:, :, 0:32]; db = dn3[:, :, 32:64]
                rt3 = rt.ap.rearrange("p s h d -> p (s h) d"); rt23 = rt2.ap.rearrange("p s h d -> p (s h) d")
                dbf3 = dbf.ap.rearrange("p s h d -> p (s h) d")
                V(lambda e, cs_d=cs_d: e.tensor_tensor(rt3, da, cs_d, ALU.mult), r=[dn.res, cosD.res, qbf.res], w=[rt.res])
                V(lambda e, sn_d=sn_d: e.tensor_tensor(rt23, db, sn_d, ALU.mult), r=[dn.res, sinD.res], w=[rt2.res])
                V(lambda e: e.tensor_tensor(dbf3[:, :, 0:32], rt3, rt23, ALU.subtract), r=[rt.res, rt2.res], w=[dbf.res], g=("dbf", l, tt))
                V(lambda e, sn_d=sn_d: e.tensor_tensor(rt3, da, sn_d, ALU.mult), r=[dn.res, sinD.res, dbf.res], w=[rt.res])
                V(lambda e, cs_d=cs_d: e.tensor_tensor(rt23, db, cs_d, ALU.mult), r=[dn.res, cosD.res, dbf.res], w=[rt2.res])
                V(lambda e: e.tensor_tensor(dbf3[:, :, 32:64], rt3, rt23, ALU.add), r=[rt.res, rt2.res], w=[dbf.res], g=("dbf", l, tt))
                vdbf = vdbf_r.next()
                ACT(lambda e, vdbf=vdbf: e.copy(vdbf.ap[:, :, 0:64], u_sb.ap[:, OFF_DV:OFF_DV + 384].rearrange("p (a b) -> p a b", a=HD)), r=[u_sb.res], w=[vdbf.res])
                dma(vdD[b, t0:t0 + 128, :], vdbf.ap.rearrange("p a b -> p (a b)"), reads=[vdbf.res], writes=[dres[("vdD", b)]], wgroup="vd")
                if tl == 0 and b == 0:
                    dbg(f"dbf{l}", dbf.ap.rearrange("p s h d -> p (s h d)"), [dbf.res])
                yield
                dbf2 = dbf.ap.rearrange("p s h d -> p s (h d)")
                for s_ in range(2):
                    for j in range(3):
                        PE(lambda e, s_=s_, j=j: e.transpose(psB[:, 1024 * s_ + 128 * j:1024 * s_ + 128 * (j + 1)], dbf2[:, s_, 128 * j:128 * (j + 1)], ident.ap),
                           r=[dbf.res, ident.res], w=[psres[6 + s_]], g=("dTt", l, tt, s_))
                dT = dT_r.next()
                ACT(lambda e, dT=dT: e.copy(dT.ap[:, 0, :, :].rearrange("p a b -> p (a b)"), psB[:, 0:384]), r=[psres[6]], w=[dT.res], g=("dT", l, tt))
                V(lambda e, dT=dT: e.tensor_copy(dT.ap[:, 1, :, :].rearrange("p a b -> p (a b)"), psB[:, 1024:1408]), r=[psres[7]], w=[dT.res], g=("dT", l, tt))
                dma(qdT[b, :, :, t0:t0 + 128].rearrange("j d t -> d j t"), dT.ap[:, 0, :, :], reads=[dT.res], writes=[dres[("qdT", b)]], wgroup="qd")
                dma(kdT[b, :, :, t0:t0 + 128].rearrange("j d t -> d j t"), dT.ap[:, 1, :, :], reads=[dT.res], writes=[dres[("kdT", b)]], wgroup="kd")
                yield
                dma(uctok[b, t0:t0 + 128, :], u_sb.ap[:, OFF_UC:OFF_UC + CH], reads=[u_sb.res], writes=[dres[("uctok", b)]], wgroup="uct")
                ACT(lambda e: e.copy(ucb.ap, u_sb.ap[:, OFF_UC:OFF_UC + CH]), r=[u_sb.res], w=[ucb.res])
                for hh in range(2):
                    PE(lambda e, hh=hh: e.transpose(psB[:, 512 + 128 * hh:512 + 128 * (hh + 1)], ucb.ap[:, 128 * hh:128 * (hh + 1)], ident.ap),
                       r=[ucb.res, ident.res], w=[psres[6]], g=("ucTt", l, tt))
                ucTs = ucT_r.next()
                ACT(lambda e, ucTs=ucTs: e.copy(ucTs.ap.rearrange("p a b -> p (a b)"), psB[:, 512:768]), r=[psres[6]], w=[ucTs.res])
                dma(ucT[b, :, t0:t0 + 128].rearrange("(hh p) t -> p hh t", p=128), ucTs.ap, reads=[ucTs.res], writes=[dres[("ucT", b)]], wgroup="uc")
            active = []
            nxt = 0
            ntile = S // 128
            while active or nxt < ntile:
                if len(active) < 2 and nxt < ntile:
                    active.append(tile_gen(nxt))
                    nxt += 1
                for g_ in list(active):
                    try:
                        next(g_)
                    except StopIteration:
                        active.remove(g_)
            A.reset(m)
            P.barrier()

        def phase_attn(b, kind):
            m = A.mark()
            if kind == "mla":
                KD, scale, off = QKM, QKM ** -0.5, 0
                qsrc, ksrc, vsrc, rq, rk_, rv = qmT, kmT, vmD, dres[("qmT", b)], dres[("kmT", b)], dres[("vmD", b)]
                qall = A.alloc("qall", [128, HM, S], BF16); kall = A.alloc("kall", [128, HM, S], BF16)
                dma(qall.ap[0:QKM], qsrc[b].rearrange("h d t -> d h t"), reads=[rq], writes=[qall.res])
                dma(kall.ap[0:QKM], ksrc[b].rearrange("h d t -> d h t"), reads=[rk_], writes=[kall.res])
                qv = lambda h, c0, c1: qall.ap[0:QKM, h, c0:c1]
                kv_ = lambda h, c0, c1: kall.ap[0:QKM, h, c0:c1]
            else:
                KD, scale, off = DD, DD ** -0.5, 384
                qsrc, ksrc, vsrc, rq, rk_, rv = qdT, kdT, vdD, dres[("qdT", b)], dres[("kdT", b)], dres[("vdD", b)]
                qall = A.alloc("qall", [128, 3, S], BF16); kall = A.alloc("kall", [128, 3, S], BF16)
                dma(qall.ap, qsrc[b].rearrange("j d t -> d j t"), reads=[rq], writes=[qall.res])
                dma(kall.ap, ksrc[b].rearrange("j d t -> d j t"), reads=[rk_], writes=[kall.res])
                qv = lambda h, c0, c1: qall.ap[64 * (h % 2):64 * (h % 2) + 64, h // 2, c0:c1]
                kv_ = lambda h, c0, c1: kall.ap[64 * (h % 2):64 * (h % 2) + 64, h // 2, c0:c1]
            Vs = A.alloc("Vs", [128, 16, 6 * 65], BF16)
            dma(Vs.ap, vsrc[b].rearrange("(t p) c -> p t c", p=128), reads=[rv], writes=[Vs.res])
            pT_r = A.ring("pT", [128, 1024], BF16, 4)
            pTm_r = A.ring("pTm", [128, 1024], BF16, 4)
            osb_r = A.ring("osb", [128, 4, 65], F32, 2)
            oa = A.alloc("oa", [128, 4, 384], F32)
            rden = A.alloc("rden", [128, 4], F32)
            mx_r = A.ring("mx", [128, 4, 384], BF16, 2)
            junk2 = A.alloc("junk2", [128, 384], BF16)
            steps = []
            for qc in range(4):
                for h in range(6):
                    kts = []
                    for kt in range(16):
                        if kind == "dil":
                            dmin = 128 * kt - 512 * qc - 511
                            dmax = 128 * kt + 127 - 512 * qc
                            if dmin > 1024 or dmax < -1024:
                                continue
                        kts.append(kt)
                    prs = [kts[i:i + 2] for i in range(0, len(kts), 2)]
                    for pi, pr in enumerate(prs):
                        steps.append((qc, h, pr, pi == 0, pi == len(prs) - 1))

            SPAIR = (0, 2, 6)

            def spair_ap(pb, ncol):
                return psF[:, 512 * pb:512 * pb + ncol] if pb < 6 else psB_f[:, 0:ncol]

            def emit_S(si):
                qc, h, pr, _, _ = steps[si]
                pb = SPAIR[si % 3]
                for i_, kt in enumerate(pr):
                    PE(lambda e, bk=pb + i_, h=h, kt=kt, qc=qc: e.matmul(bankF(bk), kv_(h, 128 * kt, 128 * kt + 128), qv(h, 512 * qc, 512 * qc + 512), start=True, stop=True),
                       r=[qall.res, kall.res], w=[psres[pb + i_]])

            S_ = lambda i, n=1: stat.ap[:, 4 * i:4 * i + n]
            ocnt = 0
            emit_S(0)
            emit_S(1)
            for si, (qc, h, pr, first, last) in enumerate(steps):
                if si + 2 < len(steps):
                    emit_S(si + 2)
                pb = SPAIR[si % 3]
                npr = len(pr)
                ob = 4 + (ocnt % 2)
                pT = pT_r.next()
                ACT(lambda e, pb=pb, pT=pT, npr=npr: e.activation(pT.ap[:, 0:512 * npr], spair_ap(pb, 512 * npr), AF.Exp, bias=0.0, scale=scale),
                    r=[psres[pb + i_] for i_ in range(npr)], w=[pT.res])
                if kind == "dil":
                    pm = pTm_r.next()
                    for i_, kt in enumerate(pr):
                        x0 = 1920 - 128 * (kt - 4 * qc)
                        eng = "dve"
                        P.op(eng, lambda e, pm=pm, pT=pT, x0=x0, i_=i_: e.tensor_tensor(pm.ap[:, 512 * i_:512 * i_ + 512], pT.ap[:, 512 * i_:512 * i_ + 512], maskS.ap[:, x0:x0 + 512], ALU.mult),
                             reads=[pT.res, maskS.res], writes=[pm.res], wgroup=("pm", l, b, si))
                    pT = pm
                for i_, kt in enumerate(pr):
                    for j in range(4):
                        PE(lambda e, j=j, pT=pT, kt=kt, h=h, ob=ob, i_=i_, st_=(first and i_ == 0 and j == 0), sp_=(last and i_ == npr - 1 and j == 3):
                           e.matmul(bankF(ob)[:, 65 * j:65 * j + 65], pT.ap[:, 512 * i_ + 128 * j:512 * i_ + 128 * j + 128], Vs.ap[:, kt, 65 * h:65 * h + 65], start=st_, stop=sp_),
                           r=[pT.res, Vs.res], w=[psres[ob]], g=("pv", l, b, kind, qc, h))
                if last:
                    ocnt += 1
                    osb = osb_r.next()
                    ACT(lambda e, osb=osb, ob=ob: e.copy(osb.ap.rearrange("p a b -> p (a b)"), bankF(ob)[:, 0:260]), r=[psres[ob]], w=[osb.res])
                    V(lambda e, osb=osb: e.reciprocal(rden.ap, osb.ap[:, :, 64]), r=[osb.res], w=[rden.res])
                    V(lambda e, osb=osb, h=h: e.tensor_tensor(oa.ap[:, :, 64 * h:64 * h + 64], osb.ap[:, :, 0:64], rden.ap.unsqueeze(2).to_broadcast([128, 4, 64]), ALU.mult),
                      r=[osb.res, rden.res], w=[oa.res], g=("oa", l, b, kind, qc))
                    if h == 5:
                        for j in range(4):
                            ACT(lambda e, j=j: e.activation(junk2.ap, oa.ap[:, j, :], AF.Square, accum_out=stat.ap[:, j:j + 1]),
                                r=[oa.res], w=[junk2.res, statres[0]], g=("oass", l, b, kind, qc))
                        ACT(lambda e: e.activation(stat.ap[:, 4:8], stat.ap[:, 0:4], AF.Sqrt, bias=EPS, scale=1.0 / 384), r=[statres[0]], w=[statres[1]])
                        V(lambda e: e.reciprocal(stat.ap[:, 4:8], stat.ap[:, 4:8]), r=[statres[1]], w=[statres[1]])
                        mx = mx_r.next()
                        V(lambda e: e.tensor_tensor(oa.ap, oa.ap, stat.ap[:, 4:8].unsqueeze(2).to_broadcast([128, 4, 384]), ALU.mult), r=[oa.res, statres[1]], w=[oa.res])
                        GP(lambda e, mx=mx: e.tensor_tensor(mx.ap, oa.ap, mixB.ap[:, off:off + 384].unsqueeze(1).to_broadcast([128, 4, 384]), ALU.mult), r=[oa.res, mixB.res], w=[mx.res])
                        dma(mixedD[b, 512 * qc:512 * qc + 512, off:off + 384].rearrange("(j p) c -> p j c", p=128), mx.ap, reads=[mx.res], writes=[dres[("mixedD", b)]], wgroup="mixw")
            A.reset(m)
            P.barrier()

        def phase_s5():
            m = A.mark()
            rr = A.alloc("rr", [128, 32], F32); th = A.alloc("th", [128, 32], F32); thb = A.alloc("thb", [128, 32], F32)
            LB = A.alloc("LB", [128, 32, 128], BF16); LBs = A.alloc("LBs", [128, 32, 128], BF16)
            W1T = A.alloc("W1T", [128, 4, 128], BF16); W2T = A.alloc("W2T", [128, 4, 128], BF16)
            scr5 = sincos_scratch(144)
            m_keep = A.mark()
            lre = A.alloc("lre", [128, 32], F32); lim = A.alloc("lim", [128, 32], F32); ldt = A.alloc("ldt", [128, 32], F32)
            for half in range(2):
                dma(lre.ap[64 * half:64 * half + 64, :], ssm_a_re[l].rearrange("d g p -> p (d g)"), writes=[lre.res], wgroup="lre", ncont=True)
                dma(lim.ap[64 * half:64 * half + 64, :], ssm_a_im[l].rearrange("d g p -> p (d g)"), writes=[lim.res], wgroup="lim", ncont=True)
            dma(ldt.ap, ssm_log_dt[l].rearrange("d g -> (d g)").partition_broadcast(128), writes=[ldt.res])
            dtt = A.alloc("dtt", [128, 32], F32)
            ACT(lambda e: e.activation(dtt.ap, ldt.ap, AF.Exp), r=[ldt.res], w=[dtt.res])
            V(lambda e: e.tensor_mul(rr.ap, lre.ap, dtt.ap), r=[lre.res, dtt.res], w=[rr.res])
            ACT(lambda e: e.activation(rr.ap, rr.ap, AF.Exp), r=[rr.res], w=[rr.res])
            V(lambda e: e.tensor_mul(th.ap, lim.ap, dtt.ap), r=[lim.res, dtt.res], w=[th.res])
            V(lambda e: e.tensor_scalar(th.ap, th.ap, INV2PI, None, ALU.mult), r=[th.res], w=[th.res])
            V(lambda e: e.tensor_scalar(thb.ap, th.ap, 1024.0, None, ALU.mult), r=[th.res], w=[thb.res])
            sn0 = A.alloc("sn0", [128, 32], F32); cs0 = A.alloc("cs0", [128, 32], F32)
            sincos(sn0, cs0, th, 32, scr5)
            kre = A.alloc("kre", [128, 32], F32); kim = A.alloc("kim", [128, 32], F32)
            t_a = A.alloc("t_a", [128, 32], F32); t_b = A.alloc("t_b", [128, 32], F32); den = A.alloc("den", [128, 32], F32)
            V(lambda e: e.tensor_mul(cs0.ap, cs0.ap, rr.ap), r=[cs0.res, rr.res], w=[cs0.res])
            V(lambda e: e.tensor_scalar(cs0.ap, cs0.ap, -1.0, None, ALU.add), r=[cs0.res], w=[cs0.res])
            V(lambda e: e.tensor_mul(sn0.ap, sn0.ap, rr.ap), r=[sn0.res, rr.res], w=[sn0.res])
            V(lambda e: e.tensor_mul(t_a.ap, lre.ap, lre.ap), r=[lre.res], w=[t_a.res])
            V(lambda e: e.tensor_mul(t_b.ap, lim.ap, lim.ap), r=[lim.res], w=[t_b.res])
            V(lambda e: e.tensor_add(den.ap, t_a.ap, t_b.ap), r=[t_a.res, t_b.res], w=[den.res])
            V(lambda e: e.reciprocal(den.ap, den.ap), r=[den.res], w=[den.res])
            V(lambda e: e.tensor_mul(t_a.ap, cs0.ap, lre.ap), r=[cs0.res, lre.res, den.res], w=[t_a.res])
            V(lambda e: e.tensor_mul(t_b.ap, sn0.ap, lim.ap), r=[sn0.res, lim.res], w=[t_b.res])
            V(lambda e: e.tensor_add(kre.ap, t_a.ap, t_b.ap), r=[t_a.res, t_b.res], w=[kre.res])
            V(lambda e: e.tensor_mul(kre.ap, kre.ap, den.ap), r=[kre.res, den.res], w=[kre.res])
            V(lambda e: e.tensor_mul(t_a.ap, sn0.ap, lre.ap), r=[sn0.res, lre.res, kre.res], w=[t_a.res])
            V(lambda e: e.tensor_mul(t_b.ap, cs0.ap, lim.ap), r=[cs0.res, lim.res, kre.res], w=[t_b.res])
            V(lambda e: e.tensor_sub(kim.ap, t_a.ap, t_b.ap), r=[t_a.res, t_b.res], w=[kim.res])
            V(lambda e: e.tensor_mul(kim.ap, kim.ap, den.ap), r=[kim.res, den.res], w=[kim.res])
            bre = A.alloc("bre", [64, 32, GC], F32); bim = A.alloc("bim", [64, 32, GC], F32)
            dma(bre.ap, ssm_b_re[l].rearrange("d g p c -> p (d g) c"), writes=[bre.res])
            dma(bim.ap, ssm_b_im[l].rearrange("d g p c -> p (d g) c"), writes=[bim.res])
            Bre = A.alloc("Bre", [64, 32, GC], BF16); Bim = A.alloc("Bim", [64, 32, GC], BF16); Bren = A.alloc("Bren", [64, 32, GC], BF16)
            tb1 = A.alloc("tb1", [64, 32, GC], F32); tb2 = A.alloc("tb2", [64, 32, GC], F32)
            kre_b = kre.ap[0:64, :].unsqueeze(2).to_broadcast([64, 32, GC]); kim_b = kim.ap[0:64, :].unsqueeze(2).to_broadcast([64, 32, GC])
            V(lambda e: e.tensor_tensor(tb1.ap, bre.ap, kre_b, ALU.mult), r=[bre.res, kre.res], w=[tb1.res])
            V(lambda e: e.tensor_tensor(tb2.ap, bim.ap, kim_b, ALU.mult), r=[bim.res, kim.res], w=[tb2.res])
            V(lambda e: e.tensor_tensor(Bre.ap, tb1.ap, tb2.ap, ALU.subtract), r=[tb1.res, tb2.res], w=[Bre.res])
            V(lambda e: e.tensor_tensor(Bren.ap, tb2.ap, tb1.ap, ALU.subtract), r=[tb1.res, tb2.res], w=[Bren.res])
            V(lambda e: e.tensor_tensor(tb1.ap, bim.ap, kre_b, ALU.mult), r=[bim.res, kre.res, Bre.res, Bren.res], w=[tb1.res])
            V(lambda e: e.tensor_tensor(tb2.ap, bre.ap, kim_b, ALU.mult), r=[bre.res, kim.res, Bre.res, Bren.res], w=[tb2.res])
            V(lambda e: e.tensor_tensor(Bim.ap, tb1.ap, tb2.ap, ALU.add), r=[tb1.res, tb2.res], w=[Bim.res])
            for blk in range(4):
                sl = lambda t_, blk=blk: t_.ap[:, 8 * blk:8 * blk + 8, :].rearrange("p a b -> p (a b)")
                s_re, s_im, s_ren = sl(Bre), sl(Bim), sl(Bren)
                PE(lambda e, a_=s_re: e.transpose(psB[:, 0:64], a_, ident.ap[0:64, 0:64]), r=[Bre.res, ident.res], w=[psres[6]], g=("Bt", l, blk))
                PE(lambda e, a_=s_im: e.transpose(psB[:, 64:128], a_, ident.ap[0:64, 0:64]), r=[Bim.res, ident.res], w=[psres[6]], g=("Bt", l, blk))
                PE(lambda e, a_=s_im: e.transpose(psB[:, 128:192], a_, ident.ap[0:64, 0:64]), r=[Bim.res, ident.res], w=[psres[6]], g=("Bt", l, blk))
                PE(lambda e, a_=s_ren: e.transpose(psB[:, 192:256], a_, ident.ap[0:64, 0:64]), r=[Bren.res, ident.res], w=[psres[6]], g=("Bt", l, blk))
                for g8 in range(8):
                    gd = blk * 8 + g8
                    V(lambda e, gd=gd, g8=g8: e.tensor_scalar(LB.ap[:, gd, :], psB[:, 0:128], misc.ap[:, 144 + g8:145 + g8], None, ALU.mult), r=[psres[6], misc.res], w=[LB.res], g=("LB", l))
                    V(lambda e, gd=gd, g8=g8: e.tensor_scalar(LBs.ap[:, gd, :], psB[:, 128:256], misc.ap[:, 144 + g8:145 + g8], None, ALU.mult), r=[psres[6], misc.res], w=[LBs.res], g=("LBs", l))
            Cin = A.alloc("Cin", [128, 4, 128], F32)
            dma(Cin.ap[:, :, 0:64], ssm_c_re[l].rearrange("d g c p -> (d g c) p").rearrange("(k q) p -> q k p", q=128), writes=[Cin.res], wgroup="cin")
            dma(Cin.ap[:, :, 64:128], ssm_c_im[l].rearrange("d g c p -> (d g c) p").rearrange("(k q) p -> q k p", q=128), writes=[Cin.res], wgroup="cin")
            W1s = A.alloc("W1s", [128, 4, 128], BF16); W2s = A.alloc("W2s", [128, 4, 128], BF16)
            V(lambda e: e.tensor_copy(W1s.ap[:, :, 0:64], Cin.ap[:, :, 0:64]), r=[Cin.res], w=[W1s.res], g="w1s")
            V(lambda e: e.tensor_scalar(W1s.ap[:, :, 64:128], Cin.ap[:, :, 64:128], -1.0, None, ALU.mult), r=[Cin.res], w=[W1s.res], g="w1s")
            V(lambda e: e.tensor_scalar(W2s.ap[:, :, 0:64], Cin.ap[:, :, 64:128], -1.0, None, ALU.mult), r=[Cin.res], w=[W2s.res], g="w2s")
            V(lambda e: e.tensor_scalar(W2s.ap[:, :, 64:128], Cin.ap[:, :, 0:64], -1.0, None, ALU.mult), r=[Cin.res], w=[W2s.res], g="w2s")
            for blk in range(4):
                PE(lambda e, blk=blk: e.transpose(psB[:, 1024:1152], W1s.ap[:, blk, :], ident.ap), r=[W1s.res, ident.res], w=[psres[7]], g=("Wt", l, blk))
                PE(lambda e, blk=blk: e.transpose(psB[:, 1152:1280], W2s.ap[:, blk, :], ident.ap), r=[W2s.res, ident.res], w=[psres[7]], g=("Wt", l, blk))
                V(lambda e, blk=blk: e.tensor_copy(W1T.ap[:, blk, :], psB[:, 1024:1152]), r=[psres[7]], w=[W1T.res], g="W1T")
                V(lambda e, blk=blk: e.tensor_copy(W2T.ap[:, blk, :], psB[:, 1152:1280]), r=[psres[7]], w=[W2T.res], g="W2T")
            P.barrier()
            A.reset(m_keep)
            m_main = A.mark()

            def rev(ap2d, c0, n):
                a = ap2d[:, c0:c0 + n]
                return bass.AP(a.tensor, a.offset + (n - 1) * a.ap[-1][0], [list(a.ap[0]), [-a.ap[-1][0], n]])

            yalls = [A.alloc(f"yall{b}", [128, 16, CH], F32) for b in range(NB)]
            m_y = A.mark()
            uT = A.alloc("uT", [128, NB, 2, S], BF16)
            for b in range(NB):
                dma(uT.ap[:, b, :, :], ucT[b].rearrange("(hh p) t -> p hh t", p=128), reads=[dres[("ucT", b)]], writes=[uT.res], wgroup="uTl")
            ytab = A.alloc("ytab", [128, 1024], F32); ftab = A.alloc("ftab", [128, 1024], F32)
            cosTs = [A.alloc(f"cosT{d_}", [128, 16, 128], F32) for d_ in range(2)]
            sinTs = [A.alloc(f"sinT{d_}", [128, 16, 128], F32) for d_ in range(2)]
            z_r = A.ring("z", [128, S], F32, 2)
            t1_r = A.ring("t1", [128, 512], F32, 2); t2_r = A.ring("t2", [128, 512], F32, 2)
            P12 = [[A.alloc(f"P{i}{d_}", [128, S], BF16) for i in range(2)] for d_ in range(2)]
            ycnt = 0
            for g in range(G):
                half = g // 8
                for d_ in range(2):
                    gd = d_ * 16 + g
                    cosT, sinT = cosTs[d_], sinTs[d_]
                    cosf = cosT.ap.rearrange("p a b -> p (a b)"); sinf = sinT.ap.rearrange("p a b -> p (a b)")
                    for hb_ in range(2):
                        c0_ = 1024 * hb_
                        if hb_ == 0:
                            ACT(lambda e, gd=gd: e.activation(ytab.ap, iota.ap, AF.Identity, bias=0.0, scale=th.ap[:, gd:gd + 1]), r=[iota.res, th.res], w=[ytab.res])
                        else:
                            ACT(lambda e, gd=gd: e.activation(ytab.ap, iota.ap, AF.Identity, bias=thb.ap[:, gd:gd + 1], scale=th.ap[:, gd:gd + 1]), r=[iota.res, th.res, thb.res], w=[ytab.res])
                        V(lambda e: e.tensor_scalar(ftab.ap, ytab.ap, MAGIC, MAGIC, ALU.add, ALU.subtract), r=[ytab.res], w=[ftab.res])
                        V(lambda e: e.tensor_sub(ftab.ap, ytab.ap, ftab.ap), r=[ytab.res, ftab.res], w=[ftab.res])
                        ACT(lambda e, sinf=sinf, c0_=c0_: e.activation(sinf[:, c0_:c0_ + 1024], ftab.ap, AF.Sin, bias=0.0, scale=TWO_PI_S), r=[ftab.res], w=[sinT.res], g=("sinT", l, gd))
                        ACT(lambda e: e.activation(ytab.ap, ftab.ap, AF.Abs), r=[ftab.res], w=[ytab.res])
                        ACT(lambda e, cosf=cosf, c0_=c0_: e.activation(cosf[:, c0_:c0_ + 1024], ytab.ap, AF.Sin, bias=HALF_PI_S, scale=-TWO_PI_S), r=[ytab.res], w=[cosT.res], g=("cosT", l, gd))
                units = [(b, d_) for b in range(NB) for d_ in range(2)]

                def stage_A(b, d_, g=g, half=half):
                    gd = d_ * 16 + g
                    cosT, sinT = cosTs[d_], sinTs[d_]
                    cosTf = cosT.ap.rearrange("p a b -> p (a b)"); sinTf = sinT.ap.rearrange("p a b -> p (a b)")
                    z = z_r.next()
                    for c in range(4):
                        cn_ = c if d_ == 0 else 3 - c
                        PE(lambda e, gd=gd, cn_=cn_, b=b: e.matmul(bankF(0), LB.ap[:, gd, :], uT.ap[:, b, half, 512 * cn_:512 * cn_ + 512], start=True, stop=True),
                           r=[LB.res, uT.res], w=[psres[0]])
                        PE(lambda e, gd=gd, cn_=cn_, b=b: e.matmul(bankF(1), LBs.ap[:, gd, :], uT.ap[:, b, half, 512 * cn_:512 * cn_ + 512], start=True, stop=True),
                           r=[LBs.res, uT.res], w=[psres[1]])
                        t1 = t1_r.next(); t2 = t2_r.next()
                        v0 = bankF(0) if d_ == 0 else rev(bankF(0), 0, 512)
                        v1 = bankF(1) if d_ == 0 else rev(bankF(1), 0, 512)
                        V(lambda e, t1=t1, v0=v0, c=c, cosTf=cosTf: e.tensor_tensor(t1.ap, v0, cosTf[:, 512 * c:512 * c + 512], ALU.mult), r=[psres[0], cosT.res], w=[t1.res])
                        V(lambda e, t2=t2, v1=v1, c=c, sinTf=sinTf: e.tensor_tensor(t2.ap, v1, sinTf[:, 512 * c:512 * c + 512], ALU.mult), r=[psres[1], sinT.res], w=[t2.res])
                        GP(lambda e, t1=t1, t2=t2, c=c, z=z: e.tensor_tensor(z.ap[:, 512 * c:512 * c + 512], t1.ap, t2.ap, ALU.add), r=[t1.res, t2.res], w=[z.res], g=("z", l, g, b, d_))
                    return z

                def stage_B(b, d_, z, g=g, half=half):
                    gd = d_ * 16 + g
                    cosT, sinT = cosTs[d_], sinTs[d_]
                    cosTf = cosT.ap.rearrange("p a b -> p (a b)"); sinTf = sinT.ap.rearrange("p a b -> p (a b)")
                    V(lambda e, gd=gd, z=z: e.tensor_tensor_scan(z.ap, rr.ap[:, gd:gd + 1].to_broadcast([128, S]), z.ap, 0.0, ALU.mult, ALU.add), r=[z.res, rr.res], w=[z.res])
                    p1, p2 = P12[d_]
                    o1 = p1.ap if d_ == 0 else rev(p1.ap, 0, S)
                    o2 = p2.ap if d_ == 0 else rev(p2.ap, 0, S)
                    V(lambda e, o1=o1, cosTf=cosTf, z=z: e.tensor_tensor(o1, z.ap, cosTf, ALU.mult), r=[z.res, cosT.res], w=[p1.res])
                    GP(lambda e, o2=o2, sinTf=sinTf, z=z: e.tensor_tensor(o2, z.ap, sinTf, ALU.mult), r=[z.res, sinT.res], w=[p2.res])
                    if b == 0 and g == 0:
                        dbg(f"P1_{l}_{d_}", p1.ap, [p1.res])

                def y_mm(b, g=g, half=half):
                    nonlocal ycnt
                    yall = yalls[b]
                    yb_ = 2 + ycnt % 2
                    ycnt += 1
                    for tl in range(16):
                        for d_ in range(2):
                            blk = d_ * 2 + half
                            for i in range(2):
                                Wt = (W1T, W2T)[i]
                                pp = P12[d_][i]
                                PE(lambda e, tl=tl, i=i, d_=d_, Wt=Wt, blk=blk, yb_=yb_, pp=pp, g=g: e.matmul(bankF(yb_)[:, 16 * tl:16 * tl + 16], pp.ap[:, 128 * tl:128 * tl + 128],
                                                                                       Wt.ap[:, blk, 16 * (g % 8):16 * (g % 8) + 16], start=(d_ == 0 and i == 0), stop=(d_ == 1 and i == 1)),
                                   r=[pp.res, Wt.res], w=[psres[yb_]], g=("ymm", l, g, b, tl))
                    ACT(lambda e, yb_=yb_, yall=yall, g=g: e.copy(yall.ap[:, :, 16 * g:16 * g + 16], bankF(yb_)[:, 0:256].rearrange("p (a b) -> p a b", a=16)), r=[psres[yb_]], w=[yall.res], g=("yall", l, b))

                zs_ = {}
                zs_[0] = stage_A(*units[0])
                for ui, (b, d_) in enumerate(units):
                    if ui + 1 < len(units):
                        zs_[ui + 1] = stage_A(*units[ui + 1])
                    stage_B(b, d_, zs_[ui])
                    if d_ == 1:
                        y_mm(b)
            P.barrier()
            for b in range(NB):
                A.reset(m_y)
                s5_epilogue(b, yalls[b])
                P.barrier()
            A.reset(m)
            P.barrier()

        def s5_epilogue(b, yall):
            m2 = A.mark()
            wglu_sb = A.alloc("wglu_sb", [128, 2, 2 * CH], BF16)
            dma(wglu_sb.ap, wglu_b.rearrange("(k p) n -> p k n", p=128), reads=[dres["wglu_b"]], writes=[wglu_sb.res])
            uct = A.alloc("uct", [128, 16, CH], F32)
            dma(uct.ap, uctok[b].rearrange("(t p) c -> p t c", p=128), reads=[dres[("uctok", b)]], writes=[uct.res])
            yw = A.alloc("yw", [128, 16, CH], F32)
            ybf = A.alloc("ybf", [128, 16, CH], BF16)
            V(lambda e: e.tensor_tensor(uct.ap, uct.ap, dB.ap.unsqueeze(1).to_broadcast([128, 16, CH]), ALU.mult), r=[uct.res, dB.res], w=[uct.res])
            V(lambda e: e.tensor_tensor(yw.ap, yall.ap, uct.ap, ALU.add), r=[yall.res, uct.res], w=[yw.res])
            if b == 0:
                dbg(f"ypre{l}", yw.ap[:, 0, :], [yw.res])
            GP(lambda e: e.tensor_tensor(uct.ap, yw.ap, yw.ap, ALU.mult), r=[yw.res], w=[uct.res])
            V(lambda e: e.tensor_scalar(uct.ap, uct.ap, 0.044715, 1.0, ALU.mult, ALU.add), r=[uct.res], w=[uct.res])
            V(lambda e: e.tensor_tensor(uct.ap, uct.ap, yw.ap, ALU.mult), r=[uct.res, yw.res], w=[uct.res])
            ACT(lambda e: e.activation(uct.ap.rearrange("p a b -> p (a b)"), uct.ap.rearrange("p a b -> p (a b)"), AF.Sigmoid, bias=0.0, scale=1.5957691216057308), r=[uct.res], w=[uct.res])
            V(lambda e: e.tensor_tensor(ybf.ap, uct.ap, yw.ap, ALU.mult), r=[uct.res, yw.res], w=[ybf.res])
            yT_r = A.ring("yTs", [128, 2, 128], BF16, 2)
            glu = A.alloc("glu", [128, 2 * CH], F32); sg = A.alloc("sg", [128, CH], F32); oc = A.alloc("oc", [128, CH], F32)
            junk3 = A.alloc("junk3", [128, CH], BF16)
            ocb_r = A.ring("ocb", [128, CH], BF16, 2)
            for tl in range(16):
                for hh in range(2):
                    PE(lambda e, tl=tl, hh=hh: e.transpose(psB[:, 128 * hh:128 * hh + 128], ybf.ap[:, tl, 128 * hh:128 * hh + 128], ident.ap), r=[ybf.res, ident.res], w=[psres[6]], g=("yTt", l, b, tl))
                yT = yT_r.next()
                ACT(lambda e, yT=yT: e.copy(yT.ap.rearrange("p a b -> p (a b)"), psB[:, 0:256]), r=[psres[6]], w=[yT.res])
                for hh in range(2):
                    PE(lambda e, yT=yT, hh=hh: e.matmul(bankF(4), yT.ap[:, hh, :], wglu_sb.ap[:, hh, :], start=(hh == 0), stop=(hh == 1)), r=[yT.res, wglu_sb.res], w=[psres[4]], g=("glumm", l, b, tl))
                V(lambda e: e.tensor_tensor(glu.ap, bankF(4), bgluB.ap, ALU.add), r=[psres[4], bgluB.res], w=[glu.res])
                ACT(lambda e: e.activation(sg.ap, glu.ap[:, CH:2 * CH], AF.Sigmoid), r=[glu.res], w=[sg.res])
                V(lambda e: e.tensor_tensor(oc.ap, glu.ap[:, 0:CH], sg.ap, ALU.mult), r=[glu.res, sg.res], w=[oc.res])
                ACT(lambda e: e.activation(junk3.ap, oc.ap, AF.Square, accum_out=stat.ap[:, 0:1]), r=[oc.res], w=[junk3.res, statres[0]])
                ACT(lambda e: e.activation(stat.ap[:, 4:5], stat.ap[:, 0:1], AF.Sqrt, bias=EPS, scale=1.0 / CH), r=[statres[0]], w=[statres[1]])
                V(lambda e: e.reciprocal(stat.ap[:, 4:5], stat.ap[:, 4:5]), r=[statres[1]], w=[statres[1]])
                ocb = ocb_r.next()
                V(lambda e, ocb=ocb: e.scalar_tensor_tensor(ocb.ap, oc.ap, stat.ap[:, 4:5], mixB.ap[:, 768:1024], ALU.mult, ALU.mult), r=[oc.res, statres[1], mixB.res], w=[ocb.res])
                dma(mixedD[b, 128 * tl:128 * tl + 128, 768:1024], ocb.ap, reads=[ocb.res], writes=[dres[("mixedD", b)]], wgroup="mixw")
            A.reset(m2)

        def phase_out(b):
            m = A.mark()
            g1, G2, SH2 = modT[0], modT[1], modT[2]
            load_mod(g1, b, 2)
            load_mod(G2, b, 4, norm2B)
            load_mod(SH2, b, 3)
            wout_sb = A.alloc("wout_sb", [128, 8, D], BF16)
            dma(wout_sb.ap, wout_b.rearrange("(k p) n -> p k n", p=128), reads=[dres["wout_b"]], writes=[wout_sb.res])
            mx_r = A.ring("mxin", [128, D], BF16, 2)
            mT_r = A.ring("mT", [128, 8, 128], BF16, 2)
            xt_r = A.ring("xt6", [128, D], F32, 2)
            tmp_r = A.ring("tmp6", [128, D], F32, 2)
            xn_r = A.ring("xn", [128, D], F32, 2)
            junk = A.alloc("junk6", [128, D], BF16)
            hf_r = A.ring("hf6", [128, D], F32, 2)
            hb_r = A.ring("hb6", [128, D], BF16, 2)
            h2T_r = A.ring("h2Ts", [128, 8, 128], BF16, 2)
            def tile_gen(tl):
                tt = b * (S // 128) + tl
                t0 = tl * 128
                tmp = tmp_r.next(); hf = hf_r.next(); hb = hb_r.next()
                mx = mx_r.next(); xt = xt_r.next()
                dma(mx.ap, mixedD[b, t0:t0 + 128, :], reads=[dres[("mixedD", b)]], writes=[mx.res])
                dma(xt.ap, xin_ap[tt * 128:(tt + 1) * 128, :], reads=[xres_tt[tt]], writes=[xt.res])
                if tl == 0 and b == 0:
                    dbg(f"mixed{l}", mx.ap, [mx.res])
                for k in range(8):
                    PE(lambda e, k=k, mx=mx: e.transpose(psB[:, 128 * k:128 * (k + 1)], mx.ap[:, 128 * k:128 * (k + 1)], ident.ap), r=[mx.res, ident.res], w=[psres[6]], g=("mTt", l, tt))
                yield
                mT = mT_r.next()
                ACT(lambda e, mT=mT: e.copy(mT.ap.rearrange("p a b -> p (a b)"), psB[:, 0:1024]), r=[psres[6]], w=[mT.res])
                yield
                for cch in range(2):
                    for k in range(8):
                        PE(lambda e, k=k, cch=cch, mT=mT: e.matmul(bankF(cch), mT.ap[:, k, :], wout_sb.ap[:, k, 512 * cch:512 * cch + 512], start=(k == 0), stop=(k == 7)),
                           r=[mT.res, wout_sb.res], w=[psres[cch]], g=("omm", l, tt, cch))
                yield
                V(lambda e: e.tensor_tensor(tmp.ap, psF[:, 0:1024], g1.ap, ALU.mult), r=[psres[0], psres[1], g1.res], w=[tmp.res])
                xn = xn_r.next()
                V(lambda e, xn=xn, xt=xt: e.tensor_tensor(xn.ap, tmp.ap, xt.ap, ALU.add), r=[tmp.res, xt.res], w=[xn.res])
                dma(xmid[tt * 128:(tt + 1) * 128, :], xn.ap, reads=[xn.res], writes=[xmid_tt[tt]])
                yield
                ACT(lambda e, xn=xn: e.activation(junk.ap, xn.ap, AF.Square, accum_out=stat.ap[:, 0:1]), r=[xn.res], w=[junk.res, statres[0]])
                ACT(lambda e: e.activation(stat.ap[:, 4:5], stat.ap[:, 0:1], AF.Sqrt, bias=EPS, scale=1.0 / D), r=[statres[0]], w=[statres[1]])
                V(lambda e: e.reciprocal(stat.ap[:, 8:9], stat.ap[:, 4:5]), r=[statres[1]], w=[statres[2]])
                V(lambda e, xn=xn: e.scalar_tensor_tensor(hf.ap, xn.ap, stat.ap[:, 8:9], G2.ap, ALU.mult, ALU.mult), r=[xn.res, statres[2], G2.res], w=[hf.res])
                V(lambda e: e.tensor_tensor(hb.ap, hf.ap, SH2.ap, ALU.add), r=[hf.res, SH2.res], w=[hb.res])
                yield
                for k in range(8):
                    PE(lambda e, k=k: e.transpose(psB[:, 1024 + 128 * k:1024 + 128 * (k + 1)], hb.ap[:, 128 * k:128 * (k + 1)], ident.ap), r=[hb.res, ident.res], w=[psres[7]], g=("h2Tt", l, tt))
                yield
                h2T = h2T_r.next()
                ACT(lambda e, h2T=h2T: e.copy(h2T.ap.rearrange("p a b -> p (a b)"), psB[:, 1024:2048]), r=[psres[7]], w=[h2T.res])
                dma(h2TD[b, :, t0:t0 + 128].rearrange("(k p) t -> p k t", p=128), h2T.ap, reads=[h2T.res], writes=[dres[("h2TD", b)]], wgroup="h2w")
            active = []
            nxt = 0
            ntile = S // 128
            while active or nxt < ntile:
                if len(active) < 2 and nxt < ntile:
                    active.append(tile_gen(nxt))
                    nxt += 1
                for g_ in list(active):
                    try:
                        next(g_)
                    except StopIteration:
                        active.remove(g_)
            A.reset(m)
            P.barrier()

        def phase_ffn(b):
            m = A.mark()
            g2 = modT[0]
            load_mod(g2, b, 5)
            wdown_sb = A.alloc("wdown_sb", [128, 22, D], BF16)
            dma(wdown_sb.ap, wdown_b.rearrange("(f p) n -> p f n", p=128), reads=[dres["wdown_b"]], writes=[wdown_sb.res])
            hw_r = A.ring("hw", [128, 8, 514], BF16, 2)
            wup_r = A.ring("wup", [128, 2, 8, 128], BF16, 3)
            tv_r = A.ring("tv", [128, 512], F32, 3); tg_r = A.ring("tg", [128, 512], F32, 3)
            aT = A.alloc("aT", [128, 22, 512], BF16)
            xt_r = A.ring("xt7", [128, D], F32, 2)
            tmp = A.alloc("tmp7", [128, D], F32)
            xo_r = A.ring("xo", [128, D], F32, 2)
            zsel = 0
            pend = None

            def finish_pair(tv, tg, f, w_):
                ACT(lambda e, tg=tg: e.activation(tg.ap, tg.ap, AF.Silu), r=[tg.res], w=[tg.res])
                GP(lambda e, tv=tv, tg=tg, f=f: e.tensor_tensor(aT.ap[:, f, :], tv.ap, tg.ap, ALU.mult), r=[tv.res, tg.res], w=[aT.res], g=("aT", l, b, w_))

            for w_ in range(4):
                c0 = 512 * w_
                hw = hw_r.next()
                lo = max(c0 - 1, 0); hi = min(c0 + 513, S)
                j0 = lo - (c0 - 1)
                if j0 > 0:
                    GP(lambda e, hw=hw: e.memset(hw.ap[:, :, 0:1], 0.0), w=[hw.res], g=("hwl", l, b, w_))
                if hi < c0 + 513:
                    GP(lambda e, hw=hw: e.memset(hw.ap[:, :, 513:514], 0.0), w=[hw.res], g=("hwl", l, b, w_))
                dma(hw.ap[:, :, j0:j0 + (hi - lo)], h2TD[b, :, lo:hi].rearrange("(k p) t -> p k t", p=128), reads=[dres[("h2TD", b)]], writes=[hw.res], wgroup=("hwl", l, b, w_))
                for f in range(22):
                    wu = wup_r.next()
                    dma(wu.ap[:, 0, :, :].rearrange("p k n -> p (k n)"), wup_b[f], reads=[dres["wup_b"]], writes=[wu.res], wgroup=("wul", l, b, w_, f))
                    dma(wu.ap[:, 1, :, :].rearrange("p k n -> p (k n)"), wup_b[22 + f], reads=[dres["wup_b"]], writes=[wu.res], wgroup=("wul", l, b, w_, f))
                    zs = zsel % 2
                    zsel += 1
                    zb = [4 * zs, 4 * zs + 2]
                    zaps = []
                    for vg in range(2):
                        b0 = zb[vg]
                        zap = psF[:, 512 * b0:512 * b0 + 1024] if b0 < 6 else psB_f
                        zaps.append(zap)
                        for k in range(8):
                            PE(lambda e, k=k, vg=vg, zap=zap, wu=wu, hw=hw: e.matmul(zap[:, 0:512], wu.ap[:, vg, k, :], hw.ap[:, k, 0:512], start=(k == 0), stop=(k == 7)),
                               r=[wu.res, hw.res], w=[psres[b0]], g=("zmm", l, b, w_, f, vg, 0))
                        for k in range(8):
                            PE(lambda e, k=k, vg=vg, zap=zap, wu=wu, hw=hw: e.matmul(zap[:, 512:514], wu.ap[:, vg, k, :], hw.ap[:, k, 512:514], start=(k == 0), stop=(k == 7)),
                               r=[wu.res, hw.res], w=[psres[b0 + 1]], g=("zmm", l, b, w_, f, vg, 1))
                    tv = tv_r.next(); tg = tg_r.next()
                    for vg, tt_ in ((0, tv), (1, tg)):
                        fi = f + 22 * vg
                        zap = zaps[vg]
                        rs = [psres[zb[vg]], psres[zb[vg] + 1]]
                        ACT(lambda e, zap=zap, tt_=tt_, fi=fi: e.activation(tt_.ap, zap[:, 1:513], AF.Identity, bias=cb.ap[:, fi:fi + 1], scale=cw.ap[:, 1, fi:fi + 1]),
                            r=rs + [cw.res, cb.res], w=[tt_.res])
                        V(lambda e, zap=zap, tt_=tt_, fi=fi: e.scalar_tensor_tensor(tt_.ap, zap[:, 0:512], cw.ap[:, 0, fi:fi + 1], tt_.ap, ALU.mult, ALU.add), r=rs + [cw.res, tt_.res], w=[tt_.res])
                        V(lambda e, zap=zap, tt_=tt_, fi=fi: e.scalar_tensor_tensor(tt_.ap, zap[:, 2:514], cw.ap[:, 2, fi:fi + 1], tt_.ap, ALU.mult, ALU.add), r=rs + [cw.res, tt_.res], w=[tt_.res])
                    if w_ == 0 and b == 0 and f == 0:
                        dbg(f"zc{l}", tv.ap, [tv.res])
                    if pend is not None:
                        finish_pair(*pend)
                    pend = (tv, tg, f, w_)
                if pend is not None:
                    finish_pair(*pend)
                    pend = None
                for j in range(4):
                    tt = b * 16 + w_ * 4 + j
                    xt = xt_r.next()
                    dma(xt.ap, xmid[tt * 128:(tt + 1) * 128, :], reads=[xmid_tt[tt]], writes=[xt.res])
                    for cch in range(2):
                        for f in range(22):
                            PE(lambda e, f=f, cch=cch, j=j: e.matmul(bankF(cch), aT.ap[:, f, 128 * j:128 * j + 128], wdown_sb.ap[:, f, 512 * cch:512 * cch + 512], start=(f == 0), stop=(f == 21)),
                               r=[aT.res, wdown_sb.res], w=[psres[cch]], g=("dmm", l, tt, cch))
                    V(lambda e: e.tensor_tensor(tmp.ap, psF[:, 0:1024], g2.ap, ALU.mult), r=[psres[0], psres[1], g2.res], w=[tmp.res])
                    xo = xo_r.next()
                    V(lambda e, xo=xo, xt=xt: e.tensor_tensor(xo.ap, tmp.ap, xt.ap, ALU.add), r=[tmp.res, xt.res], w=[xo.res])
                    dma(out[tt * 128:(tt + 1) * 128, :], xo.ap, reads=[xo.res], writes=[xres_tt[tt]])
            A.reset(m)
            P.barrier()

        for b in range(NB):
            phase_proj(b)
            if stop_after == "proj":
                break
            phase_attn(b, "mla")
            phase_attn(b, "dil")
        if stop_after == "proj":
            break
        phase_s5()
        for b in range(NB):
            phase_out(b)
        for b in range(NB):
            phase_ffn(b)
        A.reset(m_layer)

    P.emit(st)
    st.close()
    return nc, P, A


def host_constants():
    ident = np.eye(128, dtype=np.float32).astype(ml_dtypes.bfloat16)
    k = np.arange(128)[:, None]
    xx = np.arange(MASKW)[None, :]
    d = k - xx + 1920
    w = (np.abs(d) <= 64).astype(np.float32) + ((d % 4 == 0) & (np.abs(d) <= 256)) + ((d % 16 == 0) & (np.abs(d) <= 1024))
    mask = w.astype(ml_dtypes.bfloat16)
    misc = np.zeros((128, 256), np.float32)
    misc[:, 0:16] = 128.0 * np.arange(16)[None, :]
    misc[:, 16:144] = np.arange(128)[None, :]
    for j in range(8):
        misc[16 * j:16 * j + 16, 144 + j] = 1.0
    misc[:, 160:176] = (1.0 / (10000.0 ** (np.arange(0, 32, 2, dtype=np.float32) / 32.0))).astype(np.float32)[None, :]
    misc[:, 192:224] = (1.0 / (10000.0 ** (np.arange(0, 64, 2, dtype=np.float32) / 64.0))).astype(np.float32)[None, :]
    iota = np.tile(np.arange(1024, dtype=np.float32)[None, :], (128, 1))
    return ident, mask, misc, iota


_CACHE = {}


def kernel(**inputs):
    n = 8
    if "nc" not in _CACHE:
        _CACHE["nc"] = build_program()[0]
    nc = _CACHE["nc"]
    ident, mask, misc, iota = host_constants()
    x = np.ascontiguousarray(np.asarray(inputs["x"], dtype=np.float32))
    c = np.asarray(inputs["c"], dtype=np.float32)
    pos = np.asarray(inputs["positions"], dtype=np.int32)
    shared = {k: np.ascontiguousarray(np.asarray(v)) for k, v in inputs.items() if k not in ("x", "c", "positions")}
    shared.update({"k_ident": ident, "k_mask": mask, "k_misc": misc, "k_iota": iota})
    in_maps = []
    for i in range(n):
        mp = dict(shared)
        mp["x"] = x[NB * i:NB * (i + 1)].reshape(T, D)
        mp["c"] = np.ascontiguousarray(c[NB * i:NB * (i + 1)])
        mp["positions"] = np.ascontiguousarray(pos[NB * i:NB * (i + 1)])
        in_maps.append(mp)
    res = run_bass_kernel_spmd(nc, in_maps, core_ids=list(range(n)))
    outs = [np.asarray(r["out"]).reshape(NB, S, D) for r in res.results]
    return np.concatenate(outs, axis=0).astype(np.float32)
```
